# Optimizing a Trainium2 kernel written in Bass

```python
import jax, jax.numpy as jnp
from jax import lax
import numpy as np


D_MODEL = 1024
BATCH = 16
SEQ = 256
DEPTH = 4
DEC_BATCH = 8
DEC_SEQ = 1024
PAST_LEN = 256

GRID_W = 64
N_HEADS = 8
N_KV_HEADS = 2
HEAD_DIM = 64
GQA_GROUP = N_HEADS // N_KV_HEADS
ATTN_W = N_HEADS * HEAD_DIM
KV_W = N_KV_HEADS * HEAD_DIM
POOL_WINDOWS = (2, 4, 8, 16)
N_POOL_GROUPS = 4
POOL_W = D_MODEL // 2
POOL_GROUP_W = POOL_W // N_POOL_GROUPS
MIX_W = ATTN_W + POOL_W
IN_W = ATTN_W + 2 * KV_W + POOL_W
D_FF = 2816
CONV_W = 3
WINDOW = 128
BLOCK = 128
ROPE_BASE = 10000.0
LN_EPS = 1e-5
DEEPNORM_ALPHA = (2 * DEPTH) ** 0.25
DEEPNORM_BETA = (8 * DEPTH) ** -0.25
ATTN_SCALE = HEAD_DIM ** -0.5
NEG_INF = -1e30

kernel_name = 'hybrid_pool_swa_prefix_dit_step'


def _layer_norm(x, g, b):
    xf = x.astype(jnp.float32)
    mu = jnp.mean(xf, axis=-1, keepdims=True)
    var = jnp.mean(jnp.square(xf - mu), axis=-1, keepdims=True)
    y = (xf - mu) * lax.rsqrt(var + LN_EPS) * g.astype(jnp.float32) + b.astype(jnp.float32)
    return y.astype(x.dtype)


def _modulation(cond, w_mod, b_mod):
    return jnp.split(jax.nn.silu(cond) @ w_mod + b_mod, 6, axis=-1)


def _project(h, w_in):
    B, T, _ = h.shape
    q, k, v, p = jnp.split(h @ w_in, [ATTN_W, ATTN_W + KV_W, ATTN_W + 2 * KV_W], axis=-1)
    q = q.reshape(B, T, N_HEADS, HEAD_DIM)
    k = k.reshape(B, T, N_KV_HEADS, HEAD_DIM)
    v = v.reshape(B, T, N_KV_HEADS, HEAD_DIM)
    return q, k, v, p


def _axial_rope(x):
    T = x.shape[1]
    rows = T // GRID_W
    row = jnp.repeat(jnp.arange(rows), GRID_W)
    col = jnp.tile(jnp.arange(GRID_W), rows)
    half = HEAD_DIM // 2
    inv_freq = ROPE_BASE ** (-jnp.arange(0, half, 2, dtype=jnp.float32) / half)

    def rotate(xa, pos):
        ang = pos.astype(jnp.float32)[:, None] * inv_freq[None, :]
        ang = jnp.concatenate([ang, ang], axis=-1)[None, :, None, :]
        cos = jnp.cos(ang).astype(x.dtype)
        sin = jnp.sin(ang).astype(x.dtype)
        x1, x2 = jnp.split(xa, 2, axis=-1)
        return xa * cos + jnp.concatenate([-x2, x1], axis=-1) * sin

    return jnp.concatenate([rotate(x[..., :half], row), rotate(x[..., half:], col)], axis=-1)


def _attend(q, k, v, sink, bias):
    s = jnp.einsum('bqhgd,bkhd->bhgqk', q, k).astype(jnp.float32) * ATTN_SCALE
    if bias is not None:
        s = s + bias
    sink_col = jnp.broadcast_to(sink.astype(jnp.float32)[None, :, :, None, None], s.shape[:-1] + (1,))
    p = jax.nn.softmax(jnp.concatenate([sink_col, s], axis=-1), axis=-1)[..., 1:]
    return jnp.einsum('bhgqk,bkhd->bqhgd', p.astype(v.dtype), v)


def _context_attention(q, k, v, sink):
    B, T = q.shape[:2]
    nb = T // BLOCK
    qb = jnp.moveaxis(q.reshape(B, nb, BLOCK, N_KV_HEADS, GQA_GROUP, HEAD_DIM), 1, 0)
    sk = sink.reshape(N_KV_HEADS, GQA_GROUP)
    out = lax.map(lambda qi: _attend(qi, k, v, sk, None), qb)
    return jnp.moveaxis(out, 0, 1).reshape(B, T, ATTN_W)


def _latent_attention(q, k, v, k_ctx, v_ctx, sink):
    B, T = q.shape[:2]
    nb = T // BLOCK
    Lc = k_ctx.shape[1]
    qg = q.reshape(B, T, N_KV_HEADS, GQA_GROUP, HEAD_DIM)
    pad = ((0, 0), (BLOCK, BLOCK), (0, 0), (0, 0))
    kp = jnp.pad(k, pad)
    vp = jnp.pad(v, pad)
    sk = sink.reshape(N_KV_HEADS, GQA_GROUP)
    r = jnp.arange(BLOCK)[:, None]
    cidx = jnp.arange(3 * BLOCK)[None, :]
    ctx_bias = jnp.zeros((BLOCK, Lc), jnp.float32)

    def block(i):
        qi = lax.dynamic_slice_in_dim(qg, i * BLOCK, BLOCK, axis=1)
        ki = lax.dynamic_slice_in_dim(kp, i * BLOCK, 3 * BLOCK, axis=1)
        vi = lax.dynamic_slice_in_dim(vp, i * BLOCK, 3 * BLOCK, axis=1)
        qpos = i * BLOCK + r
        kpos = (i - 1) * BLOCK + cidx
        valid = (jnp.abs(qpos - kpos) <= WINDOW) & (kpos >= 0) & (kpos < T)
        bias = jnp.concatenate([ctx_bias, jnp.where(valid, 0.0, NEG_INF).astype(jnp.float32)], axis=1)
        kk = jnp.concatenate([k_ctx.astype(ki.dtype), ki], axis=1)
        vv = jnp.concatenate([v_ctx.astype(vi.dtype), vi], axis=1)
        return _attend(qi, kk, vv, sk, bias)

    out = lax.map(block, jnp.arange(nb))
    return jnp.moveaxis(out, 0, 1).reshape(B, T, ATTN_W)


def _pool_mixer(p, w_pool, pool_scale):
    B, T, _ = p.shape
    pf = p.astype(jnp.float32).reshape(B, T, N_POOL_GROUPS, POOL_GROUP_W)
    cs = jnp.pad(jnp.cumsum(pf, axis=1), ((0, 0), (1, 0), (0, 0), (0, 0)))
    t = jnp.arange(T)[:, None]
    win = jnp.array(POOL_WINDOWS, dtype=jnp.int32)[None, :]
    start = jnp.maximum(t - win // 2, 0)
    end = jnp.minimum(t + win - win // 2, T)
    grp = jnp.arange(N_POOL_GROUPS)[None, :]
    total = cs[:, end, grp] - cs[:, start, grp]
    mean = total / (end - start).astype(jnp.float32)[None, :, :, None]
    d = (mean - pf).astype(p.dtype)
    y = jnp.einsum('btgc,gcd->btgd', d, w_pool).reshape(B, T, POOL_W)
    return y * pool_scale


def _conv_ffn(h, w_up, conv_w, conv_b, w_down):
    T = h.shape[1]
    u = h @ w_up
    up = jnp.pad(u, ((0, 0), (1, 1), (0, 0)))
    u = up[:, :T] * conv_w[0] + up[:, 1:T + 1] * conv_w[1] + up[:, 2:] * conv_w[2] + conv_b
    a, g = jnp.split(u, 2, axis=-1)
    return (jax.nn.silu(g) * a) @ w_down


def _trunk_layer(x, mods, attend, w_in, w_pool, pool_scale, w_out, ln1_g, ln1_b,
                 w_up, conv_w, conv_b, w_down, ln2_g, ln2_b):
    sh1, sc1, g1, sh2, sc2, g2 = mods
    h = x * (1.0 + sc1) + sh1
    q, k, v, p = _project(h, w_in)
    attn = attend(q, k, v)
    pool = _pool_mixer(p, w_pool, pool_scale)
    mix = jnp.concatenate([attn, pool], axis=-1) @ w_out
    x = _layer_norm(DEEPNORM_ALPHA * x + g1 * mix, ln1_g, ln1_b)
    h = x * (1.0 + sc2) + sh2
    ff = _conv_ffn(h, w_up, conv_w, conv_b, w_down)
    x = _layer_norm(DEEPNORM_ALPHA * x + g2 * ff, ln2_g, ln2_b)
    return x, k, v


def setup_inputs(seed: int = 0) -> dict:
    key = jax.random.key(seed)
    ks = jax.random.split(key, 24)
    f32 = jnp.float32

    def nrm(k, shape, scale):
        return jax.random.normal(k, shape, f32) * scale

    x_prompt = nrm(ks[0], (BATCH, SEQ, D_MODEL), 1.0)
    x_sample = nrm(ks[1], (DEC_BATCH, DEC_SEQ, D_MODEL), 1.0)
    cache_k = nrm(ks[2], (DEC_BATCH, DEPTH, PAST_LEN, N_KV_HEADS, HEAD_DIM), 1.0)
    cache_v = nrm(ks[3], (DEC_BATCH, DEPTH, PAST_LEN, N_KV_HEADS, HEAD_DIM), DEEPNORM_BETA)
    c = nrm(ks[4], (DEC_BATCH, D_MODEL), 1.0)
    c_ctx = nrm(ks[5], (D_MODEL,), 1.0)
    w_mod = nrm(ks[6], (DEPTH, D_MODEL, 6 * D_MODEL), 0.5 * D_MODEL ** -0.5)
    b_mod = nrm(ks[7], (DEPTH, 6 * D_MODEL), 0.01)
    cols = jnp.arange(IN_W)
    v_cols = (cols >= ATTN_W + KV_W) & (cols < ATTN_W + 2 * KV_W)
    w_in = nrm(ks[8], (DEPTH, D_MODEL, IN_W), D_MODEL ** -0.5) * jnp.where(v_cols, DEEPNORM_BETA, 1.0)
    attn_sink = nrm(ks[9], (DEPTH, N_HEADS), 0.5)
    w_pool = nrm(ks[10], (DEPTH, N_POOL_GROUPS, POOL_GROUP_W, POOL_GROUP_W), POOL_GROUP_W ** -0.5)
    pool_scale = 1.0 + nrm(ks[11], (DEPTH, POOL_W), 0.1)
    w_out = nrm(ks[12], (DEPTH, MIX_W, D_MODEL), DEEPNORM_BETA * MIX_W ** -0.5)
    ln1_g = 1.0 + nrm(ks[13], (DEPTH, D_MODEL), 0.05)
    ln1_b = nrm(ks[14], (DEPTH, D_MODEL), 0.02)
    w_up = nrm(ks[15], (DEPTH, D_MODEL, 2 * D_FF), D_MODEL ** -0.5)
    conv_w = jnp.array([0.0, 1.0, 0.0], f32)[None, :, None] + nrm(ks[16], (DEPTH, CONV_W, 2 * D_FF), 0.3)
    conv_b = nrm(ks[17], (DEPTH, 2 * D_FF), 0.02)
    w_down = nrm(ks[18], (DEPTH, D_FF, D_MODEL), DEEPNORM_BETA * D_FF ** -0.5)
    ln2_g = 1.0 + nrm(ks[19], (DEPTH, D_MODEL), 0.05)
    ln2_b = nrm(ks[20], (DEPTH, D_MODEL), 0.02)
    return {'x_prompt': x_prompt, 'x_sample': x_sample, 'cache_k': cache_k, 'cache_v': cache_v,
            'c': c, 'c_ctx': c_ctx, 'w_mod': w_mod, 'b_mod': b_mod, 'w_in': w_in,
            'attn_sink': attn_sink, 'w_pool': w_pool, 'pool_scale': pool_scale, 'w_out': w_out,
            'ln1_g': ln1_g, 'ln1_b': ln1_b, 'w_up': w_up, 'conv_w': conv_w, 'conv_b': conv_b,
            'w_down': w_down, 'ln2_g': ln2_g, 'ln2_b': ln2_b}


def reference(x_prompt, x_sample, cache_k, cache_v, c, c_ctx, w_mod, b_mod, w_in, attn_sink,
              w_pool, pool_scale, w_out, ln1_g, ln1_b, w_up, conv_w, conv_b, w_down, ln2_g, ln2_b):
    x = x_prompt
    ks, vs = [], []
    for l in range(DEPTH):
        mods = _modulation(c_ctx, w_mod[l], b_mod[l])
        sink_l = attn_sink[l]
        attend = lambda q, k, v, s=sink_l: _context_attention(q, k, v, s)
        x, k, v = _trunk_layer(x, mods, attend, w_in[l], w_pool[l], pool_scale[l], w_out[l],
                               ln1_g[l], ln1_b[l], w_up[l], conv_w[l], conv_b[l], w_down[l],
                               ln2_g[l], ln2_b[l])
        ks.append(k)
        vs.append(v)
    y_prompt = x
    new_cache_k = jnp.stack(ks, axis=1)
    new_cache_v = jnp.stack(vs, axis=1)

    x = x_sample
    for l in range(DEPTH):
        mods = [m[:, None, :] for m in _modulation(c, w_mod[l], b_mod[l])]
        sink_l = attn_sink[l]
        kc = cache_k[:, l]
        vc = cache_v[:, l]
        attend = lambda q, k, v, s=sink_l, kc=kc, vc=vc: _latent_attention(
            _axial_rope(q), _axial_rope(k), v, kc, vc, s)
        x, _, _ = _trunk_layer(x, mods, attend, w_in[l], w_pool[l], pool_scale[l], w_out[l],
                               ln1_g[l], ln1_b[l], w_up[l], conv_w[l], conv_b[l], w_down[l],
                               ln2_g[l], ln2_b[l])
    y_sample = x
    return (y_prompt, y_sample, new_cache_k, new_cache_v)
```

```python
from contextlib import ExitStack
import numpy as np
import concourse.bass as bass
import concourse.mybir as mybir
from concourse.bass_utils import run_bass_kernel_spmd

F32 = mybir.dt.float32
BF16 = mybir.dt.bfloat16
AF = mybir.ActivationFunctionType
ALU = mybir.AluOpType

DEPTH = 4
D = 1024
NTOK = 1536
ALPHA = (2 * DEPTH) ** 0.25
EPSP = 1e-5 / (ALPHA * ALPHA)
SCALE = 64 ** -0.5
NEG = -30000.0
NS = 5
HALF_PAIRS = (14, 8)
FILL_AB, FILL_BC, FILL_CD, FILL_EA = 16, 10, 30, 0
FILL_ATT = 20
FILL_LN = 16

V_BMOD = 0
V_PSC = 192
V_LN1G = 208
V_LN1B = 240
V_LN2G = 272
V_LN2B = 304
V_CW = 336
V_CB = 864
V_COND = 1040
V_ROWS = 1152


class _Op:
    __slots__ = ('eng', 'meth', 'args', 'kw', 'deps', 'signal', 'sig_val', 'dma_key', 'dma_val', 'dma_waits', 'idx')


class Prog:
    def __init__(self):
        self.ops = []
        self.last_w = {}
        self.readers = {}
        self.dma_count = {}

    def transfer(self, old, new):
        comb = {}
        dl = []
        for r in old:
            for w in self.last_w.get(r, ()):
                if w.dma_key is not None:
                    dl.append(w)
                elif w.eng not in comb or comb[w.eng].idx < w.idx:
                    comb[w.eng] = w
            for k, v in self.readers.get(r, {}).items():
                if k == '_dma':
                    dl.extend(v)
                elif k not in comb or comb[k].idx < v.idx:
                    comb[k] = v
        for n in new:
            self.last_w[n] = []
            rd = dict(comb)
            if dl:
                rd['_dma'] = list(dl)
            self.readers[n] = rd

    def add(self, eng, meth, args=(), kw=None, reads=(), writes=(), dma_key=None):
        op = _Op()
        op.eng = eng
        op.meth = meth
        op.args = args
        op.kw = kw or {}
        op.dma_key = dma_key
        op.deps = []
        op.signal = False
        op.sig_val = None
        op.dma_waits = {}
        op.idx = len(self.ops)
        deps = {}

        def adddep(d):
            if d is None or d is op:
                return
            if eng == 'pe' and d.eng == 'pe' and d.dma_key is None:
                return
            deps[d.idx] = d

        for r in reads:
            for w in self.last_w.get(r, ()):
                adddep(w)
            if r.startswith('pb'):
                for k, v in self.readers.get(r, {}).items():
                    if k != eng and k != '_dma':
                        adddep(v)
        for w_ in writes:
            for w in self.last_w.get(w_, ()):
                adddep(w)
            rd = self.readers.get(w_)
            if rd:
                for k, v in rd.items():
                    if k == '_dma':
                        for x in v:
                            adddep(x)
                    else:
                        adddep(v)
        for d in deps.values():
            if d.dma_key is not None:
                k = d.dma_key
                op.dma_waits[k] = max(op.dma_waits.get(k, 0), self.dma_count[k])
            else:
                op.deps.append(d)
        for r in reads:
            rd = self.readers.setdefault(r, {})
            if dma_key is not None:
                rd.setdefault('_dma', []).append(op)
            else:
                rd[eng] = op
        for w_ in writes:
            self.last_w[w_] = [op]
            self.readers[w_] = {}
        if dma_key is not None:
            self.dma_count[dma_key] = self.dma_count.get(dma_key, 0) + 16
            op.dma_val = self.dma_count[dma_key]
        self.ops.append(op)
        return op

    def emit(self, nc, stack):
        for op in self.ops:
            for d in op.deps:
                d.signal = True
        cnt = {}
        for op in self.ops:
            if op.dma_key is None and op.signal:
                cnt[op.eng] = cnt.get(op.eng, 0) + 1
                op.sig_val = cnt[op.eng]
        esem = {}
        for e in ('pe', 'act', 'dve', 'pool', 'sp'):
            esem[e] = stack.enter_context(nc.semaphore("s_" + e))
        dsem = {}
        for k in self.dma_count:
            dsem[k] = stack.enter_context(nc.semaphore("d_%s" % k))
        byeng = {}
        for op in self.ops:
            byeng.setdefault(op.eng, []).append(op)
        block = stack.enter_context(nc.Block())
        self.n_waits = 0

        def run_engine(ename, eobj):
            waited = {}
            for op in byeng.get(ename, []):
                need = {}
                for d in op.deps:
                    key = ('e', d.eng)
                    need[key] = (esem[d.eng], max(need.get(key, (None, 0))[1], d.sig_val))
                for k, v in op.dma_waits.items():
                    key = ('d', k)
                    need[key] = (dsem[k], max(need.get(key, (None, 0))[1], v))
                for key, (s, v) in need.items():
                    if waited.get(key, 0) >= v:
                        continue
                    eobj.wait_ge(s, v)
                    self.n_waits += 1
                    waited[key] = v
                if op.meth is None:
                    continue
                ins = getattr(eobj, op.meth)(*op.args, **op.kw)
                if op.dma_key is not None:
                    ins.then_inc(dsem[op.dma_key], 16)
                elif op.signal:
                    ins.then_inc(esem[op.eng], 1)

        @block.tensor
        def _(e):
            run_engine('pe', e)

        @block.scalar
        def _(e):
            run_engine('act', e)

        @block.vector
        def _(e):
            run_engine('dve', e)

        @block.gpsimd
        def _(e):
            run_engine('pool', e)

        @block.sync
        def _(e):
            run_engine('sp', e)


def _seg(name):
    return (0, 512, 2, 256, 0) if name == 'P' else (512, 1024, 1, 1024, 1)


class _Stop(Exception):
    pass


def build_program(depth=DEPTH, stop=None):
    nc = bass.Bass("TRN2", target_bir_lowering=False)
    dt = nc.dram_tensor
    x_in = dt("x_in", [NTOK, D], F32, kind="ExternalInput").ap()
    ck = dt("ck", [DEPTH, 256, 128], F32, kind="ExternalInput").ap()
    cv = dt("cv", [DEPTH, 256, 128], F32, kind="ExternalInput").ap()
    vecpack = dt("vecpack", [V_ROWS, 128], F32, kind="ExternalInput").ap()
    sink_d = dt("sink", [DEPTH, 8], F32, kind="ExternalInput").ap()
    ropeC_d = dt("ropeC", [128, 1024], F32, kind="ExternalInput").ap()
    ropeS_d = dt("ropeS", [128, 1024], F32, kind="ExternalInput").ap()
    rperm_d = dt("rperm", [128, 128], F32, kind="ExternalInput").ap()
    ident_d = dt("ident", [128, 128], F32, kind="ExternalInput").ap()
    masks_d = dt("masks", [128, 1024], F32, kind="ExternalInput").ap()
    ecf_d = dt("ecf", [128, 256], F32, kind="ExternalInput").ap()
    w_mod = dt("w_mod", [DEPTH, D, 6 * D], F32, kind="ExternalInput").ap()
    w_in = dt("w_in", [DEPTH, D, 1280], F32, kind="ExternalInput").ap()
    w_pool = dt("w_pool", [DEPTH, 4, 128, 128], F32, kind="ExternalInput").ap()
    w_out = dt("w_out", [DEPTH, D, D], F32, kind="ExternalInput").ap()
    w_up = dt("w_up", [DEPTH, D, 5632], F32, kind="ExternalInput").ap()
    w_down = dt("w_down", [DEPTH, 2816, D], F32, kind="ExternalInput").ap()
    y_out = dt("y", [NTOK, D], F32, kind="ExternalOutput").ap()
    nk_out = dt("nk", [2, DEPTH, 256, 128], F32, kind="ExternalOutput").ap()
    nv_out = dt("nv", [2, DEPTH, 256, 128], F32, kind="ExternalOutput").ap()

    P = Prog()
    st = ExitStack()
    with st:
        def sb(n, s, d):
            return st.enter_context(nc.sbuf_tensor(n, s, d))

        xT = sb("xT", [128, 8, NTOK], F32)
        hT = sb("hT", [128, 8, NTOK], BF16)
        vecs = sb("vecs", [128, V_ROWS], F32)
        ropeC = sb("ropeCs", [128, 1024], F32)
        ropeS = sb("ropeSs", [128, 1024], F32)
        rperm = sb("rperms", [128, 128], F32)
        ident32 = sb("ident32", [128, 128], F32)
        identb = sb("identb", [128, 128], BF16)
        ones = sb("ones", [128, 128], BF16)
        maskb = sb("maskb", [128, 2, 512], BF16)
        ecf = sb("ecfs", [128, 256], F32)
        sink8 = sb("sink8", [1, 8], F32)
        sinkf8 = sb("sinkf8", [1, 8], F32)
        sinkh8 = sb("sinkh8", [1, 8], BF16)
        sinkl8 = sb("sinkl8", [1, 8], BF16)
        sinkhl = sb("sinkhl", [33, 1024], BF16)
        sinkL = [sb("sinkL%d" % i, [33, 128], BF16) for i in range(2)]
        scT = sb("scT", [128, 8, 2], BF16)
        modsT = [sb("modsT%d" % i, [128, 2, 48], F32) for i in range(2)]
        coef = sb("coef", [128, 2, 12, 2, 8], F32)
        ring = [sb("ring%d" % i, [128, 4096], BF16) for i in range(NS)]
        wpool = [sb("wpool%d" % i, [128, 4, 128], BF16) for i in range(2)]
        lnm = sb("lnm", [128, 512], F32)
        lnq = sb("lnq", [128, 512], F32)
        ybf = [sb("ybf%d" % i, [128, 512], BF16) for i in range(2)]
        ysq = [sb("ysq%d" % i, [128, 512], BF16) for i in range(2)]
        UW = 16128
        U = sb("U", [128, UW], F32)
        ps = st.enter_context(nc.psum_tensor("ps", [128, 8, 512], F32))

        cur = [0]

        def carve(nwords, dtype, shape=()):
            a = cur[0]
            cur[0] += nwords
            assert cur[0] <= UW, cur[0]
            v = U[:, a:a + nwords]
            if dtype == BF16:
                v = v.bitcast(BF16)
            if len(shape) == 2:
                return v.rearrange("p (a b) -> p a b", a=shape[0])
            if len(shape) == 3:
                return v.rearrange("p (a b c) -> p a b c", a=shape[0], b=shape[1])
            return v

        qT = carve(3072, BF16, (4, NTOK))
        kT = carve(768, BF16)
        Vb = carve(1536, BF16, (12, 256))
        poolT = carve(3072, BF16, (4, NTOK))
        kcT = carve(128, BF16)
        vcb = carve(256, BF16, (2, 256))
        p32 = carve(1040, F32)
        sA = carve(1040, F32)
        sB = carve(1040, F32)
        dTt = [carve(512, BF16) for _ in range(2)]
        q32 = [carve(512, F32) for _ in range(2)]
        t1 = carve(512, F32)
        PT = [carve(512, BF16, (2, 512)) for _ in range(3)]
        kvst = sA[:, 0:1024].rearrange("p (a b) -> p a b", a=4)
        cstage = sB[:, 0:512].rearrange("p (a b c) -> p a b c", a=2, b=2)
        rcp = [lnm, lnq]
        rcpn = ['lnm', 'lnq']
        cur[0] = 0
        mT = carve(14 * 768, BF16, (14, NTOK))
        ta = [carve(1024, F32) for _ in range(2)]
        tg = [carve(1024, F32) for _ in range(2)]
        cur[0] = 0
        xs = [carve(1024, F32) for _ in range(6)]

        MIX_RES = (['qT%d%s' % (c, s) for c in range(4) for s in 'PS'] + ['poolT%d%s' % (c, s) for c in range(4) for s in 'PS'] +
                   ['kTP', 'kTS', 'VbP', 'VbS', 'kcT', 'vcb', 'p32', 'sA', 'sB', 'dT0', 'dT1', 'q320', 'q321', 't1',
                    'PT0', 'PT1', 'PT2'])
        FFN_RES = ['mT%d%s' % (j, s) for j in range(14) for s in 'PS'] + ['ta0', 'ta1', 'tg0', 'tg1']
        IO_RES = ['xs%d' % i for i in range(6)]

        def MM(out, lhsT, rhs, start, stop, reads, writes):
            P.add('pe', 'matmul', (out, lhsT, rhs), dict(start=start, stop=stop), reads, writes)

        def TR(out, in_, ident, reads, writes):
            P.add('pe', 'transpose', (out, in_, ident), None, reads, writes)

        def ACT(out, in_, func, reads, writes, bias=None, scale=None):
            kw = {}
            if bias is not None:
                kw['bias'] = bias
            if scale is not None:
                kw['scale'] = scale
            P.add('act', 'activation', (out, in_, func), kw, reads, writes)

        def ACP(out, in_, reads, writes):
            P.add('act', 'copy', (out, in_), None, reads, writes)

        def VCP(out, in_, reads, writes):
            P.add('dve', 'tensor_copy', (out, in_), None, reads, writes)

        def TT(out, in0, in1, op, reads, writes):
            P.add('dve', 'tensor_tensor', (), dict(out=out, in0=in0, in1=in1, op=op), reads, writes)

        def TS(out, in0, s1, s2, op0, op1, reads, writes):
            kw = dict(out=out, in0=in0, scalar1=s1, scalar2=s2, op0=op0)
            if op1 is not None:
                kw['op1'] = op1
            P.add('dve', 'tensor_scalar', (), kw, reads, writes)

        def STT(out, in0, scalar, in1, op0, op1, reads, writes):
            P.add('dve', 'scalar_tensor_tensor', (), dict(out=out, in0=in0, scalar=scalar, in1=in1, op0=op0, op1=op1), reads, writes)

        def DMA(q, out, in_, reads, writes, key):
            P.add(q, 'dma_start', (), dict(out=out, in_=in_), reads, writes, dma_key=key)

        slot_ctr = [0]
        reserved = set()

        def _bank_recency(bk):
            r = 'pb%d' % bk
            m = -1
            for w in P.last_w.get(r, ()):
                m = max(m, w.idx)
            for k, v in P.readers.get(r, {}).items():
                if k == '_dma':
                    for x in v:
                        m = max(m, x.idx)
                else:
                    m = max(m, v.idx)
            return m

        def next_slot():
            best, bm = None, None
            for s in range(4):
                if s in reserved:
                    continue
                m = max(_bank_recency(2 * s), _bank_recency(2 * s + 1))
                if bm is None or m < bm:
                    best, bm = s, m
            return best

        def slot_res(s):
            return ['pb%d' % (2 * s), 'pb%d' % (2 * s + 1)]

        def slot_flat(s):
            return ps[:, 2 * s:2 * s + 2, :].rearrange("p a b -> p (a b)")

        ring_ctr = [0]

        def v3(t, a):
            return t[:].rearrange("p (a b) -> p a b", a=a)

        pinned = set()

        def ring_fill(dmas):
            while True:
                i = ring_ctr[0] % NS
                ring_ctr[0] += 1
                if i not in pinned:
                    break
            for dst_fn, src in dmas:
                DMA('pool', dst_fn(ring[i]), src, [], ['ring%d' % i], 'ring%d' % i)
            return i

        DMA('sp', ident32[:], ident_d, [], ['ident32'], 'c0')
        DMA('sp', rperm[:], rperm_d, [], ['rperm'], 'c0')
        DMA('sp', ecf[:], ecf_d, [], ['ecf'], 'c0')
        DMA('sp', ropeC[:], ropeC_d, [], ['ropeC'], 'c0')
        DMA('sp', ropeS[:], ropeS_d, [], ['ropeS'], 'c0')
        DMA('pool', identb[:], ident_d, [], ['identb'], 'c1')
        DMA('pool', maskb[:].rearrange("p a b -> p (a b)"), masks_d, [], ['maskb'], 'c1')
        P.add('dve', 'memset', (ones[:], 1.0), None, [], ['ones'])
        P.add('dve', 'memset', (sinkhl[:], 0.0), None, [], ['sinkhl'])
        for kvh in range(2):
            P.add('dve', 'memset', (sinkL[kvh][:], 0.0), None, [], ['sinkL'])
            P.add('dve', 'memset', (sinkL[kvh][:, (1 - kvh) * 64:(2 - kvh) * 64], 1.0), None, ['sinkL'], ['sinkL'])
        DMA('sp', xs[4].rearrange("p (t c) -> p t c", t=8), vecpack[0:1024, :].rearrange("(t p) c -> p t c", p=128), [], ['xs4'], 'xs4')
        DMA('sp', xs[5][:, 0:128], vecpack[1024:1152, :], [], ['xs5'], 'xs5')
        for t in range(9):
            src = xs[4][:, t * 128:(t + 1) * 128] if t < 8 else xs[5][:, 0:128]
            sn = 'xs4' if t < 8 else 'xs5'
            s = next_slot()
            TR(ps[:, 2 * s, 0:128], src, ident32[:], [sn, 'ident32'], slot_res(s))
            ACP(vecs[:, t * 128:(t + 1) * 128], ps[:, 2 * s, 0:128], slot_res(s), ['vecs'])
        for c in range(2):
            ACT(scT[:, :, c], vecs[:, V_COND + 8 * c:V_COND + 8 * c + 8], AF.Silu, ['vecs'], ['scT'])

        for ti_, tb in enumerate(list(range(4, 12)) + list(range(4))):
            b = ti_ % 4
            DMA('sp', xs[b], x_in[tb * 128:(tb + 1) * 128, :], [], ['xs%d' % b], 'xs%d' % b)
            s = next_slot()
            for c in range(8):
                TR(slot_flat(s)[:, c * 128:(c + 1) * 128], xs[b][:, c * 128:(c + 1) * 128], ident32[:], ['xs%d' % b, 'ident32'], slot_res(s))
            sg = 'P' if tb < 4 else 'S'
            ACP(xT[:, :, tb * 128:(tb + 1) * 128], slot_flat(s).rearrange("p (c t) -> p c t", c=8), slot_res(s), ['xT%d%s' % (c, sg) for c in range(8)])

        def cf(par, kind, cond, c):
            return coef[:, par, kind, cond, c:c + 1]

        K_A1, K_B1, K_G1P, K_G2P, K_A2, K_B2, K_OP1, K_OP2 = range(8)

        def mods_for_layer(l, pieces):
            par = l % 2
            for pc in pieces:
                i = ring_fill([(lambda r: v3(r, 8), w_mod[l, :, pc * 512:(pc + 1) * 512].rearrange("(k p) n -> p k n", p=128))])
                s = next_slot()
                wv = v3(ring[i], 8)
                for mm in range(4):
                    for kc in range(8):
                        MM(ps[:, 2 * s, mm * 2:mm * 2 + 2], wv[:, kc, mm * 128:(mm + 1) * 128], scT[:, kc, :], kc == 0, kc == 7, ['ring%d' % i, 'scT'], slot_res(s))
                for c in range(2):
                    bc = V_BMOD + l * 48 + pc * 4
                    TT(modsT[par][:, c, pc * 4:pc * 4 + 4], ps[:, 2 * s, c:8:2], vecs[:, bc:bc + 4], ALU.add, slot_res(s) + ['vecs'], ['mods%d_%d' % (par, pc)])

        def mres(par, j):
            return ['mods%d_%d' % (par, 2 * j), 'mods%d_%d' % (par, 2 * j + 1)]

        def coef_layer_start(l):
            par = l % 2
            for c in range(2):
                TS(coef[:, par, K_A1, c, :], modsT[par][:, c, 8:16], 1.0, None, ALU.add, None, mres(par, 1), ['cfA1_%d' % par])
                VCP(coef[:, par, K_B1, c, :], modsT[par][:, c, 0:8], mres(par, 0), ['cfB1_%d' % par])

        def coef_mid(l):
            par = l % 2
            g1 = vecs[:, V_LN1G + l * 8:V_LN1G + l * 8 + 8]
            b1 = vecs[:, V_LN1B + l * 8:V_LN1B + l * 8 + 8]
            for c in range(2):
                TS(coef[:, par, K_G1P, c, :], modsT[par][:, c, 16:24], 1.0 / ALPHA, None, ALU.mult, None, mres(par, 2), ['cfG1_%d' % par])
                TS(coef[:, par, K_G2P, c, :], modsT[par][:, c, 40:48], 1.0 / ALPHA, None, ALU.mult, None, mres(par, 5), ['cfG2_%d' % par])
                TS(coef[:, par, K_OP2, c, :], modsT[par][:, c, 32:40], 1.0, None, ALU.add, None, mres(par, 4), ['cfO2_%d' % par])
                TT(coef[:, par, K_A2, c, :], coef[:, par, K_OP2, c, :], g1, ALU.mult, ['cfO2_%d' % par, 'vecs'], ['cfA2_%d' % par])
                TT(coef[:, par, K_B2, c, :], coef[:, par, K_OP2, c, :], b1, ALU.mult, ['cfO2_%d' % par, 'vecs'], ['cfB2_%d' % par])
                TT(coef[:, par, K_B2, c, :], coef[:, par, K_B2, c, :], modsT[par][:, c, 24:32], ALU.add, ['cfB2_%d' % par] + mres(par, 3), ['cfB2_%d' % par])

        def coef_next(l):
            par = (l + 1) % 2
            g2 = vecs[:, V_LN2G + l * 8:V_LN2G + l * 8 + 8]
            b2 = vecs[:, V_LN2B + l * 8:V_LN2B + l * 8 + 8]
            for c in range(2):
                TS(coef[:, par, K_OP1, c, :], modsT[par][:, c, 8:16], 1.0, None, ALU.add, None, mres(par, 1), ['cfO1_%d' % par])
                TT(coef[:, par, K_A1, c, :], coef[:, par, K_OP1, c, :], g2, ALU.mult, ['cfO1_%d' % par, 'vecs'], ['cfA1_%d' % par])
                TT(coef[:, par, K_B1, c, :], coef[:, par, K_OP1, c, :], b2, ALU.mult, ['cfO1_%d' % par, 'vecs'], ['cfB1_%d' % par])
                TT(coef[:, par, K_B1, c, :], coef[:, par, K_B1, c, :], modsT[par][:, c, 0:8], ALU.add, ['cfB1_%d' % par] + mres(par, 0), ['cfB1_%d' % par])

        def mm_fm(s, sgn, lhs_list, rhs_fn, reads):
            c0, n, _, _, _ = _seg(sgn)
            nk = len(lhs_list)
            for k in range(nk):
                for h in range(n // 512):
                    MM(ps[:, 2 * s + h, :], lhs_list[k], rhs_fn(k, c0 + h * 512, c0 + (h + 1) * 512), k == 0, k == nk - 1, reads, slot_res(s))

        def layer_norm(l, which, last):
            par = l % 2
            for tb in range(3):
                sgn = 'P' if tb == 0 else 'S'
                cond = 0 if tb == 0 else 1
                c0 = tb * 512
                s = next_slot()
                xres = ['xT%d%s' % (c, sgn) for c in range(8)]
                for c in range(8):
                    b = c % 2
                    xv = xT[:, c, c0:c0 + 512]
                    ACP(ybf[b][:], xv, [xres[c]], ['ybf%d' % b])
                    ACT(ysq[b][:], xv, AF.Square, [xres[c]], ['ysq%d' % b])
                    MM(ps[:, 2 * s, :], ones[:], ybf[b][:], c == 0, c == 7, ['ones', 'ybf%d' % b], slot_res(s))
                    MM(ps[:, 2 * s + 1, :], ones[:], ysq[b][:], c == 0, c == 7, ['ones', 'ysq%d' % b], slot_res(s))
                ACT(lnm[:], ps[:, 2 * s, :], AF.Identity, slot_res(s), ['lnm'], scale=1.0 / D)
                TT(lnq[:], lnm[:], lnm[:], ALU.mult, ['lnm'], ['lnq'])
                STT(lnq[:], ps[:, 2 * s + 1, :], 1.0 / D, lnq[:], ALU.mult, ALU.subtract, slot_res(s) + ['lnq'], ['lnq'])
                TS(lnq[:], lnq[:], EPSP, None, ALU.add, None, ['lnq'], ['lnq'])
                ACT(lnq[:], lnq[:], AF.Ln, ['lnq'], ['lnq'])
                ACT(lnq[:], lnq[:], AF.Exp, ['lnq'], ['lnq'], scale=-0.5)
                if which == 1:
                    gcol, bcol = V_LN1G + l * 8, V_LN1B + l * 8
                    kA, kB, cpar = K_A2, K_B2, par
                    cres = ['cfA2_%d' % par, 'cfB2_%d' % par]
                else:
                    gcol, bcol = V_LN2G + l * 8, V_LN2B + l * 8
                    kA, kB, cpar = K_A1, K_B1, (l + 1) % 2
                    cres = ['cfA1_%d' % cpar, 'cfB1_%d' % cpar]
                for c in range(8):
                    xv = xT[:, c, c0:c0 + 512]
                    TT(xv, xv, lnm[:], ALU.subtract, [xres[c], 'lnm'], [xres[c]])
                    TT(xv, xv, lnq[:], ALU.mult, [xres[c], 'lnq'], [xres[c]])
                    if not last:
                        TS(hT[:, c, c0:c0 + 512], xv, cf(cpar, kA, cond, c), cf(cpar, kB, cond, c), ALU.mult, ALU.add, [xres[c]] + cres, ['hT%d%s' % (c, sgn)])
                    ACT(xv, xv, AF.Identity, [xres[c], 'vecs'], [xres[c]], bias=vecs[:, bcol + c:bcol + c + 1], scale=vecs[:, gcol + c:gcol + c + 1])

        def hrhs(k, a, b):
            return hT[:, k, a:b]

        def mixrhs(k, a, b):
            return hT[:, k, a:b] if k < 4 else poolT[:, k - 4, a:b]

        if stop == 'setup':
            depth = 0
        mods_for_layer(0, range(0, 4))
        coef_layer_start(0)
        for sgn in 'SP':
            c0, n, _, _, cond = _seg(sgn)
            for c in range(8):
                if c % 2 == 0:
                    ACT(hT[:, c, c0:c0 + n], xT[:, c, c0:c0 + n], AF.Identity, ['xT%d%s' % (c, sgn), 'cfA1_0', 'cfB1_0'], ['hT%d%s' % (c, sgn)],
                        bias=cf(0, K_B1, cond, c), scale=cf(0, K_A1, cond, c))
                else:
                    TS(hT[:, c, c0:c0 + n], xT[:, c, c0:c0 + n], cf(0, K_A1, cond, c), cf(0, K_B1, cond, c), ALU.mult, ALU.add,
                       ['xT%d%s' % (c, sgn), 'cfA1_0', 'cfB1_0'], ['hT%d%s' % (c, sgn)])
        P.transfer(IO_RES, MIX_RES)

        hres = {sgn: ['hT%d%s' % (c, sgn) for c in range(8)] for sgn in 'PS'}
        mixres = {sgn: ['hT%d%s' % (c, sgn) for c in range(4)] + ['poolT%d%s' % (c, sgn) for c in range(4)] for sgn in 'PS'}

        def run_pipeline(units, reverse=True):
            nst = max(len(u) for u in units)
            for step in range(len(units) + nst - 1):
                ks = range(nst - 1, -1, -1) if reverse else range(nst)
                for k in ks:
                    n = step - k
                    if 0 <= n < len(units) and k < len(units[n]) and units[n][k] is not None:
                        units[n][k]()

        pending = []

        def emit_pending(k):
            for _ in range(min(k, len(pending))):
                pending.pop(0)()

        def ln_tail_pieces(l, which, last, tb, sb0, sb1):
            par = l % 2
            sgn = 'P' if tb == 0 else 'S'
            cond = 0 if tb == 0 else 1
            c0 = tb * 512
            xres = ['xT%d%s' % (c, sgn) for c in range(8)]
            if which == 1:
                gcol, bcol = V_LN1G + l * 8, V_LN1B + l * 8
                kA, kB, cpar = K_A2, K_B2, par
                cres = ['cfA2_%d' % par, 'cfB2_%d' % par]
            else:
                gcol, bcol = V_LN2G + l * 8, V_LN2B + l * 8
                kA, kB, cpar = K_A1, K_B1, (l + 1) % 2
                cres = ['cfA1_%d' % cpar, 'cfB1_%d' % cpar]

            def head_a():
                ACT(lnm[:], ps[:, sb0, :], AF.Identity, ['pb%d' % sb0], ['lnm'], scale=1.0 / D)

            def head_b():
                TT(lnq[:], lnm[:], lnm[:], ALU.mult, ['lnm'], ['lnq'])
                STT(lnq[:], ps[:, sb1, :], 1.0 / D, lnq[:], ALU.mult, ALU.subtract, ['pb%d' % sb1, 'lnq'], ['lnq'])
                TS(lnq[:], lnq[:], EPSP, None, ALU.add, None, ['lnq'], ['lnq'])

            def head_c():
                ACT(lnq[:], lnq[:], AF.Ln, ['lnq'], ['lnq'])
                ACT(lnq[:], lnq[:], AF.Exp, ['lnq'], ['lnq'], scale=-0.5)

            def chunk(c):
                def f():
                    xv = xT[:, c, c0:c0 + 512]
                    TT(xv, xv, lnm[:], ALU.subtract, [xres[c], 'lnm'], [xres[c]])
                    TT(xv, xv, lnq[:], ALU.mult, [xres[c], 'lnq'], [xres[c]])
                    if not last:
                        if c % 2 == 0:
                            ACT(hT[:, c, c0:c0 + 512], xv, AF.Identity, [xres[c]] + cres, ['hT%d%s' % (c, sgn)], bias=cf(cpar, kB, cond, c), scale=cf(cpar, kA, cond, c))
                        else:
                            TS(hT[:, c, c0:c0 + 512], xv, cf(cpar, kA, cond, c), cf(cpar, kB, cond, c), ALU.mult, ALU.add, [xres[c]] + cres, ['hT%d%s' % (c, sgn)])
                    ACT(xv, xv, AF.Identity, [xres[c], 'vecs'], [xres[c]], bias=vecs[:, bcol + c:bcol + c + 1], scale=vecs[:, gcol + c:gcol + c + 1])
                return f
            return [head_a, head_b, head_c] + [chunk(c) for c in range(8)]

        def store_block(tb):
            sgn = 'P' if tb == 0 else 'S'
            for tbk in range(4 * tb, 4 * tb + 4):
                b = tbk % 2
                s = 2 + tbk % 2
                for c in range(8):
                    TR(slot_flat(s)[:, c * 128:(c + 1) * 128], xT[:, c, tbk * 128:(tbk + 1) * 128], ident32[:], ['xT%d%s' % (c, sgn), 'ident32'], slot_res(s))
                ACP(ta[b], slot_flat(s), slot_res(s), ['ta%d' % b])
                DMA('sp', y_out[tbk * 128:(tbk + 1) * 128, :], ta[b], ['ta%d' % b], ['y_out%d' % b], 'yo%d' % b)

        def filler(n):
            if n <= 0:
                return
            s = next_slot()
            for i in range(n):
                MM(ps[:, 2 * s, :], ones[:], maskb[:, 0, :], True, True, ['ones', 'maskb'], ['pb%d' % (2 * s)])

        def proj_ln(l, which, last, lhs_fn, nk, rhs_fn, rres_fn, wres, gkind, gres, nfill=0, defer_tail=False):
            par = l % 2
            reserved.update([0, 1, 2, 3])
            bctr = [0]
            prev_tb = None
            lo_rec = max(_bank_recency(bk_) for bk_ in range(0, 4))
            hi_rec = max(_bank_recency(bk_) for bk_ in range(4, 8))
            ubase, sbase = (4, 0) if hi_rec <= lo_rec or last else (0, 4)
            for ti, tb in enumerate((1, 2, 0)):
                sgn = 'P' if tb == 0 else 'S'
                cond = 0 if tb == 0 else 1
                c0 = tb * 512
                sb0, sb1 = (sbase, sbase + 1) if ti % 2 == 0 else (sbase + 2, sbase + 3)
                units = []
                for dc in range(8):
                    def s1(dc=dc, sgn=sgn, cond=cond, c0=c0):
                        bk = ubase + bctr[0] % 4
                        bctr[0] += 1
                        lhs = lhs_fn(dc)
                        for k in range(nk):
                            MM(ps[:, bk, :], lhs[k], rhs_fn(k, c0, c0 + 512), k == 0, k == nk - 1, wres + rres_fn(sgn), ['pb%d' % bk])
                        xv = xT[:, dc, c0:c0 + 512]
                        STT(xv, ps[:, bk, :], cf(par, gkind, cond, dc), xv, ALU.mult, ALU.add, ['pb%d' % bk, gres, 'xT%d%s' % (dc, sgn)], ['xT%d%s' % (dc, sgn)])
                        b = dc % 2
                        ACP(ybf[b][:], xv, ['xT%d%s' % (dc, sgn)], ['ybf%d' % b])
                        ACT(ysq[b][:], xv, AF.Square, ['xT%d%s' % (dc, sgn)], ['ysq%d' % b])
                        emit_pending((1, 1, 1, 2, 2, 2, 1, 1)[dc])

                    def s3(dc=dc, sb0=sb0, sb1=sb1):
                        b = dc % 2
                        MM(ps[:, sb0, :], ones[:], ybf[b][:], dc == 0, dc == 7, ['ones', 'ybf%d' % b], ['pb%d' % sb0])
                        MM(ps[:, sb1, :], ones[:], ysq[b][:], dc == 0, dc == 7, ['ones', 'ysq%d' % b], ['pb%d' % sb1])
                    units.append([s1, None, s3])
                run_pipeline(units)
                emit_pending(len(pending))
                if last and prev_tb is not None:
                    store_block(prev_tb)
                prev_tb = tb
                pending.extend(ln_tail_pieces(l, which, last, tb, sb0, sb1))
            reserved.difference_update([0, 1, 2, 3])
            filler(nfill)
            emit_pending(3 if (defer_tail and not last) else len(pending))
            if last:
                store_block(0)

        for l in range(depth):
          try:
            par = l % 2
            lastl = (l == depth - 1)
            DMA('sp', cstage[:, 0, :, :], ck[l].rearrange("(b p) d -> p b d", p=128), [], ['sB'], 'cst')
            DMA('sp', cstage[:, 1, :, :], cv[l].rearrange("(b p) d -> p b d", p=128), [], ['sB'], 'cst')
            DMA('sp', sink8[:], sink_d[l:l + 1, :], [], ['sink8'], 'snk')
            DMA('pool', wpool[par][:], w_pool[l].rearrange("g c d -> c g d"), [], ['wpool%d' % par], 'wpool%d' % par)
            ACT(sinkf8[:], sink8[:], AF.Exp, ['sink8'], ['sinkf8'])
            VCP(sinkh8[:], sinkf8[:], ['sinkf8'], ['sinkh8'])
            TT(sinkf8[:], sinkf8[:], sinkh8[:], ALU.subtract, ['sinkf8', 'sinkh8'], ['sinkf8'])
            VCP(sinkl8[:], sinkf8[:], ['sinkf8'], ['sinkl8'])
            VCP(sinkhl[0:1, :].rearrange("p (a b) -> p a b", a=8), sinkh8[0:1, :].unsqueeze(2).to_broadcast([1, 8, 128]), ['sinkh8', 'sinkhl'], ['sinkhl'])
            VCP(sinkhl[32:33, :].rearrange("p (a b) -> p a b", a=8), sinkl8[0:1, :].unsqueeze(2).to_broadcast([1, 8, 128]), ['sinkl8', 'sinkhl'], ['sinkhl'])
            P.add('dve', 'memset', (Vb[:, :, 64:192], 1.0), None, [], ['VbP', 'VbS'])
            P.add('dve', 'memset', (vcb[:, :, 64:192], 1.0), None, [], ['vcb'])
            s = next_slot()
            for b in range(2):
                TR(ps[:, 2 * s, b * 128:(b + 1) * 128], cstage[:, 0, b, :], ident32[:], ['sB', 'ident32'], slot_res(s))
            ACP(kcT, ps[:, 2 * s, 0:256], slot_res(s), ['kcT'])
            VCP(vcb[:, :, 0:64], cstage[:, 1, :, 0:64], ['sB', 'vcb'], ['vcb'])
            VCP(vcb[:, :, 192:256], cstage[:, 1, :, 64:128], ['sB', 'vcb'], ['vcb'])

            iq = ring_fill([
                ((lambda r, c=c, hh=hh: v3(r, 8)[:, :, c * 128 + hh * 64:c * 128 + hh * 64 + 64]),
                 w_in[l, :, (hh * 4 + c) * 64:(hh * 4 + c) * 64 + 64].rearrange("(k p) n -> p k n", p=128))
                for c in range(4) for hh in range(2)])
            ikv = ring_fill([(lambda r: v3(r, 8)[:, :, 0:256], w_in[l, :, 512:768].rearrange("(k p) n -> p k n", p=128))])
            ipp = ring_fill([(lambda r: v3(r, 8), w_in[l, :, 768:1280].rearrange("(k p) n -> p k n", p=128))])
            pinned.update([iq, ikv, ipp])
            wq = v3(ring[iq], 8)
            wkv = v3(ring[ikv], 8)
            wpp = v3(ring[ipp], 8)

            def proj_stage(box, sgn, lhs, rres):
                def f():
                    box['s'] = next_slot()
                    mm_fm(box['s'], sgn, lhs, hrhs, rres + hres[sgn])
                return f

            def rope_stage_a(box):
                def f():
                    s = box['s']
                    for h in range(2):
                        ACP(q32[h], ps[:, 2 * s + h, :], ['pb%d' % (2 * s + h)], ['q32%d' % h])
                return f

            def rope_stage_b(dst, dres):
                def f():
                    s2 = next_slot()
                    for h in range(2):
                        MM(ps[:, 2 * s2 + h, :], rperm[:], q32[h], True, True, ['rperm', 'q32%d' % h], ['pb%d' % (2 * s2 + h)])
                    for h in range(2):
                        cs = slice(h * 512, (h + 1) * 512)
                        TT(t1, q32[h], ropeC[:, cs], ALU.mult, ['q32%d' % h, 'ropeC'], ['t1'])
                        TT(q32[h], ps[:, 2 * s2 + h, :], ropeS[:, cs], ALU.mult, ['pb%d' % (2 * s2 + h), 'ropeS', 'q32%d' % h], ['q32%d' % h])
                        TT(dst[:, cs], t1, q32[h], ALU.add, ['t1', 'q32%d' % h], dres)
                return f

            def copy_stage(box, dst, dres):
                def f():
                    s = box['s']
                    ACP(dst, ps[:, 2 * s, :], slot_res(s), dres)
                return f

            def pool_stage2(box, g, sgn, di):
                c0, n, nseq, L, _ = _seg(sgn)
                w = (2, 4, 8, 16)[g]
                hw_ = w // 2
                LP = L + 16

                def pv(buf, lo, hi):
                    return buf[:, 0:nseq * LP].rearrange("p (a b) -> p a b", a=nseq)[:, :, lo:hi]

                def f():
                    s = box['s']
                    P.add('dve', 'memset', (pv(p32, 0, 8), 0.0), None, [], ['p32'])
                    P.add('dve', 'memset', (pv(p32, LP - 8, LP), 0.0), None, ['p32'], ['p32'])
                    P.add('act', 'copy', (pv(p32, 8, 8 + L), slot_flat(s)[:, 0:n].rearrange("p (a b) -> p a b", a=nseq)), None, slot_res(s) + ['p32'], ['p32'])
                    TT(pv(sA, 1, LP), pv(p32, 1, LP), pv(p32, 0, LP - 1), ALU.add, ['p32', 'sA'], ['sA'])
                    src, dst, sn, dn = sA, sB, 'sA', 'sB'
                    lo, hi, sh = 1, LP, 1
                    for _ in range(g):
                        TT(pv(dst, lo + sh, hi - sh), pv(src, lo + 2 * sh, hi), pv(src, lo, hi - 2 * sh), ALU.add, [sn, dn], [dn])
                        lo, hi = lo + sh, hi - sh
                        src, dst, sn, dn = dst, src, dn, sn
                        sh *= 2
                    tot, tn = src, sn
                    eb = (g * 2 + (0 if sgn == 'P' else 1)) * 32
                    ev = ecf[:, eb:eb + nseq * 8].rearrange("p (a b) -> p a b", a=nseq)
                    TT(pv(tot, 8, 8 + hw_), pv(tot, 8, 8 + hw_), ev[:, :, 0:hw_], ALU.mult, [tn, 'ecf'], [tn])
                    if hw_ > 1:
                        ev2 = ecf[:, eb + 16:eb + 16 + nseq * 8].rearrange("p (a b) -> p a b", a=nseq)
                        TT(pv(tot, 8 + L - hw_ + 1, 8 + L), pv(tot, 8 + L - hw_ + 1, 8 + L), ev2[:, :, 0:hw_ - 1], ALU.mult, [tn, 'ecf'], [tn])
                    STT(dTt[di][:, 0:n].rearrange("p (a b) -> p a b", a=nseq), pv(tot, 8, 8 + L), 1.0 / w, pv(p32, 8, 8 + L), ALU.mult, ALU.subtract, [tn, 'p32'], ['dT%d' % di])
                return f

            def pool_stage3(g, sgn, di):
                c0, n, nseq, L, _ = _seg(sgn)

                def f():
                    s = next_slot()
                    for h in range(n // 512):
                        MM(ps[:, 2 * s + h, :], wpool[par][:, g, :], dTt[di][:, h * 512:(h + 1) * 512], True, True, ['wpool%d' % par, 'dT%d' % di], ['pb%d' % (2 * s + h)])
                    psc = V_PSC + l * 4 + g
                    ACT(poolT[:, g, c0:c0 + n], slot_flat(s)[:, 0:n], AF.Identity, slot_res(s) + ['vecs'], ['poolT%d%s' % (g, sgn)], scale=vecs[:, psc:psc + 1])
                return f

            def kvp_units():
                box = {}

                def s1():
                    s = next_slot()
                    box['s'] = s
                    pv4 = slot_flat(s).rearrange("p (a b) -> p a b", a=4)
                    for tbk in range(4):
                        for k in range(8):
                            MM(pv4[:, tbk, :], hT[:, k, tbk * 128:(tbk + 1) * 128], wkv[:, k, 0:256], k == 0, k == 7, ['ring%d' % ikv] + hres['P'], slot_res(s))

                def s2():
                    s = box['s']
                    pv4 = slot_flat(s).rearrange("p (a b) -> p a b", a=4)
                    ACP(kvst, pv4, slot_res(s), ['sA'])
                    ACP(Vb[:, 0:4, 0:64], pv4[:, :, 128:192], slot_res(s) + ['VbP'], ['VbP'])
                    ACP(Vb[:, 0:4, 192:256], pv4[:, :, 192:256], slot_res(s) + ['VbP'], ['VbP'])
                    for b in range(2):
                        DMA('sp', nk_out[b, l].rearrange("(k p) d -> p k d", p=128), kvst[:, 2 * b:2 * b + 2, 0:128], ['sA'], ['nk_out'], 'kvo')
                        DMA('sp', nv_out[b, l].rearrange("(k p) d -> p k d", p=128), kvst[:, 2 * b:2 * b + 2, 128:256], ['sA'], ['nv_out'], 'kvo')
                return [s1, s2]

            def vs_units():
                box = {}

                def s1():
                    s = next_slot()
                    box['s'] = s
                    pv8 = slot_flat(s).rearrange("p (a b) -> p a b", a=8)
                    for tbk in range(8):
                        for k in range(8):
                            MM(pv8[:, tbk, :], hT[:, k, 512 + tbk * 128:512 + (tbk + 1) * 128], wkv[:, k, 128:256], k == 0, k == 7, ['ring%d' % ikv] + hres['S'], slot_res(s))

                def s2():
                    s = box['s']
                    pv8 = slot_flat(s).rearrange("p (a b) -> p a b", a=8)
                    ACP(Vb[:, 4:12, 0:64], pv8[:, :, 0:64], slot_res(s) + ['VbS'], ['VbS'])
                    ACP(Vb[:, 4:12, 192:256], pv8[:, :, 64:128], slot_res(s) + ['VbS'], ['VbS'])
                return [s1, s2]

            klhs = [wkv[:, k, 0:128] for k in range(8)]

            def qu(c, sgn):
                qlhs = [wq[:, k, c * 128:(c + 1) * 128] for k in range(8)]
                box = {}
                if sgn == 'P':
                    return [proj_stage(box, 'P', qlhs, ['ring%d' % iq]), copy_stage(box, qT[:, c, 0:512], ['qT%dP' % c])]
                return [proj_stage(box, 'S', qlhs, ['ring%d' % iq]), rope_stage_a(box), None, rope_stage_b(qT[:, c, 512:1536], ['qT%dS' % c])]

            def pu(c, sgn):
                plhs = [wpp[:, k, c * 128:(c + 1) * 128] for k in range(8)]
                box = {}
                di = 0 if sgn == 'P' else 1
                return [proj_stage(box, sgn, plhs, ['ring%d' % ipp]), pool_stage2(box, c, sgn, di), None, pool_stage3(c, sgn, di)]

            def ku(sgn):
                box = {}
                if sgn == 'P':
                    return [proj_stage(box, 'P', klhs, ['ring%d' % ikv]), copy_stage(box, kT[:, 0:512], ['kTP'])]
                return [proj_stage(box, 'S', klhs, ['ring%d' % ikv]), rope_stage_a(box), None, rope_stage_b(kT[:, 512:1536], ['kTS'])]

            def fill_stage(u, n):
                u[2] = (lambda: filler(n))
                return u

            unitsA = [ku('S'), pu(0, 'S'), qu(0, 'S'), pu(1, 'S'), qu(1, 'S'), pu(2, 'S'), qu(2, 'S'), pu(3, 'S'), qu(3, 'S'), vs_units(),
                      ku('P'), qu(0, 'P'), kvp_units(), qu(1, 'P'), pu(0, 'P'), qu(2, 'P'), pu(1, 'P'), qu(3, 'P'),
                      fill_stage(pu(2, 'P'), FILL_AB), [lambda: None], fill_stage(pu(3, 'P'), FILL_AB)]
            if pending:
                for pos in (8, 7, 6, 5, 4, 3, 2, 1):
                    unitsA.insert(pos, [lambda: emit_pending(1)])
            if l == 0:
                for i_, pc in enumerate(range(4, 12)):
                    unitsA.insert(3 + 2 * i_, [lambda pc=pc: mods_for_layer(0, [pc])])
            run_pipeline(unitsA)
            pinned.difference_update([iq, ikv, ipp])
            if stop == 'A':
                raise _Stop()

            coef_mid(l)
            aunits = []
            acc_ctr = [0]

            def add_qblock(q0, kblocks, qres, ores):
                for kvh in range(2):
                    acc = acc_ctr[0] % 4
                    acc_ctr[0] += 1
                    groups = [kblocks[g0:g0 + 2] for g0 in range(0, len(kblocks), 2)]
                    for gi, grp in enumerate(groups):
                        aunits.append(dict(q0=q0, kvh=kvh, grp=grp, first=(gi == 0), last=(gi == len(groups) - 1), acc=acc, qres=qres, ores=ores))

            def att_S(u, n):
                def f():
                    sl = 2 + n % 2
                    h0 = u['kvh'] * 64
                    q0 = u['q0']
                    for j, (kap, vap, mi, kres, vres) in enumerate(u['grp']):
                        bank = 2 * sl + j
                        if mi is not None:
                            MM(ps[:, bank, :], identb[:], maskb[:, mi, :], True, False, ['identb', 'maskb'], ['pb%d' % bank])
                        MM(ps[:, bank, :], kap[h0:h0 + 64, :], qT[h0:h0 + 64, :, q0:q0 + 128], mi is None, True, [kres] + u['qres'], ['pb%d' % bank])
                return f

            def att_PV(u, n):
                def f():
                    sl = 2 + n % 2
                    pti = n % 3
                    kvh = u['kvh']
                    h0 = kvh * 64
                    s0 = 64 - h0
                    q0 = u['q0']
                    accb = u['acc']
                    grp = u['grp']
                    ng = len(grp)
                    ACT(PT[pti][:, 0:ng, :], ps[:, 2 * sl:2 * sl + ng, :], AF.Exp, ['pb%d' % (2 * sl + j) for j in range(ng)], ['PT%d' % pti], scale=SCALE)
                    if u['first']:
                        MM(ps[:, accb, :], sinkL[kvh][:], sinkhl[:, kvh * 512:(kvh + 1) * 512], True, False, ['sinkL', 'sinkhl'], ['pb%d' % accb])
                    for j, (kap, vap, mi, kres, vres) in enumerate(grp):
                        MM(ps[:, accb, :], vap[:, kvh * 128:(kvh + 1) * 128], PT[pti][:, j, :], False, u['last'] and j == ng - 1, [vres, 'PT%d' % pti], ['pb%d' % accb])
                return f

            def att_NORM(u, n):
                if not u['last']:
                    return None

                def f():
                    kvh = u['kvh']
                    h0 = kvh * 64
                    s0 = 64 - h0
                    q0 = u['q0']
                    accb = u['acc']
                    if kvh == 0:
                        P.add('dve', 'reciprocal', (rcp[kvh][s0:s0 + 64, :], ps[s0:s0 + 64, accb, :]), None, ['pb%d' % accb], [rcpn[kvh]])
                    else:
                        ACT(rcp[kvh][s0:s0 + 64, :], ps[s0:s0 + 64, accb, :], AF.Ln, ['pb%d' % accb], [rcpn[kvh]])
                        ACT(rcp[kvh][s0:s0 + 64, :], rcp[kvh][s0:s0 + 64, :], AF.Exp, [rcpn[kvh]], [rcpn[kvh]], scale=-1.0)
                    TT(hT[h0:h0 + 64, 0:4, q0:q0 + 128], ps[h0:h0 + 64, accb, :].rearrange("p (a b) -> p a b", a=4),
                       rcp[kvh][s0:s0 + 64, :].rearrange("p (a b) -> p a b", a=4), ALU.mult, ['pb%d' % accb, rcpn[kvh]], u['ores'])
                return f

            for i in range(8):
                kbl = [(kcT[:, kb * 128:(kb + 1) * 128], vcb[:, kb, :], None, 'kcT', 'vcb') for kb in range(2)]
                for j in (i - 1, i, i + 1):
                    if j < 0 or j > 7:
                        continue
                    mi = None if j == i else (0 if j == i - 1 else 1)
                    kbl.append((kT[:, 512 + j * 128:512 + (j + 1) * 128], Vb[:, 4 + j, :], mi, 'kTS', 'VbS'))
                add_qblock(512 + i * 128, kbl, ['qT%dS' % c for c in range(4)], ['hT%dS' % c for c in range(4)])
            for b in range(2):
                kbl = [(kT[:, b * 256 + kb * 128:b * 256 + (kb + 1) * 128], Vb[:, 2 * b + kb, :], None, 'kTP', 'VbP') for kb in range(2)]
                for qb in range(2):
                    add_qblock(b * 256 + qb * 128, kbl, ['qT%dP' % c for c in range(4)], ['hT%dP' % c for c in range(4)])
            def att_warm():
                for i in range(FILL_ATT):
                    MM(ps[:, 3, :], ones[:], maskb[:, 0, :], True, True, ['ones', 'maskb'], ['pb3'])

            att_stages = [[att_S(u, n), att_PV(u, n), None, att_NORM(u, n)] for n, u in enumerate(aunits)]
            s0_ = att_stages[0][0]
            att_stages[0][0] = (lambda: (s0_(), att_warm()))
            run_pipeline(att_stages, reverse=False)
            if stop == 'B':
                raise _Stop()

            filler(FILL_BC)

            ios = []
            for half in range(2):
                c_a, c_b = half * 512, (half + 1) * 512
                ios.append(ring_fill([
                    (lambda r: v3(r, 8)[0:64, 0:4, :], w_out[l, 0:256, c_a:c_b].rearrange("(c p) n -> p c n", p=64)),
                    (lambda r: v3(r, 8)[64:128, 0:4, :], w_out[l, 256:512, c_a:c_b].rearrange("(c p) n -> p c n", p=64)),
                    (lambda r: v3(r, 8)[:, 4:8, :], w_out[l, 512:1024, c_a:c_b].rearrange("(c p) n -> p c n", p=128)),
                ]))
            wos = [v3(ring[i], 8) for i in ios]
            proj_ln(l, 1, False, lambda dc: [wos[dc // 4][:, k, (dc % 4) * 128:(dc % 4 + 1) * 128] for k in range(8)], 8,
                    mixrhs, lambda sgn: mixres[sgn], ['ring%d' % i for i in ios], K_G1P, 'cfG1_%d' % par, defer_tail=True)
            if stop == 'C':
                raise _Stop()
            P.transfer(MIX_RES, FFN_RES)

            nmod_done = [0]

            def next_mod_piece():
                if (not lastl) and nmod_done[0] < 12:
                    mods_for_layer(l + 1, [nmod_done[0]])
                    nmod_done[0] += 1

            def cw(tap, ch):
                cc = V_CW + (l * 3 + tap) * 44 + ch
                return vecs[:, cc:cc + 1]

            def cb(ch):
                cc = V_CB + l * 44 + ch
                return vecs[:, cc:cc + 1]

            tb_ctr = [0]
            j0 = 0
            for hf in range(2):
                npairs = HALF_PAIRS[hf]
                for jp in range(npairs // 2):
                    jA = j0 + 2 * jp
                    iu = ring_fill([
                        (lambda r: v3(r, 8)[:, :, 0:256], w_up[l, :, jA * 128:(jA + 2) * 128].rearrange("(k p) n -> p k n", p=128)),
                        (lambda r: v3(r, 8)[:, :, 256:512], w_up[l, :, 2816 + jA * 128:2816 + (jA + 2) * 128].rearrange("(k p) n -> p k n", p=128)),
                    ])
                    wu = v3(ring[iu], 8)
                    first_slot = (hf == 0 and jp == 0)
                    last_slot = (jp == npairs // 2 - 1)
                    order = [(0, 'S'), (1, 'S'), (0, 'P'), (1, 'P')] if (first_slot or last_slot) else [(0, 'P'), (0, 'S'), (1, 'P'), (1, 'S')]
                    for oi, (jj, sgn) in enumerate(order):
                        j = jA + jj
                        jm = j - j0
                        la = [wu[:, k, jj * 128:(jj + 1) * 128] for k in range(8)]
                        lg = [wu[:, k, 256 + jj * 128:256 + (jj + 1) * 128] for k in range(8)]
                        for sgn in (sgn,):
                            c0, n, nseq, L, _ = _seg(sgn)
                            tbi = tb_ctr[0] % 2
                            tb_ctr[0] += 1
                            rr = ['ring%d' % iu] + hres[sgn]
                            if sgn == 'P':
                                s = next_slot()
                                for k in range(8):
                                    MM(ps[:, 2 * s, :], la[k], hT[:, k, 0:512], k == 0, k == 7, rr, ['pb%d' % (2 * s)])
                                for k in range(8):
                                    MM(ps[:, 2 * s + 1, :], lg[k], hT[:, k, 0:512], k == 0, k == 7, rr, ['pb%d' % (2 * s + 1)])
                                srcs = [(ps[:, 2 * s, :], ['pb%d' % (2 * s)], ta[tbi], 'ta%d' % tbi, j), (ps[:, 2 * s + 1, :], ['pb%d' % (2 * s + 1)], tg[tbi], 'tg%d' % tbi, 22 + j)]
                            else:
                                s = next_slot()
                                mm_fm(s, 'S', la, hrhs, rr)
                                s2 = next_slot()
                                mm_fm(s2, 'S', lg, hrhs, rr)
                                srcs = [(slot_flat(s), slot_res(s), ta[tbi], 'ta%d' % tbi, j), (slot_flat(s2), slot_res(s2), tg[tbi], 'tg%d' % tbi, 22 + j)]
                            for (src, sres, tbuf, tname, ch) in srcs:
                                sv = src.rearrange("p (a b) -> p a b", a=nseq)
                                tv = tbuf[:, 0:n].rearrange("p (a b) -> p a b", a=nseq)
                                ACT(tv, sv, AF.Identity, sres + ['vecs'], [tname], bias=cb(ch), scale=cw(1, ch))
                                STT(tv[:, :, 1:L], sv[:, :, 0:L - 1], cw(0, ch), tv[:, :, 1:L], ALU.mult, ALU.add, sres + ['vecs', tname], [tname])
                                STT(tv[:, :, 0:L - 1], sv[:, :, 1:L], cw(2, ch), tv[:, :, 0:L - 1], ALU.mult, ALU.add, sres + ['vecs', tname], [tname])
                            ACT(tg[tbi][:, 0:n], tg[tbi][:, 0:n], AF.Silu, ['tg%d' % tbi], ['tg%d' % tbi])
                            TT(mT[:, jm, c0:c0 + n], ta[tbi][:, 0:n], tg[tbi][:, 0:n], ALU.mult, ['ta%d' % tbi, 'tg%d' % tbi], ['mT%d%s' % (jm, sgn)])
                            if first_slot and oi < 2:
                                emit_pending(4 if oi == 0 else len(pending))
                    next_mod_piece()
                if hf == 0:
                    for dcp in range(4):
                        idn = ring_fill([(lambda r: r[:, 0:npairs * 256].rearrange("p (a b) -> p a b", a=npairs),
                                          w_down[l, j0 * 128:(j0 + npairs) * 128, dcp * 256:(dcp + 1) * 256].rearrange("(k p) n -> p k n", p=128))])
                        wd = ring[idn][:, 0:npairs * 256].rearrange("p (a b) -> p a b", a=npairs)
                        for d2, sgn in ((0, 'S'), (1, 'S'), (0, 'P'), (1, 'P')):
                            dc = dcp * 2 + d2
                            lhs = [wd[:, k, d2 * 128:(d2 + 1) * 128] for k in range(npairs)]
                            for sgn in (sgn,):
                                c0, n, _, _, cond = _seg(sgn)
                                s = next_slot()
                                mm_fm(s, sgn, lhs, lambda k, a, b: mT[:, k, a:b], ['ring%d' % idn] + ['mT%d%s' % (k, sgn) for k in range(npairs)])
                                xv = xT[:, dc, c0:c0 + n]
                                STT(xv, slot_flat(s)[:, 0:n], cf(par, K_G2P, cond, dc), xv, ALU.mult, ALU.add, slot_res(s) + ['cfG2_%d' % par, 'xT%d%s' % (dc, sgn)], ['xT%d%s' % (dc, sgn)])
                    next_mod_piece()
                else:
                    while (not lastl) and nmod_done[0] < 12:
                        next_mod_piece()
                    if not lastl:
                        coef_next(l)
                    idns = []
                    for dq in range(2):
                        idns.append(ring_fill([(lambda r: r[:, 0:npairs * 512].rearrange("p (a b) -> p a b", a=npairs),
                                                w_down[l, j0 * 128:(j0 + npairs) * 128, dq * 512:(dq + 1) * 512].rearrange("(k p) n -> p k n", p=128))]))
                    wds = [ring[i][:, 0:npairs * 512].rearrange("p (a b) -> p a b", a=npairs) for i in idns]
                    proj_ln(l, 2, lastl, lambda dc: [wds[dc // 4][:, k, (dc % 4) * 128:(dc % 4 + 1) * 128] for k in range(npairs)], npairs,
                            lambda k, a, b: mT[:, k, a:b], lambda sgn: ['mT%d%s' % (k, sgn) for k in range(npairs)], ['ring%d' % i for i in idns], K_G2P, 'cfG2_%d' % par, nfill=0, defer_tail=True)
                j0 += npairs
            P.transfer(FFN_RES, IO_RES if lastl else MIX_RES)
          except _Stop:
            P.transfer(MIX_RES + FFN_RES, IO_RES)
            break

        for tb in (range(12) if (stop is not None or depth == 0) else []):
            b = tb % 2
            sgn = 'P' if tb < 4 else 'S'
            s = next_slot()
            for c in range(8):
                TR(slot_flat(s)[:, c * 128:(c + 1) * 128], xT[:, c, tb * 128:(tb + 1) * 128], ident32[:], ['xT%d%s' % (c, sgn), 'ident32'], slot_res(s))
            if tb % 2 == 0:
                ACP(xs[b], slot_flat(s), slot_res(s), ['xs%d' % b])
            else:
                VCP(xs[b], slot_flat(s), slot_res(s), ['xs%d' % b])
            DMA('sp', y_out[tb * 128:(tb + 1) * 128, :], xs[b], ['xs%d' % b], ['y_out%d' % b], 'yo%d' % b)
        P.add('sp', None, (), None, ['y_out0', 'y_out1', 'nk_out', 'nv_out'], [])
        P.emit(nc, st)
        nc._prog_stats = (len(P.ops), P.n_waits)
    return nc


def _constants():
    half = 32
    inv = (10000.0 ** (-np.arange(0, half, 2, dtype=np.float32) / half)).astype(np.float32)
    t = np.arange(1024)
    row = (t // 64).astype(np.float32)
    col = (t % 64).astype(np.float32)
    C = np.zeros((128, 1024), np.float32)
    S = np.zeros((128, 1024), np.float32)
    for p in range(128):
        d = p % 64
        pos = row if d < 32 else col
        f = inv[(d % 32) % 16]
        ang = (pos * f).astype(np.float32)
        C[p] = np.cos(ang)
        S[p] = np.sin(ang)
    R = np.zeros((128, 128), np.float32)
    for m in range(128):
        if (m % 32) < 16:
            R[m + 16, m] = -1.0
        else:
            R[m - 16, m] = 1.0
    ident = np.eye(128, dtype=np.float32)
    cc = np.arange(128)[:, None]
    rr = np.arange(128)[None, :]
    mA = np.where(cc >= rr, 0.0, NEG).astype(np.float32)
    mB = np.where(cc <= rr, 0.0, NEG).astype(np.float32)
    masks = np.concatenate([np.tile(mA, (1, 4)), np.tile(mB, (1, 4))], axis=1)
    ecf = np.ones((128, 256), np.float32)
    for g, w in enumerate((2, 4, 8, 16)):
        hw = w // 2
        for si in range(2):
            base = (g * 2 + si) * 32
            for sq in range(2):
                for i in range(hw):
                    ecf[:, base + sq * 8 + i] = w / float(i + hw)
                for i in range(hw - 1):
                    ecf[:, base + 16 + sq * 8 + i] = w / float(2 * hw - 1 - i)
    return C, S, R, ident, masks, ecf


def _run(inputs, depth=DEPTH, trace=False, stop=None):
    f = lambda a: np.ascontiguousarray(np.asarray(a, dtype=np.float32))
    x_prompt, x_sample = f(inputs['x_prompt']), f(inputs['x_sample'])
    cache_k, cache_v = f(inputs['cache_k']), f(inputs['cache_v'])
    c, c_ctx = f(inputs['c']), f(inputs['c_ctx'])
    C, S, R, ident, masks, ecf = _constants()
    common = np.concatenate([
        f(inputs['b_mod']).reshape(-1, 128), f(inputs['pool_scale']).reshape(-1, 128),
        f(inputs['ln1_g']).reshape(-1, 128), f(inputs['ln1_b']).reshape(-1, 128),
        f(inputs['ln2_g']).reshape(-1, 128), f(inputs['ln2_b']).reshape(-1, 128),
        f(inputs['conv_w']).reshape(-1, 128), f(inputs['conv_b']).reshape(-1, 128),
        c_ctx.reshape(8, 128)], axis=0)
    assert common.shape[0] == V_COND + 8
    shared = {
        'ropeC': C, 'ropeS': S, 'rperm': R, 'ident': ident, 'masks': masks, 'ecf': ecf,
        'sink': f(inputs['attn_sink']),
        'w_mod': f(inputs['w_mod']), 'w_in': f(inputs['w_in']), 'w_pool': f(inputs['w_pool']),
        'w_out': f(inputs['w_out']), 'w_up': f(inputs['w_up']), 'w_down': f(inputs['w_down']),
    }
    in_maps = []
    for core in range(8):
        vp = np.zeros((V_ROWS, 128), np.float32)
        vp[:common.shape[0]] = common
        vp[V_COND + 8:V_COND + 16] = c[core].reshape(8, 128)
        m = dict(shared)
        m['x_in'] = np.ascontiguousarray(np.concatenate([x_prompt[2 * core], x_prompt[2 * core + 1], x_sample[core]], axis=0))
        m['ck'] = np.ascontiguousarray(cache_k[core].reshape(DEPTH, 256, 128))
        m['cv'] = np.ascontiguousarray(cache_v[core].reshape(DEPTH, 256, 128))
        m['vecpack'] = vp
        in_maps.append(m)
    nc = build_program(depth, stop)
    res = run_bass_kernel_spmd(nc, in_maps, core_ids=list(range(8)), trace=trace)
    y_prompt = np.zeros((16, 256, D), np.float32)
    y_sample = np.zeros((8, 1024, D), np.float32)
    nk = np.zeros((16, DEPTH, 256, 2, 64), np.float32)
    nv = np.zeros((16, DEPTH, 256, 2, 64), np.float32)
    for core in range(8):
        r = res.results[core]
        y = np.asarray(r['y'])
        y_prompt[2 * core] = y[0:256]
        y_prompt[2 * core + 1] = y[256:512]
        y_sample[core] = y[512:1536]
        nk[2 * core:2 * core + 2] = np.asarray(r['nk']).reshape(2, DEPTH, 256, 2, 64)
        nv[2 * core:2 * core + 2] = np.asarray(r['nv']).reshape(2, DEPTH, 256, 2, 64)
    return (y_prompt, y_sample, nk, nv), res


def kernel(**inputs):
    outs, _ = _run(inputs)
    return outs
```

```python
from contextlib import ExitStack
import numpy as np
import concourse.bass as bass
import concourse.mybir as mybir
from concourse.bass_utils import run_bass_kernel_spmd

F32 = mybir.dt.float32
BF16 = mybir.dt.bfloat16
AF = mybir.ActivationFunctionType
ALU = mybir.AluOpType

DEPTH = 4
D = 1024
NTOK = 1536
ALPHA = (2 * DEPTH) ** 0.25
EPSP = 1e-5 / (ALPHA * ALPHA)
SCALE = 64 ** -0.5
NEG = -30000.0
NS = 5
HALF_PAIRS = (14, 8)
FILL_AB, FILL_BC, FILL_CD, FILL_EA = 16, 10, 30, 0
FILL_ATT = 20
FILL_LN = 16

V_BMOD = 0
V_PSC = 192
V_LN1G = 208
V_LN1B = 240
V_LN2G = 272
V_LN2B = 304
V_CW = 336
V_CB = 864
V_COND = 1040
V_ROWS = 1152


class _Op:
    __slots__ = ('eng', 'meth', 'args', 'kw', 'deps', 'signal', 'sig_val', 'dma_key', 'dma_val', 'dma_waits', 'idx')


class Prog:
    def __init__(self):
        self.ops = []
        self.last_w = {}
        self.readers = {}
        self.dma_count = {}

    def transfer(self, old, new):
        comb = {}
        dl = []
        for r in old:
            for w in self.last_w.get(r, ()):
                if w.dma_key is not None:
                    dl.append(w)
                elif w.eng not in comb or comb[w.eng].idx < w.idx:
                    comb[w.eng] = w
            for k, v in self.readers.get(r, {}).items():
                if k == '_dma':
                    dl.extend(v)
                elif k not in comb or comb[k].idx < v.idx:
                    comb[k] = v
        for n in new:
            self.last_w[n] = []
            rd = dict(comb)
            if dl:
                rd['_dma'] = list(dl)
            self.readers[n] = rd

    def add(self, eng, meth, args=(), kw=None, reads=(), writes=(), dma_key=None):
        op = _Op()
        op.eng = eng
        op.meth = meth
        op.args = args
        op.kw = kw or {}
        op.dma_key = dma_key
        op.deps = []
        op.signal = False
        op.sig_val = None
        op.dma_waits = {}
        op.idx = len(self.ops)
        deps = {}

        def adddep(d):
            if d is None or d is op:
                return
            if eng == 'pe' and d.eng == 'pe' and d.dma_key is None:
                return
            deps[d.idx] = d

        for r in reads:
            for w in self.last_w.get(r, ()):
                adddep(w)
            if r.startswith('pb'):
                for k, v in self.readers.get(r, {}).items():
                    if k != eng and k != '_dma':
                        adddep(v)
        for w_ in writes:
            for w in self.last_w.get(w_, ()):
                adddep(w)
            rd = self.readers.get(w_)
            if rd:
                for k, v in rd.items():
                    if k == '_dma':
                        for x in v:
                            adddep(x)
                    else:
                        adddep(v)
        for d in deps.values():
            if d.dma_key is not None:
                k = d.dma_key
                op.dma_waits[k] = max(op.dma_waits.get(k, 0), self.dma_count[k])
            else:
                op.deps.append(d)
        for r in reads:
            rd = self.readers.setdefault(r, {})
            if dma_key is not None:
                rd.setdefault('_dma', []).append(op)
            else:
                rd[eng] = op
        for w_ in writes:
            self.last_w[w_] = [op]
            self.readers[w_] = {}
        if dma_key is not None:
            self.dma_count[dma_key] = self.dma_count.get(dma_key, 0) + 16
            op.dma_val = self.dma_count[dma_key]
        self.ops.append(op)
        return op

    def emit(self, nc, stack):
        for op in self.ops:
            for d in op.deps:
                d.signal = True
        cnt = {}
        for op in self.ops:
            if op.dma_key is None and op.signal:
                cnt[op.eng] = cnt.get(op.eng, 0) + 1
                op.sig_val = cnt[op.eng]
        esem = {}
        for e in ('pe', 'act', 'dve', 'pool', 'sp'):
            esem[e] = stack.enter_context(nc.semaphore("s_" + e))
        dsem = {}
        for k in self.dma_count:
            dsem[k] = stack.enter_context(nc.semaphore("d_%s" % k))
        byeng = {}
        for op in self.ops:
            byeng.setdefault(op.eng, []).append(op)
        block = stack.enter_context(nc.Block())
        self.n_waits = 0

        def run_engine(ename, eobj):
            waited = {}
            for op in byeng.get(ename, []):
                need = {}
                for d in op.deps:
                    key = ('e', d.eng)
                    need[key] = (esem[d.eng], max(need.get(key, (None, 0))[1], d.sig_val))
                for k, v in op.dma_waits.items():
                    key = ('d', k)
                    need[key] = (dsem[k], max(need.get(key, (None, 0))[1], v))
                for key, (s, v) in need.items():
                    if waited.get(key, 0) >= v:
                        continue
                    eobj.wait_ge(s, v)
                    self.n_waits += 1
                    waited[key] = v
                if op.meth is None:
                    continue
                ins = getattr(eobj, op.meth)(*op.args, **op.kw)
                if op.dma_key is not None:
                    ins.then_inc(dsem[op.dma_key], 16)
                elif op.signal:
                    ins.then_inc(esem[op.eng], 1)

        @block.tensor
        def _(e):
            run_engine('pe', e)

        @block.scalar
        def _(e):
            run_engine('act', e)

        @block.vector
        def _(e):
            run_engine('dve', e)

        @block.gpsimd
        def _(e):
            run_engine('pool', e)

        @block.sync
        def _(e):
            run_engine('sp', e)


def _seg(name):
    return (0, 512, 2, 256, 0) if name == 'P' else (512, 1024, 1, 1024, 1)


class _Stop(Exception):
    pass


def build_program(depth=DEPTH, stop=None):
    nc = bass.Bass("TRN2", target_bir_lowering=False)
    dt = nc.dram_tensor
    x_in = dt("x_in", [NTOK, D], F32, kind="ExternalInput").ap()
    ck = dt("ck", [DEPTH, 256, 128], F32, kind="ExternalInput").ap()
    cv = dt("cv", [DEPTH, 256, 128], F32, kind="ExternalInput").ap()
    vecpack = dt("vecpack", [V_ROWS, 128], F32, kind="ExternalInput").ap()
    sink_d = dt("sink", [DEPTH, 8], F32, kind="ExternalInput").ap()
    ropeC_d = dt("ropeC", [128, 1024], F32, kind="ExternalInput").ap()
    ropeS_d = dt("ropeS", [128, 1024], F32, kind="ExternalInput").ap()
    rperm_d = dt("rperm", [128, 128], F32, kind="ExternalInput").ap()
    ident_d = dt("ident", [128, 128], F32, kind="ExternalInput").ap()
    masks_d = dt("masks", [128, 1024], F32, kind="ExternalInput").ap()
    ecf_d = dt("ecf", [128, 256], F32, kind="ExternalInput").ap()
    w_mod = dt("w_mod", [DEPTH, D, 6 * D], F32, kind="ExternalInput").ap()
    w_in = dt("w_in", [DEPTH, D, 1280], F32, kind="ExternalInput").ap()
    w_pool = dt("w_pool", [DEPTH, 4, 128, 128], F32, kind="ExternalInput").ap()
    w_out = dt("w_out", [DEPTH, D, D], F32, kind="ExternalInput").ap()
    w_up = dt("w_up", [DEPTH, D, 5632], F32, kind="ExternalInput").ap()
    w_down = dt("w_down", [DEPTH, 2816, D], F32, kind="ExternalInput").ap()
    y_out = dt("y", [NTOK, D], F32, kind="ExternalOutput").ap()
    nk_out = dt("nk", [2, DEPTH, 256, 128], F32, kind="ExternalOutput").ap()
    nv_out = dt("nv", [2, DEPTH, 256, 128], F32, kind="ExternalOutput").ap()

    P = Prog()
    st = ExitStack()
    with st:
        def sb(n, s, d):
            return st.enter_context(nc.sbuf_tensor(n, s, d))

        xT = sb("xT", [128, 8, NTOK], F32)
        hT = sb("hT", [128, 8, NTOK], BF16)
        vecs = sb("vecs", [128, V_ROWS], F32)
        ropeC = sb("ropeCs", [128, 1024], F32)
        ropeS = sb("ropeSs", [128, 1024], F32)
        rperm = sb("rperms", [128, 128], F32)
        ident32 = sb("ident32", [128, 128], F32)
        identb = sb("identb", [128, 128], BF16)
        ones = sb("ones", [128, 128], BF16)
        maskb = sb("maskb", [128, 2, 512], BF16)
        ecf = sb("ecfs", [128, 256], F32)
        sink8 = sb("sink8", [1, 8], F32)
        sinkf8 = sb("sinkf8", [1, 8], F32)
        sinkh8 = sb("sinkh8", [1, 8], BF16)
        sinkl8 = sb("sinkl8", [1, 8], BF16)
        sinkhl = sb("sinkhl", [33, 1024], BF16)
        sinkL = [sb("sinkL%d" % i, [33, 128], BF16) for i in range(2)]
        scT = sb("scT", [128, 8, 2], BF16)
        modsT = [sb("modsT%d" % i, [128, 2, 48], F32) for i in range(2)]
        coef = sb("coef", [128, 2, 12, 2, 8], F32)
        ring = [sb("ring%d" % i, [128, 4096], BF16) for i in range(NS)]
        wpool = [sb("wpool%d" % i, [128, 4, 128], BF16) for i in range(2)]
        lnm = sb("lnm", [128, 512], F32)
        lnq = sb("lnq", [128, 512], F32)
        ybf = [sb("ybf%d" % i, [128, 512], BF16) for i in range(2)]
        ysq = [sb("ysq%d" % i, [128, 512], BF16) for i in range(2)]
        UW = 16128
        U = sb("U", [128, UW], F32)
        ps = st.enter_context(nc.psum_tensor("ps", [128, 8, 512], F32))

        cur = [0]

        def carve(nwords, dtype, shape=()):
            a = cur[0]
            cur[0] += nwords
            assert cur[0] <= UW, cur[0]
            v = U[:, a:a + nwords]
            if dtype == BF16:
                v = v.bitcast(BF16)
            if len(shape) == 2:
                return v.rearrange("p (a b) -> p a b", a=shape[0])
            if len(shape) == 3:
                return v.rearrange("p (a b c) -> p a b c", a=shape[0], b=shape[1])
            return v

        qT = carve(3072, BF16, (4, NTOK))
        kT = carve(768, BF16)
        Vb = carve(1536, BF16, (12, 256))
        poolT = carve(3072, BF16, (4, NTOK))
        kcT = carve(128, BF16)
        vcb = carve(256, BF16, (2, 256))
        p32 = carve(1040, F32)
        sA = carve(1040, F32)
        sB = carve(1040, F32)
        dTt = [carve(512, BF16) for _ in range(2)]
        q32 = [carve(512, F32) for _ in range(2)]
        t1 = carve(512, F32)
        PT = [carve(512, BF16, (2, 512)) for _ in range(3)]
        kvst = sA[:, 0:1024].rearrange("p (a b) -> p a b", a=4)
        cstage = sB[:, 0:512].rearrange("p (a b c) -> p a b c", a=2, b=2)
        rcp = [lnm, lnq]
        rcpn = ['lnm', 'lnq']
        cur[0] = 0
        mT = carve(14 * 768, BF16, (14, NTOK))
        ta = [carve(1024, F32) for _ in range(2)]
        tg = [carve(1024, F32) for _ in range(2)]
        cur[0] = 0
        xs = [carve(1024, F32) for _ in range(6)]

        MIX_RES = (['qT%d%s' % (c, s) for c in range(4) for s in 'PS'] + ['poolT%d%s' % (c, s) for c in range(4) for s in 'PS'] +
                   ['kTP', 'kTS', 'VbP', 'VbS', 'kcT', 'vcb', 'p32', 'sA', 'sB', 'dT0', 'dT1', 'q320', 'q321', 't1',
                    'PT0', 'PT1', 'PT2'])
        FFN_RES = ['mT%d%s' % (j, s) for j in range(14) for s in 'PS'] + ['ta0', 'ta1', 'tg0', 'tg1']
        IO_RES = ['xs%d' % i for i in range(6)]

        def MM(out, lhsT, rhs, start, stop, reads, writes):
            P.add('pe', 'matmul', (out, lhsT, rhs), dict(start=start, stop=stop), reads, writes)

        def TR(out, in_, ident, reads, writes):
            P.add('pe', 'transpose', (out, in_, ident), None, reads, writes)

        def ACT(out, in_, func, reads, writes, bias=None, scale=None):
            kw = {}
            if bias is not None:
                kw['bias'] = bias
            if scale is not None:
                kw['scale'] = scale
            P.add('act', 'activation', (out, in_, func), kw, reads, writes)

        def ACP(out, in_, reads, writes):
            P.add('act', 'copy', (out, in_), None, reads, writes)

        def VCP(out, in_, reads, writes):
            P.add('dve', 'tensor_copy', (out, in_), None, reads, writes)

        def TT(out, in0, in1, op, reads, writes):
            P.add('dve', 'tensor_tensor', (), dict(out=out, in0=in0, in1=in1, op=op), reads, writes)

        def TS(out, in0, s1, s2, op0, op1, reads, writes):
            kw = dict(out=out, in0=in0, scalar1=s1, scalar2=s2, op0=op0)
            if op1 is not None:
                kw['op1'] = op1
            P.add('dve', 'tensor_scalar', (), kw, reads, writes)

        def STT(out, in0, scalar, in1, op0, op1, reads, writes):
            P.add('dve', 'scalar_tensor_tensor', (), dict(out=out, in0=in0, scalar=scalar, in1=in1, op0=op0, op1=op1), reads, writes)

        def DMA(q, out, in_, reads, writes, key):
            P.add(q, 'dma_start', (), dict(out=out, in_=in_), reads, writes, dma_key=key)

        slot_ctr = [0]
        reserved = set()

        def _bank_recency(bk):
            r = 'pb%d' % bk
            m = -1
            for w in P.last_w.get(r, ()):
                m = max(m, w.idx)
            for k, v in P.readers.get(r, {}).items():
                if k == '_dma':
                    for x in v:
                        m = max(m, x.idx)
                else:
                    m = max(m, v.idx)
            return m

        def next_slot():
            best, bm = None, None
            for s in range(4):
                if s in reserved:
                    continue
                m = max(_bank_recency(2 * s), _bank_recency(2 * s + 1))
                if bm is None or m < bm:
                    best, bm = s, m
            return best

        def slot_res(s):
            return ['pb%d' % (2 * s), 'pb%d' % (2 * s + 1)]

        def slot_flat(s):
            return ps[:, 2 * s:2 * s + 2, :].rearrange("p a b -> p (a b)")

        ring_ctr = [0]

        def v3(t, a):
            return t[:].rearrange("p (a b) -> p a b", a=a)

        pinned = set()

        def ring_fill(dmas):
            while True:
                i = ring_ctr[0] % NS
                ring_ctr[0] += 1
                if i not in pinned:
                    break
            for dst_fn, src in dmas:
                DMA('pool', dst_fn(ring[i]), src, [], ['ring%d' % i], 'ring%d' % i)
            return i

        DMA('sp', ident32[:], ident_d, [], ['ident32'], 'c0')
        DMA('sp', rperm[:], rperm_d, [], ['rperm'], 'c0')
        DMA('sp', ecf[:], ecf_d, [], ['ecf'], 'c0')
        DMA('sp', ropeC[:], ropeC_d, [], ['ropeC'], 'c0')
        DMA('sp', ropeS[:], ropeS_d, [], ['ropeS'], 'c0')
        DMA('pool', identb[:], ident_d, [], ['identb'], 'c1')
        DMA('pool', maskb[:].rearrange("p a b -> p (a b)"), masks_d, [], ['maskb'], 'c1')
        P.add('dve', 'memset', (ones[:], 1.0), None, [], ['ones'])
        P.add('dve', 'memset', (sinkhl[:], 0.0), None, [], ['sinkhl'])
        for kvh in range(2):
            P.add('dve', 'memset', (sinkL[kvh][:], 0.0), None, [], ['sinkL'])
            P.add('dve', 'memset', (sinkL[kvh][:, (1 - kvh) * 64:(2 - kvh) * 64], 1.0), None, ['sinkL'], ['sinkL'])
        DMA('sp', xs[4].rearrange("p (t c) -> p t c", t=8), vecpack[0:1024, :].rearrange("(t p) c -> p t c", p=128), [], ['xs4'], 'xs4')
        DMA('sp', xs[5][:, 0:128], vecpack[1024:1152, :], [], ['xs5'], 'xs5')
        for t in range(9):
            src = xs[4][:, t * 128:(t + 1) * 128] if t < 8 else xs[5][:, 0:128]
            sn = 'xs4' if t < 8 else 'xs5'
            s = next_slot()
            TR(ps[:, 2 * s, 0:128], src, ident32[:], [sn, 'ident32'], slot_res(s))
            ACP(vecs[:, t * 128:(t + 1) * 128], ps[:, 2 * s, 0:128], slot_res(s), ['vecs'])
        for c in range(2):
            ACT(scT[:, :, c], vecs[:, V_COND + 8 * c:V_COND + 8 * c + 8], AF.Silu, ['vecs'], ['scT'])

        for ti_, tb in enumerate(list(range(4, 12)) + list(range(4))):
            b = ti_ % 4
            DMA('sp', xs[b], x_in[tb * 128:(tb + 1) * 128, :], [], ['xs%d' % b], 'xs%d' % b)
            s = next_slot()
            for c in range(8):
                TR(slot_flat(s)[:, c * 128:(c + 1) * 128], xs[b][:, c * 128:(c + 1) * 128], ident32[:], ['xs%d' % b, 'ident32'], slot_res(s))
            sg = 'P' if tb < 4 else 'S'
            ACP(xT[:, :, tb * 128:(tb + 1) * 128], slot_flat(s).rearrange("p (c t) -> p c t", c=8), slot_res(s), ['xT%d%s' % (c, sg) for c in range(8)])

        def cf(par, kind, cond, c):
            return coef[:, par, kind, cond, c:c + 1]

        K_A1, K_B1, K_G1P, K_G2P, K_A2, K_B2, K_OP1, K_OP2 = range(8)

        def mods_for_layer(l, pieces):
            par = l % 2
            for pc in pieces:
                i = ring_fill([(lambda r: v3(r, 8), w_mod[l, :, pc * 512:(pc + 1) * 512].rearrange("(k p) n -> p k n", p=128))])
                s = next_slot()
                wv = v3(ring[i], 8)
                for mm in range(4):
                    for kc in range(8):
                        MM(ps[:, 2 * s, mm * 2:mm * 2 + 2], wv[:, kc, mm * 128:(mm + 1) * 128], scT[:, kc, :], kc == 0, kc == 7, ['ring%d' % i, 'scT'], slot_res(s))
                for c in range(2):
                    bc = V_BMOD + l * 48 + pc * 4
                    TT(modsT[par][:, c, pc * 4:pc * 4 + 4], ps[:, 2 * s, c:8:2], vecs[:, bc:bc + 4], ALU.add, slot_res(s) + ['vecs'], ['mods%d_%d' % (par, pc)])

        def mres(par, j):
            return ['mods%d_%d' % (par, 2 * j), 'mods%d_%d' % (par, 2 * j + 1)]

        def coef_layer_start(l):
            par = l % 2
            for c in range(2):
                TS(coef[:, par, K_A1, c, :], modsT[par][:, c, 8:16], 1.0, None, ALU.add, None, mres(par, 1), ['cfA1_%d' % par])
                VCP(coef[:, par, K_B1, c, :], modsT[par][:, c, 0:8], mres(par, 0), ['cfB1_%d' % par])

        def coef_mid(l):
            par = l % 2
            g1 = vecs[:, V_LN1G + l * 8:V_LN1G + l * 8 + 8]
            b1 = vecs[:, V_LN1B + l * 8:V_LN1B + l * 8 + 8]
            for c in range(2):
                TS(coef[:, par, K_G1P, c, :], modsT[par][:, c, 16:24], 1.0 / ALPHA, None, ALU.mult, None, mres(par, 2), ['cfG1_%d' % par])
                TS(coef[:, par, K_G2P, c, :], modsT[par][:, c, 40:48], 1.0 / ALPHA, None, ALU.mult, None, mres(par, 5), ['cfG2_%d' % par])
                TS(coef[:, par, K_OP2, c, :], modsT[par][:, c, 32:40], 1.0, None, ALU.add, None, mres(par, 4), ['cfO2_%d' % par])
                TT(coef[:, par, K_A2, c, :], coef[:, par, K_OP2, c, :], g1, ALU.mult, ['cfO2_%d' % par, 'vecs'], ['cfA2_%d' % par])
                TT(coef[:, par, K_B2, c, :], coef[:, par, K_OP2, c, :], b1, ALU.mult, ['cfO2_%d' % par, 'vecs'], ['cfB2_%d' % par])
                TT(coef[:, par, K_B2, c, :], coef[:, par, K_B2, c, :], modsT[par][:, c, 24:32], ALU.add, ['cfB2_%d' % par] + mres(par, 3), ['cfB2_%d' % par])

        def coef_next(l):
            par = (l + 1) % 2
            g2 = vecs[:, V_LN2G + l * 8:V_LN2G + l * 8 + 8]
            b2 = vecs[:, V_LN2B + l * 8:V_LN2B + l * 8 + 8]
            for c in range(2):
                TS(coef[:, par, K_OP1, c, :], modsT[par][:, c, 8:16], 1.0, None, ALU.add, None, mres(par, 1), ['cfO1_%d' % par])
                TT(coef[:, par, K_A1, c, :], coef[:, par, K_OP1, c, :], g2, ALU.mult, ['cfO1_%d' % par, 'vecs'], ['cfA1_%d' % par])
                TT(coef[:, par, K_B1, c, :], coef[:, par, K_OP1, c, :], b2, ALU.mult, ['cfO1_%d' % par, 'vecs'], ['cfB1_%d' % par])
                TT(coef[:, par, K_B1, c, :], coef[:, par, K_B1, c, :], modsT[par][:, c, 0:8], ALU.add, ['cfB1_%d' % par] + mres(par, 0), ['cfB1_%d' % par])

        def mm_fm(s, sgn, lhs_list, rhs_fn, reads):
            c0, n, _, _, _ = _seg(sgn)
            nk = len(lhs_list)
            for k in range(nk):
                for h in range(n // 512):
                    MM(ps[:, 2 * s + h, :], lhs_list[k], rhs_fn(k, c0 + h * 512, c0 + (h + 1) * 512), k == 0, k == nk - 1, reads, slot_res(s))

        def layer_norm(l, which, last):
            par = l % 2
            for tb in range(3):
                sgn = 'P' if tb == 0 else 'S'
                cond = 0 if tb == 0 else 1
                c0 = tb * 512
                s = next_slot()
                xres = ['xT%d%s' % (c, sgn) for c in range(8)]
                for c in range(8):
                    b = c % 2
                    xv = xT[:, c, c0:c0 + 512]
                    ACP(ybf[b][:], xv, [xres[c]], ['ybf%d' % b])
                    ACT(ysq[b][:], xv, AF.Square, [xres[c]], ['ysq%d' % b])
                    MM(ps[:, 2 * s, :], ones[:], ybf[b][:], c == 0, c == 7, ['ones', 'ybf%d' % b], slot_res(s))
                    MM(ps[:, 2 * s + 1, :], ones[:], ysq[b][:], c == 0, c == 7, ['ones', 'ysq%d' % b], slot_res(s))
                ACT(lnm[:], ps[:, 2 * s, :], AF.Identity, slot_res(s), ['lnm'], scale=1.0 / D)
                TT(lnq[:], lnm[:], lnm[:], ALU.mult, ['lnm'], ['lnq'])
                STT(lnq[:], ps[:, 2 * s + 1, :], 1.0 / D, lnq[:], ALU.mult, ALU.subtract, slot_res(s) + ['lnq'], ['lnq'])
                TS(lnq[:], lnq[:], EPSP, None, ALU.add, None, ['lnq'], ['lnq'])
                ACT(lnq[:], lnq[:], AF.Ln, ['lnq'], ['lnq'])
                ACT(lnq[:], lnq[:], AF.Exp, ['lnq'], ['lnq'], scale=-0.5)
                if which == 1:
                    gcol, bcol = V_LN1G + l * 8, V_LN1B + l * 8
                    kA, kB, cpar = K_A2, K_B2, par
                    cres = ['cfA2_%d' % par, 'cfB2_%d' % par]
                else:
                    gcol, bcol = V_LN2G + l * 8, V_LN2B + l * 8
                    kA, kB, cpar = K_A1, K_B1, (l + 1) % 2
                    cres = ['cfA1_%d' % cpar, 'cfB1_%d' % cpar]
                for c in range(8):
                    xv = xT[:, c, c0:c0 + 512]
                    TT(xv, xv, lnm[:], ALU.subtract, [xres[c], 'lnm'], [xres[c]])
                    TT(xv, xv, lnq[:], ALU.mult, [xres[c], 'lnq'], [xres[c]])
                    if not last:
                        TS(hT[:, c, c0:c0 + 512], xv, cf(cpar, kA, cond, c), cf(cpar, kB, cond, c), ALU.mult, ALU.add, [xres[c]] + cres, ['hT%d%s' % (c, sgn)])
                    ACT(xv, xv, AF.Identity, [xres[c], 'vecs'], [xres[c]], bias=vecs[:, bcol + c:bcol + c + 1], scale=vecs[:, gcol + c:gcol + c + 1])

        def hrhs(k, a, b):
            return hT[:, k, a:b]

        def mixrhs(k, a, b):
            return hT[:, k, a:b] if k < 4 else poolT[:, k - 4, a:b]

        if stop == 'setup':
            depth = 0
        mods_for_layer(0, range(0, 4))
        coef_layer_start(0)
        for sgn in 'SP':
            c0, n, _, _, cond = _seg(sgn)
            for c in range(8):
                if c % 2 == 0:
                    ACT(hT[:, c, c0:c0 + n], xT[:, c, c0:c0 + n], AF.Identity, ['xT%d%s' % (c, sgn), 'cfA1_0', 'cfB1_0'], ['hT%d%s' % (c, sgn)],
                        bias=cf(0, K_B1, cond, c), scale=cf(0, K_A1, cond, c))
                else:
                    TS(hT[:, c, c0:c0 + n], xT[:, c, c0:c0 + n], cf(0, K_A1, cond, c), cf(0, K_B1, cond, c), ALU.mult, ALU.add,
                       ['xT%d%s' % (c, sgn), 'cfA1_0', 'cfB1_0'], ['hT%d%s' % (c, sgn)])
        P.transfer(IO_RES, MIX_RES)

        hres = {sgn: ['hT%d%s' % (c, sgn) for c in range(8)] for sgn in 'PS'}
        mixres = {sgn: ['hT%d%s' % (c, sgn) for c in range(4)] + ['poolT%d%s' % (c, sgn) for c in range(4)] for sgn in 'PS'}

        def run_pipeline(units, reverse=True):
            nst = max(len(u) for u in units)
            for step in range(len(units) + nst - 1):
                ks = range(nst - 1, -1, -1) if reverse else range(nst)
                for k in ks:
                    n = step - k
                    if 0 <= n < len(units) and k < len(units[n]) and units[n][k] is not None:
                        units[n][k]()

        pending = []

        def emit_pending(k):
            for _ in range(min(k, len(pending))):
                pending.pop(0)()

        def ln_tail_pieces(l, which, last, tb, sb0, sb1):
            par = l % 2
            sgn = 'P' if tb == 0 else 'S'
            cond = 0 if tb == 0 else 1
            c0 = tb * 512
            xres = ['xT%d%s' % (c, sgn) for c in range(8)]
            if which == 1:
                gcol, bcol = V_LN1G + l * 8, V_LN1B + l * 8
                kA, kB, cpar = K_A2, K_B2, par
                cres = ['cfA2_%d' % par, 'cfB2_%d' % par]
            else:
                gcol, bcol = V_LN2G + l * 8, V_LN2B + l * 8
                kA, kB, cpar = K_A1, K_B1, (l + 1) % 2
                cres = ['cfA1_%d' % cpar, 'cfB1_%d' % cpar]

            def head_a():
                ACT(lnm[:], ps[:, sb0, :], AF.Identity, ['pb%d' % sb0], ['lnm'], scale=1.0 / D)

            def head_b():
                TT(lnq[:], lnm[:], lnm[:], ALU.mult, ['lnm'], ['lnq'])
                STT(lnq[:], ps[:, sb1, :], 1.0 / D, lnq[:], ALU.mult, ALU.subtract, ['pb%d' % sb1, 'lnq'], ['lnq'])
                TS(lnq[:], lnq[:], EPSP, None, ALU.add, None, ['lnq'], ['lnq'])

            def head_c():
                ACT(lnq[:], lnq[:], AF.Ln, ['lnq'], ['lnq'])
                ACT(lnq[:], lnq[:], AF.Exp, ['lnq'], ['lnq'], scale=-0.5)

            def chunk(c):
                def f():
                    xv = xT[:, c, c0:c0 + 512]
                    TT(xv, xv, lnm[:], ALU.subtract, [xres[c], 'lnm'], [xres[c]])
                    TT(xv, xv, lnq[:], ALU.mult, [xres[c], 'lnq'], [xres[c]])
                    if not last:
                        if c % 2 == 0:
                            ACT(hT[:, c, c0:c0 + 512], xv, AF.Identity, [xres[c]] + cres, ['hT%d%s' % (c, sgn)], bias=cf(cpar, kB, cond, c), scale=cf(cpar, kA, cond, c))
                        else:
                            TS(hT[:, c, c0:c0 + 512], xv, cf(cpar, kA, cond, c), cf(cpar, kB, cond, c), ALU.mult, ALU.add, [xres[c]] + cres, ['hT%d%s' % (c, sgn)])
                    ACT(xv, xv, AF.Identity, [xres[c], 'vecs'], [xres[c]], bias=vecs[:, bcol + c:bcol + c + 1], scale=vecs[:, gcol + c:gcol + c + 1])
                return f
            return [head_a, head_b, head_c] + [chunk(c) for c in range(8)]

        def store_block(tb):
            sgn = 'P' if tb == 0 else 'S'
            stg = [(ta[0], 'ta0'), (ta[1], 'ta1'), (tg[0], 'tg0'), (tg[1], 'tg1')]
            for tbk in range(4 * tb, 4 * tb + 4):
                b = tbk % 4
                buf, bn = stg[b]
                s = 2 + tbk % 2
                for c in range(8):
                    TR(slot_flat(s)[:, c * 128:(c + 1) * 128], xT[:, c, tbk * 128:(tbk + 1) * 128], ident32[:], ['xT%d%s' % (c, sgn), 'ident32'], slot_res(s))
                if tbk % 2 == 0:
                    ACP(buf, slot_flat(s), slot_res(s), [bn])
                else:
                    VCP(buf, slot_flat(s), slot_res(s), [bn])
                DMA('sp', y_out[tbk * 128:(tbk + 1) * 128, :], buf, [bn], ['y_out%d' % b], 'yo%d' % b)

        def filler(n):
            if n <= 0:
                return
            s = next_slot()
            for i in range(n):
                MM(ps[:, 2 * s, :], ones[:], maskb[:, 0, :], True, True, ['ones', 'maskb'], ['pb%d' % (2 * s)])

        def proj_ln(l, which, last, lhs_fn, nk, rhs_fn, rres_fn, wres, gkind, gres, nfill=0):
            par = l % 2
            reserved.update([0, 1, 2, 3])
            bctr = [0]
            prev_tb = None
            lo_rec = max(_bank_recency(bk_) for bk_ in range(0, 4))
            hi_rec = max(_bank_recency(bk_) for bk_ in range(4, 8))
            ubase, sbase = (4, 0) if hi_rec <= lo_rec or last else (0, 4)
            for ti, tb in enumerate((1, 2, 0)):
                sgn = 'P' if tb == 0 else 'S'
                cond = 0 if tb == 0 else 1
                c0 = tb * 512
                sb0, sb1 = (sbase, sbase + 1) if ti % 2 == 0 else (sbase + 2, sbase + 3)
                units = []
                for dc in range(8):
                    def s1(dc=dc, sgn=sgn, cond=cond, c0=c0):
                        bk = ubase + bctr[0] % 4
                        bctr[0] += 1
                        lhs = lhs_fn(dc)
                        for k in range(nk):
                            MM(ps[:, bk, :], lhs[k], rhs_fn(k, c0, c0 + 512), k == 0, k == nk - 1, wres + rres_fn(sgn), ['pb%d' % bk])
                        xv = xT[:, dc, c0:c0 + 512]
                        STT(xv, ps[:, bk, :], cf(par, gkind, cond, dc), xv, ALU.mult, ALU.add, ['pb%d' % bk, gres, 'xT%d%s' % (dc, sgn)], ['xT%d%s' % (dc, sgn)])
                        b = dc % 2
                        ACP(ybf[b][:], xv, ['xT%d%s' % (dc, sgn)], ['ybf%d' % b])
                        ACT(ysq[b][:], xv, AF.Square, ['xT%d%s' % (dc, sgn)], ['ysq%d' % b])
                        emit_pending((1, 1, 1, 2, 2, 2, 1, 1)[dc])

                    def s3(dc=dc, sb0=sb0, sb1=sb1):
                        b = dc % 2
                        MM(ps[:, sb0, :], ones[:], ybf[b][:], dc == 0, dc == 7, ['ones', 'ybf%d' % b], ['pb%d' % sb0])
                        MM(ps[:, sb1, :], ones[:], ysq[b][:], dc == 0, dc == 7, ['ones', 'ysq%d' % b], ['pb%d' % sb1])
                    units.append([s1, None, s3])
                run_pipeline(units)
                emit_pending(len(pending))
                if last and prev_tb is not None:
                    store_block(prev_tb)
                prev_tb = tb
                pending.extend(ln_tail_pieces(l, which, last, tb, sb0, sb1))
            reserved.difference_update([0, 1, 2, 3])
            filler(nfill)
            emit_pending(len(pending))
            if last:
                store_block(0)

        for l in range(depth):
          try:
            par = l % 2
            lastl = (l == depth - 1)
            DMA('sp', cstage[:, 0, :, :], ck[l].rearrange("(b p) d -> p b d", p=128), [], ['sB'], 'cst')
            DMA('sp', cstage[:, 1, :, :], cv[l].rearrange("(b p) d -> p b d", p=128), [], ['sB'], 'cst')
            DMA('sp', sink8[:], sink_d[l:l + 1, :], [], ['sink8'], 'snk')
            DMA('pool', wpool[par][:], w_pool[l].rearrange("g c d -> c g d"), [], ['wpool%d' % par], 'wpool%d' % par)
            ACT(sinkf8[:], sink8[:], AF.Exp, ['sink8'], ['sinkf8'])
            VCP(sinkh8[:], sinkf8[:], ['sinkf8'], ['sinkh8'])
            TT(sinkf8[:], sinkf8[:], sinkh8[:], ALU.subtract, ['sinkf8', 'sinkh8'], ['sinkf8'])
            VCP(sinkl8[:], sinkf8[:], ['sinkf8'], ['sinkl8'])
            VCP(sinkhl[0:1, :].rearrange("p (a b) -> p a b", a=8), sinkh8[0:1, :].unsqueeze(2).to_broadcast([1, 8, 128]), ['sinkh8', 'sinkhl'], ['sinkhl'])
            VCP(sinkhl[32:33, :].rearrange("p (a b) -> p a b", a=8), sinkl8[0:1, :].unsqueeze(2).to_broadcast([1, 8, 128]), ['sinkl8', 'sinkhl'], ['sinkhl'])
            P.add('dve', 'memset', (Vb[:, :, 64:192], 1.0), None, [], ['VbP', 'VbS'])
            P.add('dve', 'memset', (vcb[:, :, 64:192], 1.0), None, [], ['vcb'])
            s = next_slot()
            for b in range(2):
                TR(ps[:, 2 * s, b * 128:(b + 1) * 128], cstage[:, 0, b, :], ident32[:], ['sB', 'ident32'], slot_res(s))
            ACP(kcT, ps[:, 2 * s, 0:256], slot_res(s), ['kcT'])
            VCP(vcb[:, :, 0:64], cstage[:, 1, :, 0:64], ['sB', 'vcb'], ['vcb'])
            VCP(vcb[:, :, 192:256], cstage[:, 1, :, 64:128], ['sB', 'vcb'], ['vcb'])

            iq = ring_fill([
                ((lambda r, c=c, hh=hh: v3(r, 8)[:, :, c * 128 + hh * 64:c * 128 + hh * 64 + 64]),
                 w_in[l, :, (hh * 4 + c) * 64:(hh * 4 + c) * 64 + 64].rearrange("(k p) n -> p k n", p=128))
                for c in range(4) for hh in range(2)])
            ikv = ring_fill([(lambda r: v3(r, 8)[:, :, 0:256], w_in[l, :, 512:768].rearrange("(k p) n -> p k n", p=128))])
            ipp = ring_fill([(lambda r: v3(r, 8), w_in[l, :, 768:1280].rearrange("(k p) n -> p k n", p=128))])
            pinned.update([iq, ikv, ipp])
            wq = v3(ring[iq], 8)
            wkv = v3(ring[ikv], 8)
            wpp = v3(ring[ipp], 8)

            def proj_stage(box, sgn, lhs, rres):
                def f():
                    box['s'] = next_slot()
                    mm_fm(box['s'], sgn, lhs, hrhs, rres + hres[sgn])
                return f

            def rope_stage_a(box):
                def f():
                    s = box['s']
                    for h in range(2):
                        ACP(q32[h], ps[:, 2 * s + h, :], ['pb%d' % (2 * s + h)], ['q32%d' % h])
                return f

            def rope_stage_b(dst, dres):
                def f():
                    s2 = next_slot()
                    for h in range(2):
                        MM(ps[:, 2 * s2 + h, :], rperm[:], q32[h], True, True, ['rperm', 'q32%d' % h], ['pb%d' % (2 * s2 + h)])
                    for h in range(2):
                        cs = slice(h * 512, (h + 1) * 512)
                        TT(t1, q32[h], ropeC[:, cs], ALU.mult, ['q32%d' % h, 'ropeC'], ['t1'])
                        TT(q32[h], ps[:, 2 * s2 + h, :], ropeS[:, cs], ALU.mult, ['pb%d' % (2 * s2 + h), 'ropeS', 'q32%d' % h], ['q32%d' % h])
                        TT(dst[:, cs], t1, q32[h], ALU.add, ['t1', 'q32%d' % h], dres)
                return f

            def copy_stage(box, dst, dres):
                def f():
                    s = box['s']
                    ACP(dst, ps[:, 2 * s, :], slot_res(s), dres)
                return f

            def pool_stage2(box, g, sgn, di):
                c0, n, nseq, L, _ = _seg(sgn)
                w = (2, 4, 8, 16)[g]
                hw_ = w // 2
                LP = L + 16

                def pv(buf, lo, hi):
                    return buf[:, 0:nseq * LP].rearrange("p (a b) -> p a b", a=nseq)[:, :, lo:hi]

                def f():
                    s = box['s']
                    P.add('dve', 'memset', (pv(p32, 0, 8), 0.0), None, [], ['p32'])
                    P.add('dve', 'memset', (pv(p32, LP - 8, LP), 0.0), None, ['p32'], ['p32'])
                    P.add('act', 'copy', (pv(p32, 8, 8 + L), slot_flat(s)[:, 0:n].rearrange("p (a b) -> p a b", a=nseq)), None, slot_res(s) + ['p32'], ['p32'])
                    TT(pv(sA, 1, LP), pv(p32, 1, LP), pv(p32, 0, LP - 1), ALU.add, ['p32', 'sA'], ['sA'])
                    src, dst, sn, dn = sA, sB, 'sA', 'sB'
                    lo, hi, sh = 1, LP, 1
                    for _ in range(g):
                        TT(pv(dst, lo + sh, hi - sh), pv(src, lo + 2 * sh, hi), pv(src, lo, hi - 2 * sh), ALU.add, [sn, dn], [dn])
                        lo, hi = lo + sh, hi - sh
                        src, dst, sn, dn = dst, src, dn, sn
                        sh *= 2
                    tot, tn = src, sn
                    eb = (g * 2 + (0 if sgn == 'P' else 1)) * 32
                    ev = ecf[:, eb:eb + nseq * 8].rearrange("p (a b) -> p a b", a=nseq)
                    TT(pv(tot, 8, 8 + hw_), pv(tot, 8, 8 + hw_), ev[:, :, 0:hw_], ALU.mult, [tn, 'ecf'], [tn])
                    if hw_ > 1:
                        ev2 = ecf[:, eb + 16:eb + 16 + nseq * 8].rearrange("p (a b) -> p a b", a=nseq)
                        TT(pv(tot, 8 + L - hw_ + 1, 8 + L), pv(tot, 8 + L - hw_ + 1, 8 + L), ev2[:, :, 0:hw_ - 1], ALU.mult, [tn, 'ecf'], [tn])
                    STT(dTt[di][:, 0:n].rearrange("p (a b) -> p a b", a=nseq), pv(tot, 8, 8 + L), 1.0 / w, pv(p32, 8, 8 + L), ALU.mult, ALU.subtract, [tn, 'p32'], ['dT%d' % di])
                return f

            def pool_stage3(g, sgn, di):
                c0, n, nseq, L, _ = _seg(sgn)

                def f():
                    s = next_slot()
                    for h in range(n // 512):
                        MM(ps[:, 2 * s + h, :], wpool[par][:, g, :], dTt[di][:, h * 512:(h + 1) * 512], True, True, ['wpool%d' % par, 'dT%d' % di], ['pb%d' % (2 * s + h)])
                    psc = V_PSC + l * 4 + g
                    ACT(poolT[:, g, c0:c0 + n], slot_flat(s)[:, 0:n], AF.Identity, slot_res(s) + ['vecs'], ['poolT%d%s' % (g, sgn)], scale=vecs[:, psc:psc + 1])
                return f

            def kvp_units():
                box = {}

                def s1():
                    s = next_slot()
                    box['s'] = s
                    pv4 = slot_flat(s).rearrange("p (a b) -> p a b", a=4)
                    for tbk in range(4):
                        for k in range(8):
                            MM(pv4[:, tbk, :], hT[:, k, tbk * 128:(tbk + 1) * 128], wkv[:, k, 0:256], k == 0, k == 7, ['ring%d' % ikv] + hres['P'], slot_res(s))

                def s2():
                    s = box['s']
                    pv4 = slot_flat(s).rearrange("p (a b) -> p a b", a=4)
                    ACP(kvst, pv4, slot_res(s), ['sA'])
                    ACP(Vb[:, 0:4, 0:64], pv4[:, :, 128:192], slot_res(s) + ['VbP'], ['VbP'])
                    ACP(Vb[:, 0:4, 192:256], pv4[:, :, 192:256], slot_res(s) + ['VbP'], ['VbP'])
                    for b in range(2):
                        DMA('sp', nk_out[b, l].rearrange("(k p) d -> p k d", p=128), kvst[:, 2 * b:2 * b + 2, 0:128], ['sA'], ['nk_out'], 'kvo')
                        DMA('sp', nv_out[b, l].rearrange("(k p) d -> p k d", p=128), kvst[:, 2 * b:2 * b + 2, 128:256], ['sA'], ['nv_out'], 'kvo')
                return [s1, s2]

            def vs_units():
                box = {}

                def s1():
                    s = next_slot()
                    box['s'] = s
                    pv8 = slot_flat(s).rearrange("p (a b) -> p a b", a=8)
                    for tbk in range(8):
                        for k in range(8):
                            MM(pv8[:, tbk, :], hT[:, k, 512 + tbk * 128:512 + (tbk + 1) * 128], wkv[:, k, 128:256], k == 0, k == 7, ['ring%d' % ikv] + hres['S'], slot_res(s))

                def s2():
                    s = box['s']
                    pv8 = slot_flat(s).rearrange("p (a b) -> p a b", a=8)
                    ACP(Vb[:, 4:12, 0:64], pv8[:, :, 0:64], slot_res(s) + ['VbS'], ['VbS'])
                    ACP(Vb[:, 4:12, 192:256], pv8[:, :, 64:128], slot_res(s) + ['VbS'], ['VbS'])
                return [s1, s2]

            klhs = [wkv[:, k, 0:128] for k in range(8)]

            def qu(c, sgn):
                qlhs = [wq[:, k, c * 128:(c + 1) * 128] for k in range(8)]
                box = {}
                if sgn == 'P':
                    return [proj_stage(box, 'P', qlhs, ['ring%d' % iq]), copy_stage(box, qT[:, c, 0:512], ['qT%dP' % c])]
                return [proj_stage(box, 'S', qlhs, ['ring%d' % iq]), rope_stage_a(box), None, rope_stage_b(qT[:, c, 512:1536], ['qT%dS' % c])]

            def pu(c, sgn):
                plhs = [wpp[:, k, c * 128:(c + 1) * 128] for k in range(8)]
                box = {}
                di = 0 if sgn == 'P' else 1
                return [proj_stage(box, sgn, plhs, ['ring%d' % ipp]), pool_stage2(box, c, sgn, di), None, pool_stage3(c, sgn, di)]

            def ku(sgn):
                box = {}
                if sgn == 'P':
                    return [proj_stage(box, 'P', klhs, ['ring%d' % ikv]), copy_stage(box, kT[:, 0:512], ['kTP'])]
                return [proj_stage(box, 'S', klhs, ['ring%d' % ikv]), rope_stage_a(box), None, rope_stage_b(kT[:, 512:1536], ['kTS'])]

            def fill_stage(u, n):
                u[2] = (lambda: filler(n))
                return u

            unitsA = [ku('S'), pu(0, 'S'), qu(0, 'S'), pu(1, 'S'), qu(1, 'S'), pu(2, 'S'), qu(2, 'S'), pu(3, 'S'), qu(3, 'S'), vs_units(),
                      ku('P'), qu(0, 'P'), kvp_units(), qu(1, 'P'), pu(0, 'P'), qu(2, 'P'), pu(1, 'P'), qu(3, 'P'),
                      fill_stage(pu(2, 'P'), FILL_AB), [lambda: None], fill_stage(pu(3, 'P'), FILL_AB)]
            if l == 0:
                for i_, pc in enumerate(range(4, 12)):
                    unitsA.insert(3 + 2 * i_, [lambda pc=pc: mods_for_layer(0, [pc])])
            run_pipeline(unitsA)
            pinned.difference_update([iq, ikv, ipp])
            if stop == 'A':
                raise _Stop()

            coef_mid(l)
            aunits = []
            acc_ctr = [0]

            def add_qblock(q0, kblocks, qres, ores):
                for kvh in range(2):
                    acc = acc_ctr[0] % 4
                    acc_ctr[0] += 1
                    groups = [kblocks[g0:g0 + 2] for g0 in range(0, len(kblocks), 2)]
                    for gi, grp in enumerate(groups):
                        aunits.append(dict(q0=q0, kvh=kvh, grp=grp, first=(gi == 0), last=(gi == len(groups) - 1), acc=acc, qres=qres, ores=ores))

            def att_S(u, n):
                def f():
                    sl = 2 + n % 2
                    h0 = u['kvh'] * 64
                    q0 = u['q0']
                    for j, (kap, vap, mi, kres, vres) in enumerate(u['grp']):
                        bank = 2 * sl + j
                        if mi is not None:
                            MM(ps[:, bank, :], identb[:], maskb[:, mi, :], True, False, ['identb', 'maskb'], ['pb%d' % bank])
                        MM(ps[:, bank, :], kap[h0:h0 + 64, :], qT[h0:h0 + 64, :, q0:q0 + 128], mi is None, True, [kres] + u['qres'], ['pb%d' % bank])
                return f

            def att_PV(u, n):
                def f():
                    sl = 2 + n % 2
                    pti = n % 3
                    kvh = u['kvh']
                    h0 = kvh * 64
                    s0 = 64 - h0
                    q0 = u['q0']
                    accb = u['acc']
                    grp = u['grp']
                    ng = len(grp)
                    ACT(PT[pti][:, 0:ng, :], ps[:, 2 * sl:2 * sl + ng, :], AF.Exp, ['pb%d' % (2 * sl + j) for j in range(ng)], ['PT%d' % pti], scale=SCALE)
                    if u['first']:
                        MM(ps[:, accb, :], sinkL[kvh][:], sinkhl[:, kvh * 512:(kvh + 1) * 512], True, False, ['sinkL', 'sinkhl'], ['pb%d' % accb])
                    for j, (kap, vap, mi, kres, vres) in enumerate(grp):
                        MM(ps[:, accb, :], vap[:, kvh * 128:(kvh + 1) * 128], PT[pti][:, j, :], False, u['last'] and j == ng - 1, [vres, 'PT%d' % pti], ['pb%d' % accb])
                return f

            def att_NORM(u, n):
                if not u['last']:
                    return None

                def f():
                    kvh = u['kvh']
                    h0 = kvh * 64
                    s0 = 64 - h0
                    q0 = u['q0']
                    accb = u['acc']
                    if kvh == 0:
                        P.add('dve', 'reciprocal', (rcp[kvh][s0:s0 + 64, :], ps[s0:s0 + 64, accb, :]), None, ['pb%d' % accb], [rcpn[kvh]])
                    else:
                        ACT(rcp[kvh][s0:s0 + 64, :], ps[s0:s0 + 64, accb, :], AF.Ln, ['pb%d' % accb], [rcpn[kvh]])
                        ACT(rcp[kvh][s0:s0 + 64, :], rcp[kvh][s0:s0 + 64, :], AF.Exp, [rcpn[kvh]], [rcpn[kvh]], scale=-1.0)
                    TT(hT[h0:h0 + 64, 0:4, q0:q0 + 128], ps[h0:h0 + 64, accb, :].rearrange("p (a b) -> p a b", a=4),
                       rcp[kvh][s0:s0 + 64, :].rearrange("p (a b) -> p a b", a=4), ALU.mult, ['pb%d' % accb, rcpn[kvh]], u['ores'])
                return f

            for i in range(8):
                kbl = [(kcT[:, kb * 128:(kb + 1) * 128], vcb[:, kb, :], None, 'kcT', 'vcb') for kb in range(2)]
                for j in (i - 1, i, i + 1):
                    if j < 0 or j > 7:
                        continue
                    mi = None if j == i else (0 if j == i - 1 else 1)
                    kbl.append((kT[:, 512 + j * 128:512 + (j + 1) * 128], Vb[:, 4 + j, :], mi, 'kTS', 'VbS'))
                add_qblock(512 + i * 128, kbl, ['qT%dS' % c for c in range(4)], ['hT%dS' % c for c in range(4)])
            for b in range(2):
                kbl = [(kT[:, b * 256 + kb * 128:b * 256 + (kb + 1) * 128], Vb[:, 2 * b + kb, :], None, 'kTP', 'VbP') for kb in range(2)]
                for qb in range(2):
                    add_qblock(b * 256 + qb * 128, kbl, ['qT%dP' % c for c in range(4)], ['hT%dP' % c for c in range(4)])
            def att_warm():
                for i in range(FILL_ATT):
                    MM(ps[:, 3, :], ones[:], maskb[:, 0, :], True, True, ['ones', 'maskb'], ['pb3'])

            att_stages = [[att_S(u, n), att_PV(u, n), None, att_NORM(u, n)] for n, u in enumerate(aunits)]
            s0_ = att_stages[0][0]
            att_stages[0][0] = (lambda: (s0_(), att_warm()))
            run_pipeline(att_stages, reverse=False)
            if stop == 'B':
                raise _Stop()

            filler(FILL_BC)

            ios = []
            for half in range(2):
                c_a, c_b = half * 512, (half + 1) * 512
                ios.append(ring_fill([
                    (lambda r: v3(r, 8)[0:64, 0:4, :], w_out[l, 0:256, c_a:c_b].rearrange("(c p) n -> p c n", p=64)),
                    (lambda r: v3(r, 8)[64:128, 0:4, :], w_out[l, 256:512, c_a:c_b].rearrange("(c p) n -> p c n", p=64)),
                    (lambda r: v3(r, 8)[:, 4:8, :], w_out[l, 512:1024, c_a:c_b].rearrange("(c p) n -> p c n", p=128)),
                ]))
            wos = [v3(ring[i], 8) for i in ios]
            proj_ln(l, 1, False, lambda dc: [wos[dc // 4][:, k, (dc % 4) * 128:(dc % 4 + 1) * 128] for k in range(8)], 8,
                    mixrhs, lambda sgn: mixres[sgn], ['ring%d' % i for i in ios], K_G1P, 'cfG1_%d' % par)
            if stop == 'C':
                raise _Stop()
            P.transfer(MIX_RES, FFN_RES)

            nmod_done = [0]

            def next_mod_piece():
                if (not lastl) and nmod_done[0] < 12:
                    mods_for_layer(l + 1, [nmod_done[0]])
                    nmod_done[0] += 1

            def cw(tap, ch):
                cc = V_CW + (l * 3 + tap) * 44 + ch
                return vecs[:, cc:cc + 1]

            def cb(ch):
                cc = V_CB + l * 44 + ch
                return vecs[:, cc:cc + 1]

            tb_ctr = [0]
            j0 = 0
            for hf in range(2):
                npairs = HALF_PAIRS[hf]
                for jp in range(npairs // 2):
                    jA = j0 + 2 * jp
                    iu = ring_fill([
                        (lambda r: v3(r, 8)[:, :, 0:256], w_up[l, :, jA * 128:(jA + 2) * 128].rearrange("(k p) n -> p k n", p=128)),
                        (lambda r: v3(r, 8)[:, :, 256:512], w_up[l, :, 2816 + jA * 128:2816 + (jA + 2) * 128].rearrange("(k p) n -> p k n", p=128)),
                    ])
                    wu = v3(ring[iu], 8)
                    first_slot = (hf == 0 and jp == 0)
                    last_slot = (jp == npairs // 2 - 1)
                    order = [(0, 'S'), (1, 'S'), (0, 'P'), (1, 'P')] if (first_slot or last_slot) else [(0, 'P'), (0, 'S'), (1, 'P'), (1, 'S')]
                    for oi, (jj, sgn) in enumerate(order):
                        j = jA + jj
                        jm = j - j0
                        la = [wu[:, k, jj * 128:(jj + 1) * 128] for k in range(8)]
                        lg = [wu[:, k, 256 + jj * 128:256 + (jj + 1) * 128] for k in range(8)]
                        for sgn in (sgn,):
                            c0, n, nseq, L, _ = _seg(sgn)
                            tbi = tb_ctr[0] % 2
                            tb_ctr[0] += 1
                            rr = ['ring%d' % iu] + hres[sgn]
                            if sgn == 'P':
                                s = next_slot()
                                for k in range(8):
                                    MM(ps[:, 2 * s, :], la[k], hT[:, k, 0:512], k == 0, k == 7, rr, ['pb%d' % (2 * s)])
                                for k in range(8):
                                    MM(ps[:, 2 * s + 1, :], lg[k], hT[:, k, 0:512], k == 0, k == 7, rr, ['pb%d' % (2 * s + 1)])
                                srcs = [(ps[:, 2 * s, :], ['pb%d' % (2 * s)], ta[tbi], 'ta%d' % tbi, j), (ps[:, 2 * s + 1, :], ['pb%d' % (2 * s + 1)], tg[tbi], 'tg%d' % tbi, 22 + j)]
                            else:
                                s = next_slot()
                                mm_fm(s, 'S', la, hrhs, rr)
                                s2 = next_slot()
                                mm_fm(s2, 'S', lg, hrhs, rr)
                                srcs = [(slot_flat(s), slot_res(s), ta[tbi], 'ta%d' % tbi, j), (slot_flat(s2), slot_res(s2), tg[tbi], 'tg%d' % tbi, 22 + j)]
                            for (src, sres, tbuf, tname, ch) in srcs:
                                sv = src.rearrange("p (a b) -> p a b", a=nseq)
                                tv = tbuf[:, 0:n].rearrange("p (a b) -> p a b", a=nseq)
                                ACT(tv, sv, AF.Identity, sres + ['vecs'], [tname], bias=cb(ch), scale=cw(1, ch))
                                STT(tv[:, :, 1:L], sv[:, :, 0:L - 1], cw(0, ch), tv[:, :, 1:L], ALU.mult, ALU.add, sres + ['vecs', tname], [tname])
                                STT(tv[:, :, 0:L - 1], sv[:, :, 1:L], cw(2, ch), tv[:, :, 0:L - 1], ALU.mult, ALU.add, sres + ['vecs', tname], [tname])
                            ACT(tg[tbi][:, 0:n], tg[tbi][:, 0:n], AF.Silu, ['tg%d' % tbi], ['tg%d' % tbi])
                            TT(mT[:, jm, c0:c0 + n], ta[tbi][:, 0:n], tg[tbi][:, 0:n], ALU.mult, ['ta%d' % tbi, 'tg%d' % tbi], ['mT%d%s' % (jm, sgn)])
                    next_mod_piece()
                if hf == 0:
                    for dcp in range(4):
                        idn = ring_fill([(lambda r: r[:, 0:npairs * 256].rearrange("p (a b) -> p a b", a=npairs),
                                          w_down[l, j0 * 128:(j0 + npairs) * 128, dcp * 256:(dcp + 1) * 256].rearrange("(k p) n -> p k n", p=128))])
                        wd = ring[idn][:, 0:npairs * 256].rearrange("p (a b) -> p a b", a=npairs)
                        for d2, sgn in ((0, 'S'), (1, 'S'), (0, 'P'), (1, 'P')):
                            dc = dcp * 2 + d2
                            lhs = [wd[:, k, d2 * 128:(d2 + 1) * 128] for k in range(npairs)]
                            for sgn in (sgn,):
                                c0, n, _, _, cond = _seg(sgn)
                                s = next_slot()
                                mm_fm(s, sgn, lhs, lambda k, a, b: mT[:, k, a:b], ['ring%d' % idn] + ['mT%d%s' % (k, sgn) for k in range(npairs)])
                                xv = xT[:, dc, c0:c0 + n]
                                STT(xv, slot_flat(s)[:, 0:n], cf(par, K_G2P, cond, dc), xv, ALU.mult, ALU.add, slot_res(s) + ['cfG2_%d' % par, 'xT%d%s' % (dc, sgn)], ['xT%d%s' % (dc, sgn)])
                    next_mod_piece()
                else:
                    while (not lastl) and nmod_done[0] < 12:
                        next_mod_piece()
                    if not lastl:
                        coef_next(l)
                    idns = []
                    for dq in range(2):
                        idns.append(ring_fill([(lambda r: r[:, 0:npairs * 512].rearrange("p (a b) -> p a b", a=npairs),
                                                w_down[l, j0 * 128:(j0 + npairs) * 128, dq * 512:(dq + 1) * 512].rearrange("(k p) n -> p k n", p=128))]))
                    wds = [ring[i][:, 0:npairs * 512].rearrange("p (a b) -> p a b", a=npairs) for i in idns]
                    proj_ln(l, 2, lastl, lambda dc: [wds[dc // 4][:, k, (dc % 4) * 128:(dc % 4 + 1) * 128] for k in range(npairs)], npairs,
                            lambda k, a, b: mT[:, k, a:b], lambda sgn: ['mT%d%s' % (k, sgn) for k in range(npairs)], ['ring%d' % i for i in idns], K_G2P, 'cfG2_%d' % par, nfill=0)
                j0 += npairs
            P.transfer(FFN_RES, IO_RES if lastl else MIX_RES)
          except _Stop:
            P.transfer(MIX_RES + FFN_RES, IO_RES)
            break

        for tb in (range(12) if (stop is not None or depth == 0) else []):
            b = tb % 2
            sgn = 'P' if tb < 4 else 'S'
            s = next_slot()
            for c in range(8):
                TR(slot_flat(s)[:, c * 128:(c + 1) * 128], xT[:, c, tb * 128:(tb + 1) * 128], ident32[:], ['xT%d%s' % (c, sgn), 'ident32'], slot_res(s))
            if tb % 2 == 0:
                ACP(xs[b], slot_flat(s), slot_res(s), ['xs%d' % b])
            else:
                VCP(xs[b], slot_flat(s), slot_res(s), ['xs%d' % b])
            DMA('sp', y_out[tb * 128:(tb + 1) * 128, :], xs[b], ['xs%d' % b], ['y_out%d' % b], 'yo%d' % b)
        P.add('sp', None, (), None, ['y_out0', 'y_out1', 'y_out2', 'y_out3', 'nk_out', 'nv_out'], [])
        P.emit(nc, st)
        nc._prog_stats = (len(P.ops), P.n_waits)
    return nc


def _constants():
    half = 32
    inv = (10000.0 ** (-np.arange(0, half, 2, dtype=np.float32) / half)).astype(np.float32)
    t = np.arange(1024)
    row = (t // 64).astype(np.float32)
    col = (t % 64).astype(np.float32)
    C = np.zeros((128, 1024), np.float32)
    S = np.zeros((128, 1024), np.float32)
    for p in range(128):
        d = p % 64
        pos = row if d < 32 else col
        f = inv[(d % 32) % 16]
        ang = (pos * f).astype(np.float32)
        C[p] = np.cos(ang)
        S[p] = np.sin(ang)
    R = np.zeros((128, 128), np.float32)
    for m in range(128):
        if (m % 32) < 16:
            R[m + 16, m] = -1.0
        else:
            R[m - 16, m] = 1.0
    ident = np.eye(128, dtype=np.float32)
    cc = np.arange(128)[:, None]
    rr = np.arange(128)[None, :]
    mA = np.where(cc >= rr, 0.0, NEG).astype(np.float32)
    mB = np.where(cc <= rr, 0.0, NEG).astype(np.float32)
    masks = np.concatenate([np.tile(mA, (1, 4)), np.tile(mB, (1, 4))], axis=1)
    ecf = np.ones((128, 256), np.float32)
    for g, w in enumerate((2, 4, 8, 16)):
        hw = w // 2
        for si in range(2):
            base = (g * 2 + si) * 32
            for sq in range(2):
                for i in range(hw):
                    ecf[:, base + sq * 8 + i] = w / float(i + hw)
                for i in range(hw - 1):
                    ecf[:, base + 16 + sq * 8 + i] = w / float(2 * hw - 1 - i)
    return C, S, R, ident, masks, ecf


def _run(inputs, depth=DEPTH, trace=False, stop=None):
    f = lambda a: np.ascontiguousarray(np.asarray(a, dtype=np.float32))
    x_prompt, x_sample = f(inputs['x_prompt']), f(inputs['x_sample'])
    cache_k, cache_v = f(inputs['cache_k']), f(inputs['cache_v'])
    c, c_ctx = f(inputs['c']), f(inputs['c_ctx'])
    C, S, R, ident, masks, ecf = _constants()
    common = np.concatenate([
        f(inputs['b_mod']).reshape(-1, 128), f(inputs['pool_scale']).reshape(-1, 128),
        f(inputs['ln1_g']).reshape(-1, 128), f(inputs['ln1_b']).reshape(-1, 128),
        f(inputs['ln2_g']).reshape(-1, 128), f(inputs['ln2_b']).reshape(-1, 128),
        f(inputs['conv_w']).reshape(-1, 128), f(inputs['conv_b']).reshape(-1, 128),
        c_ctx.reshape(8, 128)], axis=0)
    assert common.shape[0] == V_COND + 8
    shared = {
        'ropeC': C, 'ropeS': S, 'rperm': R, 'ident': ident, 'masks': masks, 'ecf': ecf,
        'sink': f(inputs['attn_sink']),
        'w_mod': f(inputs['w_mod']), 'w_in': f(inputs['w_in']), 'w_pool': f(inputs['w_pool']),
        'w_out': f(inputs['w_out']), 'w_up': f(inputs['w_up']), 'w_down': f(inputs['w_down']),
    }
    in_maps = []
    for core in range(8):
        vp = np.zeros((V_ROWS, 128), np.float32)
        vp[:common.shape[0]] = common
        vp[V_COND + 8:V_COND + 16] = c[core].reshape(8, 128)
        m = dict(shared)
        m['x_in'] = np.ascontiguousarray(np.concatenate([x_prompt[2 * core], x_prompt[2 * core + 1], x_sample[core]], axis=0))
        m['ck'] = np.ascontiguousarray(cache_k[core].reshape(DEPTH, 256, 128))
        m['cv'] = np.ascontiguousarray(cache_v[core].reshape(DEPTH, 256, 128))
        m['vecpack'] = vp
        in_maps.append(m)
    nc = build_program(depth, stop)
    res = run_bass_kernel_spmd(nc, in_maps, core_ids=list(range(8)), trace=trace)
    y_prompt = np.zeros((16, 256, D), np.float32)
    y_sample = np.zeros((8, 1024, D), np.float32)
    nk = np.zeros((16, DEPTH, 256, 2, 64), np.float32)
    nv = np.zeros((16, DEPTH, 256, 2, 64), np.float32)
    for core in range(8):
        r = res.results[core]
        y = np.asarray(r['y'])
        y_prompt[2 * core] = y[0:256]
        y_prompt[2 * core + 1] = y[256:512]
        y_sample[core] = y[512:1536]
        nk[2 * core:2 * core + 2] = np.asarray(r['nk']).reshape(2, DEPTH, 256, 2, 64)
        nv[2 * core:2 * core + 2] = np.asarray(r['nv']).reshape(2, DEPTH, 256, 2, 64)
    return (y_prompt, y_sample, nk, nv), res


def kernel(**inputs):
    outs, _ = _run(inputs)
    return outs
```

```python
from contextlib import ExitStack
import numpy as np
import concourse.bass as bass
import concourse.mybir as mybir
from concourse.bass_utils import run_bass_kernel_spmd

F32 = mybir.dt.float32
BF16 = mybir.dt.bfloat16
AF = mybir.ActivationFunctionType
ALU = mybir.AluOpType

DEPTH = 4
D = 1024
NTOK = 1536
ALPHA = (2 * DEPTH) ** 0.25
EPSP = 1e-5 / (ALPHA * ALPHA)
SCALE = 64 ** -0.5
NEG = -30000.0
NS = 5
HALF_PAIRS = (14, 8)
FILL_AB, FILL_BC, FILL_CD, FILL_EA = 16, 10, 30, 0
FILL_ATT = 20
FILL_LN2 = 12
FILL_LN = 16

V_BMOD = 0
V_PSC = 192
V_LN1G = 208
V_LN1B = 240
V_LN2G = 272
V_LN2B = 304
V_CW = 336
V_CB = 864
V_COND = 1040
V_ROWS = 1152


class _Op:
    __slots__ = ('eng', 'meth', 'args', 'kw', 'deps', 'signal', 'sig_val', 'dma_key', 'dma_val', 'dma_waits', 'idx')


class Prog:
    def __init__(self):
        self.ops = []
        self.last_w = {}
        self.readers = {}
        self.dma_count = {}

    def transfer(self, old, new):
        comb = {}
        dl = []
        for r in old:
            for w in self.last_w.get(r, ()):
                if w.dma_key is not None:
                    dl.append(w)
                elif w.eng not in comb or comb[w.eng].idx < w.idx:
                    comb[w.eng] = w
            for k, v in self.readers.get(r, {}).items():
                if k == '_dma':
                    dl.extend(v)
                elif k not in comb or comb[k].idx < v.idx:
                    comb[k] = v
        for n in new:
            self.last_w[n] = []
            rd = dict(comb)
            if dl:
                rd['_dma'] = list(dl)
            self.readers[n] = rd

    def add(self, eng, meth, args=(), kw=None, reads=(), writes=(), dma_key=None):
        op = _Op()
        op.eng = eng
        op.meth = meth
        op.args = args
        op.kw = kw or {}
        op.dma_key = dma_key
        op.deps = []
        op.signal = False
        op.sig_val = None
        op.dma_waits = {}
        op.idx = len(self.ops)
        deps = {}

        def adddep(d):
            if d is None or d is op:
                return
            if eng == 'pe' and d.eng == 'pe' and d.dma_key is None:
                return
            deps[d.idx] = d

        for r in reads:
            for w in self.last_w.get(r, ()):
                adddep(w)
            if r.startswith('pb'):
                for k, v in self.readers.get(r, {}).items():
                    if k != eng and k != '_dma':
                        adddep(v)
        for w_ in writes:
            for w in self.last_w.get(w_, ()):
                adddep(w)
            rd = self.readers.get(w_)
            if rd:
                for k, v in rd.items():
                    if k == '_dma':
                        for x in v:
                            adddep(x)
                    else:
                        adddep(v)
        for d in deps.values():
            if d.dma_key is not None:
                k = d.dma_key
                op.dma_waits[k] = max(op.dma_waits.get(k, 0), self.dma_count[k])
            else:
                op.deps.append(d)
        for r in reads:
            rd = self.readers.setdefault(r, {})
            if dma_key is not None:
                rd.setdefault('_dma', []).append(op)
            else:
                rd[eng] = op
        for w_ in writes:
            self.last_w[w_] = [op]
            self.readers[w_] = {}
        if dma_key is not None:
            self.dma_count[dma_key] = self.dma_count.get(dma_key, 0) + 16
            op.dma_val = self.dma_count[dma_key]
        self.ops.append(op)
        return op

    def emit(self, nc, stack):
        for op in self.ops:
            for d in op.deps:
                d.signal = True
        cnt = {}
        for op in self.ops:
            if op.dma_key is None and op.signal:
                cnt[op.eng] = cnt.get(op.eng, 0) + 1
                op.sig_val = cnt[op.eng]
        esem = {}
        for e in ('pe', 'act', 'dve', 'pool', 'sp'):
            esem[e] = stack.enter_context(nc.semaphore("s_" + e))
        dsem = {}
        for k in self.dma_count:
            dsem[k] = stack.enter_context(nc.semaphore("d_%s" % k))
        byeng = {}
        for op in self.ops:
            byeng.setdefault(op.eng, []).append(op)
        block = stack.enter_context(nc.Block())
        self.n_waits = 0

        def run_engine(ename, eobj):
            waited = {}
            for op in byeng.get(ename, []):
                need = {}
                for d in op.deps:
                    key = ('e', d.eng)
                    need[key] = (esem[d.eng], max(need.get(key, (None, 0))[1], d.sig_val))
                for k, v in op.dma_waits.items():
                    key = ('d', k)
                    need[key] = (dsem[k], max(need.get(key, (None, 0))[1], v))
                for key, (s, v) in need.items():
                    if waited.get(key, 0) >= v:
                        continue
                    eobj.wait_ge(s, v)
                    self.n_waits += 1
                    waited[key] = v
                if op.meth is None:
                    continue
                ins = getattr(eobj, op.meth)(*op.args, **op.kw)
                if op.dma_key is not None:
                    ins.then_inc(dsem[op.dma_key], 16)
                elif op.signal:
                    ins.then_inc(esem[op.eng], 1)

        @block.tensor
        def _(e):
            run_engine('pe', e)

        @block.scalar
        def _(e):
            run_engine('act', e)

        @block.vector
        def _(e):
            run_engine('dve', e)

        @block.gpsimd
        def _(e):
            run_engine('pool', e)

        @block.sync
        def _(e):
            run_engine('sp', e)


def _seg(name):
    return (0, 512, 2, 256, 0) if name == 'P' else (512, 1024, 1, 1024, 1)


class _Stop(Exception):
    pass


def build_program(depth=DEPTH, stop=None):
    nc = bass.Bass("TRN2", target_bir_lowering=False)
    dt = nc.dram_tensor
    x_in = dt("x_in", [NTOK, D], F32, kind="ExternalInput").ap()
    ck = dt("ck", [DEPTH, 256, 128], F32, kind="ExternalInput").ap()
    cv = dt("cv", [DEPTH, 256, 128], F32, kind="ExternalInput").ap()
    vecpack = dt("vecpack", [V_ROWS, 128], F32, kind="ExternalInput").ap()
    sink_d = dt("sink", [DEPTH, 8], F32, kind="ExternalInput").ap()
    ropeC_d = dt("ropeC", [128, 1024], F32, kind="ExternalInput").ap()
    ropeS_d = dt("ropeS", [128, 1024], F32, kind="ExternalInput").ap()
    rperm_d = dt("rperm", [128, 128], F32, kind="ExternalInput").ap()
    ident_d = dt("ident", [128, 128], F32, kind="ExternalInput").ap()
    masks_d = dt("masks", [128, 1024], F32, kind="ExternalInput").ap()
    ecf_d = dt("ecf", [128, 256], F32, kind="ExternalInput").ap()
    w_mod = dt("w_mod", [DEPTH, D, 6 * D], F32, kind="ExternalInput").ap()
    w_in = dt("w_in", [DEPTH, D, 1280], F32, kind="ExternalInput").ap()
    w_pool = dt("w_pool", [DEPTH, 4, 128, 128], F32, kind="ExternalInput").ap()
    w_out = dt("w_out", [DEPTH, D, D], F32, kind="ExternalInput").ap()
    w_up = dt("w_up", [DEPTH, D, 5632], F32, kind="ExternalInput").ap()
    w_down = dt("w_down", [DEPTH, 2816, D], F32, kind="ExternalInput").ap()
    y_out = dt("y", [NTOK, D], F32, kind="ExternalOutput").ap()
    nk_out = dt("nk", [2, DEPTH, 256, 128], F32, kind="ExternalOutput").ap()
    nv_out = dt("nv", [2, DEPTH, 256, 128], F32, kind="ExternalOutput").ap()

    P = Prog()
    st = ExitStack()
    with st:
        def sb(n, s, d):
            return st.enter_context(nc.sbuf_tensor(n, s, d))

        xT = sb("xT", [128, 8, NTOK], F32)
        hT = sb("hT", [128, 8, NTOK], BF16)
        vecs = sb("vecs", [128, V_ROWS], F32)
        ropeC = sb("ropeCs", [128, 1024], F32)
        ropeS = sb("ropeSs", [128, 1024], F32)
        rperm = sb("rperms", [128, 128], F32)
        ident32 = sb("ident32", [128, 128], F32)
        identb = sb("identb", [128, 128], BF16)
        ones = sb("ones", [128, 128], BF16)
        maskb = sb("maskb", [128, 2, 512], BF16)
        ecf = sb("ecfs", [128, 256], F32)
        sink8 = sb("sink8", [1, 8], F32)
        sinkf8 = sb("sinkf8", [1, 8], F32)
        sinkh8 = sb("sinkh8", [1, 8], BF16)
        sinkl8 = sb("sinkl8", [1, 8], BF16)
        sinkhl = sb("sinkhl", [33, 1024], BF16)
        sinkL = [sb("sinkL%d" % i, [33, 128], BF16) for i in range(2)]
        scT = sb("scT", [128, 8, 2], BF16)
        modsT = [sb("modsT%d" % i, [128, 2, 48], F32) for i in range(2)]
        coef = sb("coef", [128, 2, 12, 2, 8], F32)
        ring = [sb("ring%d" % i, [128, 4096], BF16) for i in range(NS)]
        wpool = [sb("wpool%d" % i, [128, 4, 128], BF16) for i in range(2)]
        lnm = sb("lnm", [128, 512], F32)
        lnq = sb("lnq", [128, 512], F32)
        ybf = [sb("ybf%d" % i, [128, 512], BF16) for i in range(2)]
        ysq = [sb("ysq%d" % i, [128, 512], BF16) for i in range(2)]
        UW = 16128
        U = sb("U", [128, UW], F32)
        ps = st.enter_context(nc.psum_tensor("ps", [128, 8, 512], F32))

        cur = [0]

        def carve(nwords, dtype, shape=()):
            a = cur[0]
            cur[0] += nwords
            assert cur[0] <= UW, cur[0]
            v = U[:, a:a + nwords]
            if dtype == BF16:
                v = v.bitcast(BF16)
            if len(shape) == 2:
                return v.rearrange("p (a b) -> p a b", a=shape[0])
            if len(shape) == 3:
                return v.rearrange("p (a b c) -> p a b c", a=shape[0], b=shape[1])
            return v

        qT = carve(3072, BF16, (4, NTOK))
        kT = carve(768, BF16)
        Vb = carve(1536, BF16, (12, 256))
        poolT = carve(3072, BF16, (4, NTOK))
        kcT = carve(128, BF16)
        vcb = carve(256, BF16, (2, 256))
        p32 = carve(1040, F32)
        sA = carve(1040, F32)
        sB = carve(1040, F32)
        dTt = [carve(512, BF16) for _ in range(2)]
        q32 = [carve(512, F32) for _ in range(2)]
        t1 = carve(512, F32)
        PT = [carve(512, BF16, (2, 512)) for _ in range(3)]
        kvst = sA[:, 0:1024].rearrange("p (a b) -> p a b", a=4)
        cstage = sB[:, 0:512].rearrange("p (a b c) -> p a b c", a=2, b=2)
        rcp = [lnm, lnq]
        rcpn = ['lnm', 'lnq']
        cur[0] = 0
        mT = carve(14 * 768, BF16, (14, NTOK))
        ta = [carve(1024, F32) for _ in range(2)]
        tg = [carve(1024, F32) for _ in range(2)]
        cur[0] = 0
        xs = [carve(1024, F32) for _ in range(6)]

        MIX_RES = (['qT%d%s' % (c, s) for c in range(4) for s in 'PS'] + ['poolT%d%s' % (c, s) for c in range(4) for s in 'PS'] +
                   ['kTP', 'kTS', 'VbP', 'VbS', 'kcT', 'vcb', 'p32', 'sA', 'sB', 'dT0', 'dT1', 'q320', 'q321', 't1',
                    'PT0', 'PT1', 'PT2'])
        FFN_RES = ['mT%d%s' % (j, s) for j in range(14) for s in 'PS'] + ['ta0', 'ta1', 'tg0', 'tg1']
        IO_RES = ['xs%d' % i for i in range(6)]

        def MM(out, lhsT, rhs, start, stop, reads, writes):
            P.add('pe', 'matmul', (out, lhsT, rhs), dict(start=start, stop=stop), reads, writes)

        def TR(out, in_, ident, reads, writes):
            P.add('pe', 'transpose', (out, in_, ident), None, reads, writes)

        def ACT(out, in_, func, reads, writes, bias=None, scale=None):
            kw = {}
            if bias is not None:
                kw['bias'] = bias
            if scale is not None:
                kw['scale'] = scale
            P.add('act', 'activation', (out, in_, func), kw, reads, writes)

        def ACP(out, in_, reads, writes):
            P.add('act', 'copy', (out, in_), None, reads, writes)

        def VCP(out, in_, reads, writes):
            P.add('dve', 'tensor_copy', (out, in_), None, reads, writes)

        def TT(out, in0, in1, op, reads, writes):
            P.add('dve', 'tensor_tensor', (), dict(out=out, in0=in0, in1=in1, op=op), reads, writes)

        def TS(out, in0, s1, s2, op0, op1, reads, writes):
            kw = dict(out=out, in0=in0, scalar1=s1, scalar2=s2, op0=op0)
            if op1 is not None:
                kw['op1'] = op1
            P.add('dve', 'tensor_scalar', (), kw, reads, writes)

        def STT(out, in0, scalar, in1, op0, op1, reads, writes):
            P.add('dve', 'scalar_tensor_tensor', (), dict(out=out, in0=in0, scalar=scalar, in1=in1, op0=op0, op1=op1), reads, writes)

        def DMA(q, out, in_, reads, writes, key):
            P.add(q, 'dma_start', (), dict(out=out, in_=in_), reads, writes, dma_key=key)

        slot_ctr = [0]
        reserved = set()

        def _bank_recency(bk):
            r = 'pb%d' % bk
            m = -1
            for w in P.last_w.get(r, ()):
                m = max(m, w.idx)
            for k, v in P.readers.get(r, {}).items():
                if k == '_dma':
                    for x in v:
                        m = max(m, x.idx)
                else:
                    m = max(m, v.idx)
            return m

        def next_slot():
            best, bm = None, None
            for s in range(4):
                if s in reserved:
                    continue
                m = max(_bank_recency(2 * s), _bank_recency(2 * s + 1))
                if bm is None or m < bm:
                    best, bm = s, m
            return best

        def slot_res(s):
            return ['pb%d' % (2 * s), 'pb%d' % (2 * s + 1)]

        def slot_flat(s):
            return ps[:, 2 * s:2 * s + 2, :].rearrange("p a b -> p (a b)")

        ring_ctr = [0]

        def v3(t, a):
            return t[:].rearrange("p (a b) -> p a b", a=a)

        pinned = set()

        def ring_fill(dmas):
            while True:
                i = ring_ctr[0] % NS
                ring_ctr[0] += 1
                if i not in pinned:
                    break
            for dst_fn, src in dmas:
                DMA('pool', dst_fn(ring[i]), src, [], ['ring%d' % i], 'ring%d' % i)
            return i

        DMA('sp', ident32[:], ident_d, [], ['ident32'], 'c0')
        DMA('sp', rperm[:], rperm_d, [], ['rperm'], 'c0')
        DMA('sp', ecf[:], ecf_d, [], ['ecf'], 'c0')
        DMA('sp', ropeC[:], ropeC_d, [], ['ropeC'], 'c0')
        DMA('sp', ropeS[:], ropeS_d, [], ['ropeS'], 'c0')
        DMA('pool', identb[:], ident_d, [], ['identb'], 'c1')
        DMA('pool', maskb[:].rearrange("p a b -> p (a b)"), masks_d, [], ['maskb'], 'c1')
        P.add('dve', 'memset', (ones[:], 1.0), None, [], ['ones'])
        P.add('dve', 'memset', (sinkhl[:], 0.0), None, [], ['sinkhl'])
        for kvh in range(2):
            P.add('dve', 'memset', (sinkL[kvh][:], 0.0), None, [], ['sinkL'])
            P.add('dve', 'memset', (sinkL[kvh][:, (1 - kvh) * 64:(2 - kvh) * 64], 1.0), None, ['sinkL'], ['sinkL'])
        DMA('sp', xs[4].rearrange("p (t c) -> p t c", t=8), vecpack[0:1024, :].rearrange("(t p) c -> p t c", p=128), [], ['xs4'], 'xs4')
        DMA('sp', xs[5][:, 0:128], vecpack[1024:1152, :], [], ['xs5'], 'xs5')
        for t in range(9):
            src = xs[4][:, t * 128:(t + 1) * 128] if t < 8 else xs[5][:, 0:128]
            sn = 'xs4' if t < 8 else 'xs5'
            s = next_slot()
            TR(ps[:, 2 * s, 0:128], src, ident32[:], [sn, 'ident32'], slot_res(s))
            ACP(vecs[:, t * 128:(t + 1) * 128], ps[:, 2 * s, 0:128], slot_res(s), ['vecs'])
        for c in range(2):
            ACT(scT[:, :, c], vecs[:, V_COND + 8 * c:V_COND + 8 * c + 8], AF.Silu, ['vecs'], ['scT'])

        for ti_, tb in enumerate(list(range(4, 12)) + list(range(4))):
            b = ti_ % 4
            DMA('sp', xs[b], x_in[tb * 128:(tb + 1) * 128, :], [], ['xs%d' % b], 'xs%d' % b)
            s = next_slot()
            for c in range(8):
                TR(slot_flat(s)[:, c * 128:(c + 1) * 128], xs[b][:, c * 128:(c + 1) * 128], ident32[:], ['xs%d' % b, 'ident32'], slot_res(s))
            sg = 'P' if tb < 4 else 'S'
            ACP(xT[:, :, tb * 128:(tb + 1) * 128], slot_flat(s).rearrange("p (c t) -> p c t", c=8), slot_res(s), ['xT%d%s' % (c, sg) for c in range(8)])

        def cf(par, kind, cond, c):
            return coef[:, par, kind, cond, c:c + 1]

        K_A1, K_B1, K_G1P, K_G2P, K_A2, K_B2, K_OP1, K_OP2 = range(8)

        def mods_for_layer(l, pieces):
            par = l % 2
            for pc in pieces:
                i = ring_fill([(lambda r: v3(r, 8), w_mod[l, :, pc * 512:(pc + 1) * 512].rearrange("(k p) n -> p k n", p=128))])
                s = next_slot()
                wv = v3(ring[i], 8)
                for mm in range(4):
                    for kc in range(8):
                        MM(ps[:, 2 * s, mm * 2:mm * 2 + 2], wv[:, kc, mm * 128:(mm + 1) * 128], scT[:, kc, :], kc == 0, kc == 7, ['ring%d' % i, 'scT'], slot_res(s))
                for c in range(2):
                    bc = V_BMOD + l * 48 + pc * 4
                    TT(modsT[par][:, c, pc * 4:pc * 4 + 4], ps[:, 2 * s, c:8:2], vecs[:, bc:bc + 4], ALU.add, slot_res(s) + ['vecs'], ['mods%d_%d' % (par, pc)])

        def mres(par, j):
            return ['mods%d_%d' % (par, 2 * j), 'mods%d_%d' % (par, 2 * j + 1)]

        def coef_layer_start(l):
            par = l % 2
            for c in range(2):
                TS(coef[:, par, K_A1, c, :], modsT[par][:, c, 8:16], 1.0, None, ALU.add, None, mres(par, 1), ['cfA1_%d' % par])
                VCP(coef[:, par, K_B1, c, :], modsT[par][:, c, 0:8], mres(par, 0), ['cfB1_%d' % par])

        def coef_mid(l):
            par = l % 2
            g1 = vecs[:, V_LN1G + l * 8:V_LN1G + l * 8 + 8]
            b1 = vecs[:, V_LN1B + l * 8:V_LN1B + l * 8 + 8]
            for c in range(2):
                TS(coef[:, par, K_G1P, c, :], modsT[par][:, c, 16:24], 1.0 / ALPHA, None, ALU.mult, None, mres(par, 2), ['cfG1_%d' % par])
                TS(coef[:, par, K_G2P, c, :], modsT[par][:, c, 40:48], 1.0 / ALPHA, None, ALU.mult, None, mres(par, 5), ['cfG2_%d' % par])
                TS(coef[:, par, K_OP2, c, :], modsT[par][:, c, 32:40], 1.0, None, ALU.add, None, mres(par, 4), ['cfO2_%d' % par])
                TT(coef[:, par, K_A2, c, :], coef[:, par, K_OP2, c, :], g1, ALU.mult, ['cfO2_%d' % par, 'vecs'], ['cfA2_%d' % par])
                TT(coef[:, par, K_B2, c, :], coef[:, par, K_OP2, c, :], b1, ALU.mult, ['cfO2_%d' % par, 'vecs'], ['cfB2_%d' % par])
                TT(coef[:, par, K_B2, c, :], coef[:, par, K_B2, c, :], modsT[par][:, c, 24:32], ALU.add, ['cfB2_%d' % par] + mres(par, 3), ['cfB2_%d' % par])

        def coef_next(l):
            par = (l + 1) % 2
            g2 = vecs[:, V_LN2G + l * 8:V_LN2G + l * 8 + 8]
            b2 = vecs[:, V_LN2B + l * 8:V_LN2B + l * 8 + 8]
            for c in range(2):
                TS(coef[:, par, K_OP1, c, :], modsT[par][:, c, 8:16], 1.0, None, ALU.add, None, mres(par, 1), ['cfO1_%d' % par])
                TT(coef[:, par, K_A1, c, :], coef[:, par, K_OP1, c, :], g2, ALU.mult, ['cfO1_%d' % par, 'vecs'], ['cfA1_%d' % par])
                TT(coef[:, par, K_B1, c, :], coef[:, par, K_OP1, c, :], b2, ALU.mult, ['cfO1_%d' % par, 'vecs'], ['cfB1_%d' % par])
                TT(coef[:, par, K_B1, c, :], coef[:, par, K_B1, c, :], modsT[par][:, c, 0:8], ALU.add, ['cfB1_%d' % par] + mres(par, 0), ['cfB1_%d' % par])

        def mm_fm(s, sgn, lhs_list, rhs_fn, reads):
            c0, n, _, _, _ = _seg(sgn)
            nk = len(lhs_list)
            for k in range(nk):
                for h in range(n // 512):
                    MM(ps[:, 2 * s + h, :], lhs_list[k], rhs_fn(k, c0 + h * 512, c0 + (h + 1) * 512), k == 0, k == nk - 1, reads, slot_res(s))

        def layer_norm(l, which, last):
            par = l % 2
            for tb in range(3):
                sgn = 'P' if tb == 0 else 'S'
                cond = 0 if tb == 0 else 1
                c0 = tb * 512
                s = next_slot()
                xres = ['xT%d%s' % (c, sgn) for c in range(8)]
                for c in range(8):
                    b = c % 2
                    xv = xT[:, c, c0:c0 + 512]
                    ACP(ybf[b][:], xv, [xres[c]], ['ybf%d' % b])
                    ACT(ysq[b][:], xv, AF.Square, [xres[c]], ['ysq%d' % b])
                    MM(ps[:, 2 * s, :], ones[:], ybf[b][:], c == 0, c == 7, ['ones', 'ybf%d' % b], slot_res(s))
                    MM(ps[:, 2 * s + 1, :], ones[:], ysq[b][:], c == 0, c == 7, ['ones', 'ysq%d' % b], slot_res(s))
                ACT(lnm[:], ps[:, 2 * s, :], AF.Identity, slot_res(s), ['lnm'], scale=1.0 / D)
                TT(lnq[:], lnm[:], lnm[:], ALU.mult, ['lnm'], ['lnq'])
                STT(lnq[:], ps[:, 2 * s + 1, :], 1.0 / D, lnq[:], ALU.mult, ALU.subtract, slot_res(s) + ['lnq'], ['lnq'])
                TS(lnq[:], lnq[:], EPSP, None, ALU.add, None, ['lnq'], ['lnq'])
                ACT(lnq[:], lnq[:], AF.Ln, ['lnq'], ['lnq'])
                ACT(lnq[:], lnq[:], AF.Exp, ['lnq'], ['lnq'], scale=-0.5)
                if which == 1:
                    gcol, bcol = V_LN1G + l * 8, V_LN1B + l * 8
                    kA, kB, cpar = K_A2, K_B2, par
                    cres = ['cfA2_%d' % par, 'cfB2_%d' % par]
                else:
                    gcol, bcol = V_LN2G + l * 8, V_LN2B + l * 8
                    kA, kB, cpar = K_A1, K_B1, (l + 1) % 2
                    cres = ['cfA1_%d' % cpar, 'cfB1_%d' % cpar]
                for c in range(8):
                    xv = xT[:, c, c0:c0 + 512]
                    TT(xv, xv, lnm[:], ALU.subtract, [xres[c], 'lnm'], [xres[c]])
                    TT(xv, xv, lnq[:], ALU.mult, [xres[c], 'lnq'], [xres[c]])
                    if not last:
                        TS(hT[:, c, c0:c0 + 512], xv, cf(cpar, kA, cond, c), cf(cpar, kB, cond, c), ALU.mult, ALU.add, [xres[c]] + cres, ['hT%d%s' % (c, sgn)])
                    ACT(xv, xv, AF.Identity, [xres[c], 'vecs'], [xres[c]], bias=vecs[:, bcol + c:bcol + c + 1], scale=vecs[:, gcol + c:gcol + c + 1])

        def hrhs(k, a, b):
            return hT[:, k, a:b]

        def mixrhs(k, a, b):
            return hT[:, k, a:b] if k < 4 else poolT[:, k - 4, a:b]

        if stop == 'setup':
            depth = 0
        mods_for_layer(0, range(0, 4))
        coef_layer_start(0)
        for sgn in 'SP':
            c0, n, _, _, cond = _seg(sgn)
            for c in range(8):
                if c % 2 == 0:
                    ACT(hT[:, c, c0:c0 + n], xT[:, c, c0:c0 + n], AF.Identity, ['xT%d%s' % (c, sgn), 'cfA1_0', 'cfB1_0'], ['hT%d%s' % (c, sgn)],
                        bias=cf(0, K_B1, cond, c), scale=cf(0, K_A1, cond, c))
                else:
                    TS(hT[:, c, c0:c0 + n], xT[:, c, c0:c0 + n], cf(0, K_A1, cond, c), cf(0, K_B1, cond, c), ALU.mult, ALU.add,
                       ['xT%d%s' % (c, sgn), 'cfA1_0', 'cfB1_0'], ['hT%d%s' % (c, sgn)])
        P.transfer(IO_RES, MIX_RES)

        hres = {sgn: ['hT%d%s' % (c, sgn) for c in range(8)] for sgn in 'PS'}
        mixres = {sgn: ['hT%d%s' % (c, sgn) for c in range(4)] + ['poolT%d%s' % (c, sgn) for c in range(4)] for sgn in 'PS'}

        def run_pipeline(units, reverse=True):
            nst = max(len(u) for u in units)
            for step in range(len(units) + nst - 1):
                ks = range(nst - 1, -1, -1) if reverse else range(nst)
                for k in ks:
                    n = step - k
                    if 0 <= n < len(units) and k < len(units[n]) and units[n][k] is not None:
                        units[n][k]()

        pending = []

        def emit_pending(k):
            for _ in range(min(k, len(pending))):
                pending.pop(0)()

        def ln_tail_pieces(l, which, last, tb, sb0, sb1):
            par = l % 2
            sgn = 'P' if tb == 0 else 'S'
            cond = 0 if tb == 0 else 1
            c0 = tb * 512
            xres = ['xT%d%s' % (c, sgn) for c in range(8)]
            if which == 1:
                gcol, bcol = V_LN1G + l * 8, V_LN1B + l * 8
                kA, kB, cpar = K_A2, K_B2, par
                cres = ['cfA2_%d' % par, 'cfB2_%d' % par]
            else:
                gcol, bcol = V_LN2G + l * 8, V_LN2B + l * 8
                kA, kB, cpar = K_A1, K_B1, (l + 1) % 2
                cres = ['cfA1_%d' % cpar, 'cfB1_%d' % cpar]

            def head_a():
                ACT(lnm[:], ps[:, sb0, :], AF.Identity, ['pb%d' % sb0], ['lnm'], scale=1.0 / D)

            def head_b():
                TT(lnq[:], lnm[:], lnm[:], ALU.mult, ['lnm'], ['lnq'])
                STT(lnq[:], ps[:, sb1, :], 1.0 / D, lnq[:], ALU.mult, ALU.subtract, ['pb%d' % sb1, 'lnq'], ['lnq'])
                TS(lnq[:], lnq[:], EPSP, None, ALU.add, None, ['lnq'], ['lnq'])

            def head_c():
                ACT(lnq[:], lnq[:], AF.Ln, ['lnq'], ['lnq'])
                ACT(lnq[:], lnq[:], AF.Exp, ['lnq'], ['lnq'], scale=-0.5)

            def chunk(c):
                def f():
                    xv = xT[:, c, c0:c0 + 512]
                    TT(xv, xv, lnm[:], ALU.subtract, [xres[c], 'lnm'], [xres[c]])
                    TT(xv, xv, lnq[:], ALU.mult, [xres[c], 'lnq'], [xres[c]])
                    if not last:
                        if c % 2 == 0:
                            ACT(hT[:, c, c0:c0 + 512], xv, AF.Identity, [xres[c]] + cres, ['hT%d%s' % (c, sgn)], bias=cf(cpar, kB, cond, c), scale=cf(cpar, kA, cond, c))
                        else:
                            TS(hT[:, c, c0:c0 + 512], xv, cf(cpar, kA, cond, c), cf(cpar, kB, cond, c), ALU.mult, ALU.add, [xres[c]] + cres, ['hT%d%s' % (c, sgn)])
                    ACT(xv, xv, AF.Identity, [xres[c], 'vecs'], [xres[c]], bias=vecs[:, bcol + c:bcol + c + 1], scale=vecs[:, gcol + c:gcol + c + 1])
                return f
            return [head_a, head_b, head_c] + [chunk(c) for c in range(8)]

        def store_block(tb):
            sgn = 'P' if tb == 0 else 'S'
            for tbk in range(4 * tb, 4 * tb + 4):
                b = tbk % 2
                s = 2 + tbk % 2
                for c in range(8):
                    TR(slot_flat(s)[:, c * 128:(c + 1) * 128], xT[:, c, tbk * 128:(tbk + 1) * 128], ident32[:], ['xT%d%s' % (c, sgn), 'ident32'], slot_res(s))
                ACP(ta[b], slot_flat(s), slot_res(s), ['ta%d' % b])
                DMA('sp', y_out[tbk * 128:(tbk + 1) * 128, :], ta[b], ['ta%d' % b], ['y_out%d' % b], 'yo%d' % b)

        def filler(n):
            if n <= 0:
                return
            s = next_slot()
            for i in range(n):
                MM(ps[:, 2 * s, :], ones[:], maskb[:, 0, :], True, True, ['ones', 'maskb'], ['pb%d' % (2 * s)])

        def proj_ln(l, which, last, lhs_fn, nk, rhs_fn, rres_fn, wres, gkind, gres, nfill=0):
            par = l % 2
            reserved.update([0, 1, 2, 3])
            bctr = [0]
            prev_tb = None
            lo_rec = max(_bank_recency(bk_) for bk_ in range(0, 4))
            hi_rec = max(_bank_recency(bk_) for bk_ in range(4, 8))
            ubase, sbase = (4, 0) if hi_rec <= lo_rec or last else (0, 4)
            for ti, tb in enumerate((1, 2, 0)):
                sgn = 'P' if tb == 0 else 'S'
                cond = 0 if tb == 0 else 1
                c0 = tb * 512
                sb0, sb1 = (sbase, sbase + 1) if ti % 2 == 0 else (sbase + 2, sbase + 3)
                units = []
                for dc in range(8):
                    def s1(dc=dc, sgn=sgn, cond=cond, c0=c0, ti=ti):
                        bk = ubase + bctr[0] % 4
                        bctr[0] += 1
                        lhs = lhs_fn(dc)
                        for k in range(nk):
                            MM(ps[:, bk, :], lhs[k], rhs_fn(k, c0, c0 + 512), k == 0, k == nk - 1, wres + rres_fn(sgn), ['pb%d' % bk])
                        if ti == 0 and dc == 0:
                            for i in range(FILL_LN2):
                                MM(ps[:, ubase + 3, :], ones[:], maskb[:, 0, :], True, True, ['ones', 'maskb'], ['pb%d' % (ubase + 3)])
                        xv = xT[:, dc, c0:c0 + 512]
                        STT(xv, ps[:, bk, :], cf(par, gkind, cond, dc), xv, ALU.mult, ALU.add, ['pb%d' % bk, gres, 'xT%d%s' % (dc, sgn)], ['xT%d%s' % (dc, sgn)])
                        b = dc % 2
                        ACP(ybf[b][:], xv, ['xT%d%s' % (dc, sgn)], ['ybf%d' % b])
                        ACT(ysq[b][:], xv, AF.Square, ['xT%d%s' % (dc, sgn)], ['ysq%d' % b])
                        emit_pending((1, 1, 1, 2, 2, 2, 1, 1)[dc])

                    def s3(dc=dc, sb0=sb0, sb1=sb1):
                        b = dc % 2
                        MM(ps[:, sb0, :], ones[:], ybf[b][:], dc == 0, dc == 7, ['ones', 'ybf%d' % b], ['pb%d' % sb0])
                        MM(ps[:, sb1, :], ones[:], ysq[b][:], dc == 0, dc == 7, ['ones', 'ysq%d' % b], ['pb%d' % sb1])
                    units.append([s1, None, s3])
                run_pipeline(units)
                emit_pending(len(pending))
                if last and prev_tb is not None:
                    store_block(prev_tb)
                prev_tb = tb
                pending.extend(ln_tail_pieces(l, which, last, tb, sb0, sb1))
            reserved.difference_update([0, 1, 2, 3])
            filler(nfill)
            emit_pending(len(pending))
            if last:
                store_block(0)

        for l in range(depth):
          try:
            par = l % 2
            lastl = (l == depth - 1)
            DMA('sp', cstage[:, 0, :, :], ck[l].rearrange("(b p) d -> p b d", p=128), [], ['sB'], 'cst')
            DMA('sp', cstage[:, 1, :, :], cv[l].rearrange("(b p) d -> p b d", p=128), [], ['sB'], 'cst')
            DMA('sp', sink8[:], sink_d[l:l + 1, :], [], ['sink8'], 'snk')
            DMA('pool', wpool[par][:], w_pool[l].rearrange("g c d -> c g d"), [], ['wpool%d' % par], 'wpool%d' % par)
            ACT(sinkf8[:], sink8[:], AF.Exp, ['sink8'], ['sinkf8'])
            VCP(sinkh8[:], sinkf8[:], ['sinkf8'], ['sinkh8'])
            TT(sinkf8[:], sinkf8[:], sinkh8[:], ALU.subtract, ['sinkf8', 'sinkh8'], ['sinkf8'])
            VCP(sinkl8[:], sinkf8[:], ['sinkf8'], ['sinkl8'])
            VCP(sinkhl[0:1, :].rearrange("p (a b) -> p a b", a=8), sinkh8[0:1, :].unsqueeze(2).to_broadcast([1, 8, 128]), ['sinkh8', 'sinkhl'], ['sinkhl'])
            VCP(sinkhl[32:33, :].rearrange("p (a b) -> p a b", a=8), sinkl8[0:1, :].unsqueeze(2).to_broadcast([1, 8, 128]), ['sinkl8', 'sinkhl'], ['sinkhl'])
            P.add('dve', 'memset', (Vb[:, :, 64:192], 1.0), None, [], ['VbP', 'VbS'])
            P.add('dve', 'memset', (vcb[:, :, 64:192], 1.0), None, [], ['vcb'])
            s = next_slot()
            for b in range(2):
                TR(ps[:, 2 * s, b * 128:(b + 1) * 128], cstage[:, 0, b, :], ident32[:], ['sB', 'ident32'], slot_res(s))
            ACP(kcT, ps[:, 2 * s, 0:256], slot_res(s), ['kcT'])
            VCP(vcb[:, :, 0:64], cstage[:, 1, :, 0:64], ['sB', 'vcb'], ['vcb'])
            VCP(vcb[:, :, 192:256], cstage[:, 1, :, 64:128], ['sB', 'vcb'], ['vcb'])

            iq = ring_fill([
                ((lambda r, c=c, hh=hh: v3(r, 8)[:, :, c * 128 + hh * 64:c * 128 + hh * 64 + 64]),
                 w_in[l, :, (hh * 4 + c) * 64:(hh * 4 + c) * 64 + 64].rearrange("(k p) n -> p k n", p=128))
                for c in range(4) for hh in range(2)])
            ikv = ring_fill([(lambda r: v3(r, 8)[:, :, 0:256], w_in[l, :, 512:768].rearrange("(k p) n -> p k n", p=128))])
            ipp = ring_fill([(lambda r: v3(r, 8), w_in[l, :, 768:1280].rearrange("(k p) n -> p k n", p=128))])
            pinned.update([iq, ikv, ipp])
            wq = v3(ring[iq], 8)
            wkv = v3(ring[ikv], 8)
            wpp = v3(ring[ipp], 8)

            def proj_stage(box, sgn, lhs, rres):
                def f():
                    box['s'] = next_slot()
                    mm_fm(box['s'], sgn, lhs, hrhs, rres + hres[sgn])
                return f

            def rope_stage_a(box):
                def f():
                    s = box['s']
                    for h in range(2):
                        ACP(q32[h], ps[:, 2 * s + h, :], ['pb%d' % (2 * s + h)], ['q32%d' % h])
                return f

            def rope_stage_b(dst, dres):
                def f():
                    s2 = next_slot()
                    for h in range(2):
                        MM(ps[:, 2 * s2 + h, :], rperm[:], q32[h], True, True, ['rperm', 'q32%d' % h], ['pb%d' % (2 * s2 + h)])
                    for h in range(2):
                        cs = slice(h * 512, (h + 1) * 512)
                        TT(t1, q32[h], ropeC[:, cs], ALU.mult, ['q32%d' % h, 'ropeC'], ['t1'])
                        TT(q32[h], ps[:, 2 * s2 + h, :], ropeS[:, cs], ALU.mult, ['pb%d' % (2 * s2 + h), 'ropeS', 'q32%d' % h], ['q32%d' % h])
                        TT(dst[:, cs], t1, q32[h], ALU.add, ['t1', 'q32%d' % h], dres)
                return f

            def copy_stage(box, dst, dres):
                def f():
                    s = box['s']
                    ACP(dst, ps[:, 2 * s, :], slot_res(s), dres)
                return f

            def pool_stage2(box, g, sgn, di):
                c0, n, nseq, L, _ = _seg(sgn)
                w = (2, 4, 8, 16)[g]
                hw_ = w // 2
                LP = L + 16

                def pv(buf, lo, hi):
                    return buf[:, 0:nseq * LP].rearrange("p (a b) -> p a b", a=nseq)[:, :, lo:hi]

                def f():
                    s = box['s']
                    P.add('dve', 'memset', (pv(p32, 0, 8), 0.0), None, [], ['p32'])
                    P.add('dve', 'memset', (pv(p32, LP - 8, LP), 0.0), None, ['p32'], ['p32'])
                    P.add('act', 'copy', (pv(p32, 8, 8 + L), slot_flat(s)[:, 0:n].rearrange("p (a b) -> p a b", a=nseq)), None, slot_res(s) + ['p32'], ['p32'])
                    TT(pv(sA, 1, LP), pv(p32, 1, LP), pv(p32, 0, LP - 1), ALU.add, ['p32', 'sA'], ['sA'])
                    src, dst, sn, dn = sA, sB, 'sA', 'sB'
                    lo, hi, sh = 1, LP, 1
                    for _ in range(g):
                        TT(pv(dst, lo + sh, hi - sh), pv(src, lo + 2 * sh, hi), pv(src, lo, hi - 2 * sh), ALU.add, [sn, dn], [dn])
                        lo, hi = lo + sh, hi - sh
                        src, dst, sn, dn = dst, src, dn, sn
                        sh *= 2
                    tot, tn = src, sn
                    eb = (g * 2 + (0 if sgn == 'P' else 1)) * 32
                    ev = ecf[:, eb:eb + nseq * 8].rearrange("p (a b) -> p a b", a=nseq)
                    TT(pv(tot, 8, 8 + hw_), pv(tot, 8, 8 + hw_), ev[:, :, 0:hw_], ALU.mult, [tn, 'ecf'], [tn])
                    if hw_ > 1:
                        ev2 = ecf[:, eb + 16:eb + 16 + nseq * 8].rearrange("p (a b) -> p a b", a=nseq)
                        TT(pv(tot, 8 + L - hw_ + 1, 8 + L), pv(tot, 8 + L - hw_ + 1, 8 + L), ev2[:, :, 0:hw_ - 1], ALU.mult, [tn, 'ecf'], [tn])
                    STT(dTt[di][:, 0:n].rearrange("p (a b) -> p a b", a=nseq), pv(tot, 8, 8 + L), 1.0 / w, pv(p32, 8, 8 + L), ALU.mult, ALU.subtract, [tn, 'p32'], ['dT%d' % di])
                return f

            def pool_stage3(g, sgn, di):
                c0, n, nseq, L, _ = _seg(sgn)

                def f():
                    s = next_slot()
                    for h in range(n // 512):
                        MM(ps[:, 2 * s + h, :], wpool[par][:, g, :], dTt[di][:, h * 512:(h + 1) * 512], True, True, ['wpool%d' % par, 'dT%d' % di], ['pb%d' % (2 * s + h)])
                    psc = V_PSC + l * 4 + g
                    ACT(poolT[:, g, c0:c0 + n], slot_flat(s)[:, 0:n], AF.Identity, slot_res(s) + ['vecs'], ['poolT%d%s' % (g, sgn)], scale=vecs[:, psc:psc + 1])
                return f

            def kvp_units():
                box = {}

                def s1():
                    s = next_slot()
                    box['s'] = s
                    pv4 = slot_flat(s).rearrange("p (a b) -> p a b", a=4)
                    for tbk in range(4):
                        for k in range(8):
                            MM(pv4[:, tbk, :], hT[:, k, tbk * 128:(tbk + 1) * 128], wkv[:, k, 0:256], k == 0, k == 7, ['ring%d' % ikv] + hres['P'], slot_res(s))

                def s2():
                    s = box['s']
                    pv4 = slot_flat(s).rearrange("p (a b) -> p a b", a=4)
                    ACP(kvst, pv4, slot_res(s), ['sA'])
                    ACP(Vb[:, 0:4, 0:64], pv4[:, :, 128:192], slot_res(s) + ['VbP'], ['VbP'])
                    ACP(Vb[:, 0:4, 192:256], pv4[:, :, 192:256], slot_res(s) + ['VbP'], ['VbP'])
                    for b in range(2):
                        DMA('sp', nk_out[b, l].rearrange("(k p) d -> p k d", p=128), kvst[:, 2 * b:2 * b + 2, 0:128], ['sA'], ['nk_out'], 'kvo')
                        DMA('sp', nv_out[b, l].rearrange("(k p) d -> p k d", p=128), kvst[:, 2 * b:2 * b + 2, 128:256], ['sA'], ['nv_out'], 'kvo')
                return [s1, s2]

            def vs_units():
                box = {}

                def s1():
                    s = next_slot()
                    box['s'] = s
                    pv8 = slot_flat(s).rearrange("p (a b) -> p a b", a=8)
                    for tbk in range(8):
                        for k in range(8):
                            MM(pv8[:, tbk, :], hT[:, k, 512 + tbk * 128:512 + (tbk + 1) * 128], wkv[:, k, 128:256], k == 0, k == 7, ['ring%d' % ikv] + hres['S'], slot_res(s))

                def s2():
                    s = box['s']
                    pv8 = slot_flat(s).rearrange("p (a b) -> p a b", a=8)
                    ACP(Vb[:, 4:12, 0:64], pv8[:, :, 0:64], slot_res(s) + ['VbS'], ['VbS'])
                    ACP(Vb[:, 4:12, 192:256], pv8[:, :, 64:128], slot_res(s) + ['VbS'], ['VbS'])
                return [s1, s2]

            klhs = [wkv[:, k, 0:128] for k in range(8)]

            def qu(c, sgn):
                qlhs = [wq[:, k, c * 128:(c + 1) * 128] for k in range(8)]
                box = {}
                if sgn == 'P':
                    return [proj_stage(box, 'P', qlhs, ['ring%d' % iq]), copy_stage(box, qT[:, c, 0:512], ['qT%dP' % c])]
                return [proj_stage(box, 'S', qlhs, ['ring%d' % iq]), rope_stage_a(box), None, rope_stage_b(qT[:, c, 512:1536], ['qT%dS' % c])]

            def pu(c, sgn):
                plhs = [wpp[:, k, c * 128:(c + 1) * 128] for k in range(8)]
                box = {}
                di = 0 if sgn == 'P' else 1
                return [proj_stage(box, sgn, plhs, ['ring%d' % ipp]), pool_stage2(box, c, sgn, di), None, pool_stage3(c, sgn, di)]

            def ku(sgn):
                box = {}
                if sgn == 'P':
                    return [proj_stage(box, 'P', klhs, ['ring%d' % ikv]), copy_stage(box, kT[:, 0:512], ['kTP'])]
                return [proj_stage(box, 'S', klhs, ['ring%d' % ikv]), rope_stage_a(box), None, rope_stage_b(kT[:, 512:1536], ['kTS'])]

            def fill_stage(u, n):
                u[2] = (lambda: filler(n))
                return u

            unitsA = [ku('S'), pu(0, 'S'), qu(0, 'S'), pu(1, 'S'), qu(1, 'S'), pu(2, 'S'), qu(2, 'S'), pu(3, 'S'), qu(3, 'S'), vs_units(),
                      ku('P'), qu(0, 'P'), kvp_units(), qu(1, 'P'), pu(0, 'P'), qu(2, 'P'), pu(1, 'P'), qu(3, 'P'),
                      fill_stage(pu(2, 'P'), FILL_AB), [lambda: None], fill_stage(pu(3, 'P'), FILL_AB)]
            if l == 0:
                for i_, pc in enumerate(range(4, 12)):
                    unitsA.insert(3 + 2 * i_, [lambda pc=pc: mods_for_layer(0, [pc])])
            run_pipeline(unitsA)
            pinned.difference_update([iq, ikv, ipp])
            if stop == 'A':
                raise _Stop()

            coef_mid(l)
            aunits = []
            acc_ctr = [0]

            def add_qblock(q0, kblocks, qres, ores):
                for kvh in range(2):
                    acc = acc_ctr[0] % 4
                    acc_ctr[0] += 1
                    groups = [kblocks[g0:g0 + 2] for g0 in range(0, len(kblocks), 2)]
                    for gi, grp in enumerate(groups):
                        aunits.append(dict(q0=q0, kvh=kvh, grp=grp, first=(gi == 0), last=(gi == len(groups) - 1), acc=acc, qres=qres, ores=ores))

            def att_S(u, n):
                def f():
                    sl = 2 + n % 2
                    h0 = u['kvh'] * 64
                    q0 = u['q0']
                    for j, (kap, vap, mi, kres, vres) in enumerate(u['grp']):
                        bank = 2 * sl + j
                        if mi is not None:
                            MM(ps[:, bank, :], identb[:], maskb[:, mi, :], True, False, ['identb', 'maskb'], ['pb%d' % bank])
                        MM(ps[:, bank, :], kap[h0:h0 + 64, :], qT[h0:h0 + 64, :, q0:q0 + 128], mi is None, True, [kres] + u['qres'], ['pb%d' % bank])
                return f

            def att_PV(u, n):
                def f():
                    sl = 2 + n % 2
                    pti = n % 3
                    kvh = u['kvh']
                    h0 = kvh * 64
                    s0 = 64 - h0
                    q0 = u['q0']
                    accb = u['acc']
                    grp = u['grp']
                    ng = len(grp)
                    ACT(PT[pti][:, 0:ng, :], ps[:, 2 * sl:2 * sl + ng, :], AF.Exp, ['pb%d' % (2 * sl + j) for j in range(ng)], ['PT%d' % pti], scale=SCALE)
                    if u['first']:
                        MM(ps[:, accb, :], sinkL[kvh][:], sinkhl[:, kvh * 512:(kvh + 1) * 512], True, False, ['sinkL', 'sinkhl'], ['pb%d' % accb])
                    for j, (kap, vap, mi, kres, vres) in enumerate(grp):
                        MM(ps[:, accb, :], vap[:, kvh * 128:(kvh + 1) * 128], PT[pti][:, j, :], False, u['last'] and j == ng - 1, [vres, 'PT%d' % pti], ['pb%d' % accb])
                return f

            def att_NORM(u, n):
                if not u['last']:
                    return None

                def f():
                    kvh = u['kvh']
                    h0 = kvh * 64
                    s0 = 64 - h0
                    q0 = u['q0']
                    accb = u['acc']
                    if kvh == 0:
                        P.add('dve', 'reciprocal', (rcp[kvh][s0:s0 + 64, :], ps[s0:s0 + 64, accb, :]), None, ['pb%d' % accb], [rcpn[kvh]])
                    else:
                        ACT(rcp[kvh][s0:s0 + 64, :], ps[s0:s0 + 64, accb, :], AF.Ln, ['pb%d' % accb], [rcpn[kvh]])
                        ACT(rcp[kvh][s0:s0 + 64, :], rcp[kvh][s0:s0 + 64, :], AF.Exp, [rcpn[kvh]], [rcpn[kvh]], scale=-1.0)
                    TT(hT[h0:h0 + 64, 0:4, q0:q0 + 128], ps[h0:h0 + 64, accb, :].rearrange("p (a b) -> p a b", a=4),
                       rcp[kvh][s0:s0 + 64, :].rearrange("p (a b) -> p a b", a=4), ALU.mult, ['pb%d' % accb, rcpn[kvh]], u['ores'])
                return f

            for i in range(8):
                kbl = [(kcT[:, kb * 128:(kb + 1) * 128], vcb[:, kb, :], None, 'kcT', 'vcb') for kb in range(2)]
                for j in (i - 1, i, i + 1):
                    if j < 0 or j > 7:
                        continue
                    mi = None if j == i else (0 if j == i - 1 else 1)
                    kbl.append((kT[:, 512 + j * 128:512 + (j + 1) * 128], Vb[:, 4 + j, :], mi, 'kTS', 'VbS'))
                add_qblock(512 + i * 128, kbl, ['qT%dS' % c for c in range(4)], ['hT%dS' % c for c in range(4)])
            for b in range(2):
                kbl = [(kT[:, b * 256 + kb * 128:b * 256 + (kb + 1) * 128], Vb[:, 2 * b + kb, :], None, 'kTP', 'VbP') for kb in range(2)]
                for qb in range(2):
                    add_qblock(b * 256 + qb * 128, kbl, ['qT%dP' % c for c in range(4)], ['hT%dP' % c for c in range(4)])
            def att_warm():
                for i in range(FILL_ATT):
                    MM(ps[:, 3, :], ones[:], maskb[:, 0, :], True, True, ['ones', 'maskb'], ['pb3'])

            att_stages = [[att_S(u, n), att_PV(u, n), None, att_NORM(u, n)] for n, u in enumerate(aunits)]
            s0_ = att_stages[0][0]
            att_stages[0][0] = (lambda: (s0_(), att_warm()))
            run_pipeline(att_stages, reverse=False)
            if stop == 'B':
                raise _Stop()

            filler(FILL_BC)

            ios = []
            for half in range(2):
                c_a, c_b = half * 512, (half + 1) * 512
                ios.append(ring_fill([
                    (lambda r: v3(r, 8)[0:64, 0:4, :], w_out[l, 0:256, c_a:c_b].rearrange("(c p) n -> p c n", p=64)),
                    (lambda r: v3(r, 8)[64:128, 0:4, :], w_out[l, 256:512, c_a:c_b].rearrange("(c p) n -> p c n", p=64)),
                    (lambda r: v3(r, 8)[:, 4:8, :], w_out[l, 512:1024, c_a:c_b].rearrange("(c p) n -> p c n", p=128)),
                ]))
            wos = [v3(ring[i], 8) for i in ios]
            proj_ln(l, 1, False, lambda dc: [wos[dc // 4][:, k, (dc % 4) * 128:(dc % 4 + 1) * 128] for k in range(8)], 8,
                    mixrhs, lambda sgn: mixres[sgn], ['ring%d' % i for i in ios], K_G1P, 'cfG1_%d' % par)
            if stop == 'C':
                raise _Stop()
            P.transfer(MIX_RES, FFN_RES)

            nmod_done = [0]

            def next_mod_piece():
                if (not lastl) and nmod_done[0] < 12:
                    mods_for_layer(l + 1, [nmod_done[0]])
                    nmod_done[0] += 1

            def cw(tap, ch):
                cc = V_CW + (l * 3 + tap) * 44 + ch
                return vecs[:, cc:cc + 1]

            def cb(ch):
                cc = V_CB + l * 44 + ch
                return vecs[:, cc:cc + 1]

            tb_ctr = [0]
            j0 = 0
            for hf in range(2):
                npairs = HALF_PAIRS[hf]
                for jp in range(npairs // 2):
                    jA = j0 + 2 * jp
                    iu = ring_fill([
                        (lambda r: v3(r, 8)[:, :, 0:256], w_up[l, :, jA * 128:(jA + 2) * 128].rearrange("(k p) n -> p k n", p=128)),
                        (lambda r: v3(r, 8)[:, :, 256:512], w_up[l, :, 2816 + jA * 128:2816 + (jA + 2) * 128].rearrange("(k p) n -> p k n", p=128)),
                    ])
                    wu = v3(ring[iu], 8)
                    first_slot = (hf == 0 and jp == 0)
                    last_slot = (jp == npairs // 2 - 1)
                    order = [(0, 'S'), (1, 'S'), (0, 'P'), (1, 'P')] if (first_slot or last_slot) else [(0, 'P'), (0, 'S'), (1, 'P'), (1, 'S')]
                    for oi, (jj, sgn) in enumerate(order):
                        j = jA + jj
                        jm = j - j0
                        la = [wu[:, k, jj * 128:(jj + 1) * 128] for k in range(8)]
                        lg = [wu[:, k, 256 + jj * 128:256 + (jj + 1) * 128] for k in range(8)]
                        for sgn in (sgn,):
                            c0, n, nseq, L, _ = _seg(sgn)
                            tbi = tb_ctr[0] % 2
                            tb_ctr[0] += 1
                            rr = ['ring%d' % iu] + hres[sgn]
                            if sgn == 'P':
                                s = next_slot()
                                for k in range(8):
                                    MM(ps[:, 2 * s, :], la[k], hT[:, k, 0:512], k == 0, k == 7, rr, ['pb%d' % (2 * s)])
                                for k in range(8):
                                    MM(ps[:, 2 * s + 1, :], lg[k], hT[:, k, 0:512], k == 0, k == 7, rr, ['pb%d' % (2 * s + 1)])
                                srcs = [(ps[:, 2 * s, :], ['pb%d' % (2 * s)], ta[tbi], 'ta%d' % tbi, j), (ps[:, 2 * s + 1, :], ['pb%d' % (2 * s + 1)], tg[tbi], 'tg%d' % tbi, 22 + j)]
                            else:
                                s = next_slot()
                                mm_fm(s, 'S', la, hrhs, rr)
                                s2 = next_slot()
                                mm_fm(s2, 'S', lg, hrhs, rr)
                                srcs = [(slot_flat(s), slot_res(s), ta[tbi], 'ta%d' % tbi, j), (slot_flat(s2), slot_res(s2), tg[tbi], 'tg%d' % tbi, 22 + j)]
                            for (src, sres, tbuf, tname, ch) in srcs:
                                sv = src.rearrange("p (a b) -> p a b", a=nseq)
                                tv = tbuf[:, 0:n].rearrange("p (a b) -> p a b", a=nseq)
                                ACT(tv, sv, AF.Identity, sres + ['vecs'], [tname], bias=cb(ch), scale=cw(1, ch))
                                STT(tv[:, :, 1:L], sv[:, :, 0:L - 1], cw(0, ch), tv[:, :, 1:L], ALU.mult, ALU.add, sres + ['vecs', tname], [tname])
                                STT(tv[:, :, 0:L - 1], sv[:, :, 1:L], cw(2, ch), tv[:, :, 0:L - 1], ALU.mult, ALU.add, sres + ['vecs', tname], [tname])
                            ACT(tg[tbi][:, 0:n], tg[tbi][:, 0:n], AF.Silu, ['tg%d' % tbi], ['tg%d' % tbi])
                            TT(mT[:, jm, c0:c0 + n], ta[tbi][:, 0:n], tg[tbi][:, 0:n], ALU.mult, ['ta%d' % tbi, 'tg%d' % tbi], ['mT%d%s' % (jm, sgn)])
                    next_mod_piece()
                if hf == 0:
                    for dcp in range(4):
                        idn = ring_fill([(lambda r: r[:, 0:npairs * 256].rearrange("p (a b) -> p a b", a=npairs),
                                          w_down[l, j0 * 128:(j0 + npairs) * 128, dcp * 256:(dcp + 1) * 256].rearrange("(k p) n -> p k n", p=128))])
                        wd = ring[idn][:, 0:npairs * 256].rearrange("p (a b) -> p a b", a=npairs)
                        for d2, sgn in ((0, 'S'), (1, 'S'), (0, 'P'), (1, 'P')):
                            dc = dcp * 2 + d2
                            lhs = [wd[:, k, d2 * 128:(d2 + 1) * 128] for k in range(npairs)]
                            for sgn in (sgn,):
                                c0, n, _, _, cond = _seg(sgn)
                                s = next_slot()
                                mm_fm(s, sgn, lhs, lambda k, a, b: mT[:, k, a:b], ['ring%d' % idn] + ['mT%d%s' % (k, sgn) for k in range(npairs)])
                                xv = xT[:, dc, c0:c0 + n]
                                STT(xv, slot_flat(s)[:, 0:n], cf(par, K_G2P, cond, dc), xv, ALU.mult, ALU.add, slot_res(s) + ['cfG2_%d' % par, 'xT%d%s' % (dc, sgn)], ['xT%d%s' % (dc, sgn)])
                    next_mod_piece()
                else:
                    while (not lastl) and nmod_done[0] < 12:
                        next_mod_piece()
                    if not lastl:
                        coef_next(l)
                    idns = []
                    for dq in range(2):
                        idns.append(ring_fill([(lambda r: r[:, 0:npairs * 512].rearrange("p (a b) -> p a b", a=npairs),
                                                w_down[l, j0 * 128:(j0 + npairs) * 128, dq * 512:(dq + 1) * 512].rearrange("(k p) n -> p k n", p=128))]))
                    wds = [ring[i][:, 0:npairs * 512].rearrange("p (a b) -> p a b", a=npairs) for i in idns]
                    proj_ln(l, 2, lastl, lambda dc: [wds[dc // 4][:, k, (dc % 4) * 128:(dc % 4 + 1) * 128] for k in range(npairs)], npairs,
                            lambda k, a, b: mT[:, k, a:b], lambda sgn: ['mT%d%s' % (k, sgn) for k in range(npairs)], ['ring%d' % i for i in idns], K_G2P, 'cfG2_%d' % par, nfill=0)
                j0 += npairs
            P.transfer(FFN_RES, IO_RES if lastl else MIX_RES)
          except _Stop:
            P.transfer(MIX_RES + FFN_RES, IO_RES)
            break

        for tb in (range(12) if (stop is not None or depth == 0) else []):
            b = tb % 2
            sgn = 'P' if tb < 4 else 'S'
            s = next_slot()
            for c in range(8):
                TR(slot_flat(s)[:, c * 128:(c + 1) * 128], xT[:, c, tb * 128:(tb + 1) * 128], ident32[:], ['xT%d%s' % (c, sgn), 'ident32'], slot_res(s))
            if tb % 2 == 0:
                ACP(xs[b], slot_flat(s), slot_res(s), ['xs%d' % b])
            else:
                VCP(xs[b], slot_flat(s), slot_res(s), ['xs%d' % b])
            DMA('sp', y_out[tb * 128:(tb + 1) * 128, :], xs[b], ['xs%d' % b], ['y_out%d' % b], 'yo%d' % b)
        P.add('sp', None, (), None, ['y_out0', 'y_out1', 'nk_out', 'nv_out'], [])
        P.emit(nc, st)
        nc._prog_stats = (len(P.ops), P.n_waits)
    return nc


def _constants():
    half = 32
    inv = (10000.0 ** (-np.arange(0, half, 2, dtype=np.float32) / half)).astype(np.float32)
    t = np.arange(1024)
    row = (t // 64).astype(np.float32)
    col = (t % 64).astype(np.float32)
    C = np.zeros((128, 1024), np.float32)
    S = np.zeros((128, 1024), np.float32)
    for p in range(128):
        d = p % 64
        pos = row if d < 32 else col
        f = inv[(d % 32) % 16]
        ang = (pos * f).astype(np.float32)
        C[p] = np.cos(ang)
        S[p] = np.sin(ang)
    R = np.zeros((128, 128), np.float32)
    for m in range(128):
        if (m % 32) < 16:
            R[m + 16, m] = -1.0
        else:
            R[m - 16, m] = 1.0
    ident = np.eye(128, dtype=np.float32)
    cc = np.arange(128)[:, None]
    rr = np.arange(128)[None, :]
    mA = np.where(cc >= rr, 0.0, NEG).astype(np.float32)
    mB = np.where(cc <= rr, 0.0, NEG).astype(np.float32)
    masks = np.concatenate([np.tile(mA, (1, 4)), np.tile(mB, (1, 4))], axis=1)
    ecf = np.ones((128, 256), np.float32)
    for g, w in enumerate((2, 4, 8, 16)):
        hw = w // 2
        for si in range(2):
            base = (g * 2 + si) * 32
            for sq in range(2):
                for i in range(hw):
                    ecf[:, base + sq * 8 + i] = w / float(i + hw)
                for i in range(hw - 1):
                    ecf[:, base + 16 + sq * 8 + i] = w / float(2 * hw - 1 - i)
    return C, S, R, ident, masks, ecf


def _run(inputs, depth=DEPTH, trace=False, stop=None):
    f = lambda a: np.ascontiguousarray(np.asarray(a, dtype=np.float32))
    x_prompt, x_sample = f(inputs['x_prompt']), f(inputs['x_sample'])
    cache_k, cache_v = f(inputs['cache_k']), f(inputs['cache_v'])
    c, c_ctx = f(inputs['c']), f(inputs['c_ctx'])
    C, S, R, ident, masks, ecf = _constants()
    common = np.concatenate([
        f(inputs['b_mod']).reshape(-1, 128), f(inputs['pool_scale']).reshape(-1, 128),
        f(inputs['ln1_g']).reshape(-1, 128), f(inputs['ln1_b']).reshape(-1, 128),
        f(inputs['ln2_g']).reshape(-1, 128), f(inputs['ln2_b']).reshape(-1, 128),
        f(inputs['conv_w']).reshape(-1, 128), f(inputs['conv_b']).reshape(-1, 128),
        c_ctx.reshape(8, 128)], axis=0)
    assert common.shape[0] == V_COND + 8
    shared = {
        'ropeC': C, 'ropeS': S, 'rperm': R, 'ident': ident, 'masks': masks, 'ecf': ecf,
        'sink': f(inputs['attn_sink']),
        'w_mod': f(inputs['w_mod']), 'w_in': f(inputs['w_in']), 'w_pool': f(inputs['w_pool']),
        'w_out': f(inputs['w_out']), 'w_up': f(inputs['w_up']), 'w_down': f(inputs['w_down']),
    }
    in_maps = []
    for core in range(8):
        vp = np.zeros((V_ROWS, 128), np.float32)
        vp[:common.shape[0]] = common
        vp[V_COND + 8:V_COND + 16] = c[core].reshape(8, 128)
        m = dict(shared)
        m['x_in'] = np.ascontiguousarray(np.concatenate([x_prompt[2 * core], x_prompt[2 * core + 1], x_sample[core]], axis=0))
        m['ck'] = np.ascontiguousarray(cache_k[core].reshape(DEPTH, 256, 128))
        m['cv'] = np.ascontiguousarray(cache_v[core].reshape(DEPTH, 256, 128))
        m['vecpack'] = vp
        in_maps.append(m)
    nc = build_program(depth, stop)
    res = run_bass_kernel_spmd(nc, in_maps, core_ids=list(range(8)), trace=trace)
    y_prompt = np.zeros((16, 256, D), np.float32)
    y_sample = np.zeros((8, 1024, D), np.float32)
    nk = np.zeros((16, DEPTH, 256, 2, 64), np.float32)
    nv = np.zeros((16, DEPTH, 256, 2, 64), np.float32)
    for core in range(8):
        r = res.results[core]
        y = np.asarray(r['y'])
        y_prompt[2 * core] = y[0:256]
        y_prompt[2 * core + 1] = y[256:512]
        y_sample[core] = y[512:1536]
        nk[2 * core:2 * core + 2] = np.asarray(r['nk']).reshape(2, DEPTH, 256, 2, 64)
        nv[2 * core:2 * core + 2] = np.asarray(r['nv']).reshape(2, DEPTH, 256, 2, 64)
    return (y_prompt, y_sample, nk, nv), res


def kernel(**inputs):
    outs, _ = _run(inputs)
    return outs
```

```python
from contextlib import ExitStack
import numpy as np
import concourse.bass as bass
import concourse.mybir as mybir
from concourse.bass_utils import run_bass_kernel_spmd

F32 = mybir.dt.float32
BF16 = mybir.dt.bfloat16
AF = mybir.ActivationFunctionType
ALU = mybir.AluOpType

DEPTH = 4
D = 1024
NTOK = 1536
ALPHA = (2 * DEPTH) ** 0.25
EPSP = 1e-5 / (ALPHA * ALPHA)
SCALE = 64 ** -0.5
NEG = -30000.0
NS = 5
HALF_PAIRS = (14, 8)
FILL_AB, FILL_BC, FILL_CD, FILL_EA = 16, 10, 30, 0
FILL_ATT = 12
FILL_LN = 16

V_BMOD = 0
V_PSC = 192
V_LN1G = 208
V_LN1B = 240
V_LN2G = 272
V_LN2B = 304
V_CW = 336
V_CB = 864
V_COND = 1040
V_ROWS = 1152


class _Op:
    __slots__ = ('eng', 'meth', 'args', 'kw', 'deps', 'signal', 'sig_val', 'dma_key', 'dma_val', 'dma_waits', 'idx')


class Prog:
    def __init__(self):
        self.ops = []
        self.last_w = {}
        self.readers = {}
        self.dma_count = {}

    def transfer(self, old, new):
        comb = {}
        dl = []
        for r in old:
            for w in self.last_w.get(r, ()):
                if w.dma_key is not None:
                    dl.append(w)
                elif w.eng not in comb or comb[w.eng].idx < w.idx:
                    comb[w.eng] = w
            for k, v in self.readers.get(r, {}).items():
                if k == '_dma':
                    dl.extend(v)
                elif k not in comb or comb[k].idx < v.idx:
                    comb[k] = v
        for n in new:
            self.last_w[n] = []
            rd = dict(comb)
            if dl:
                rd['_dma'] = list(dl)
            self.readers[n] = rd

    def add(self, eng, meth, args=(), kw=None, reads=(), writes=(), dma_key=None):
        op = _Op()
        op.eng = eng
        op.meth = meth
        op.args = args
        op.kw = kw or {}
        op.dma_key = dma_key
        op.deps = []
        op.signal = False
        op.sig_val = None
        op.dma_waits = {}
        op.idx = len(self.ops)
        deps = {}

        def adddep(d):
            if d is None or d is op:
                return
            if eng == 'pe' and d.eng == 'pe' and d.dma_key is None:
                return
            deps[d.idx] = d

        for r in reads:
            for w in self.last_w.get(r, ()):
                adddep(w)
            if r.startswith('pb'):
                for k, v in self.readers.get(r, {}).items():
                    if k != eng and k != '_dma':
                        adddep(v)
        for w_ in writes:
            for w in self.last_w.get(w_, ()):
                adddep(w)
            rd = self.readers.get(w_)
            if rd:
                for k, v in rd.items():
                    if k == '_dma':
                        for x in v:
                            adddep(x)
                    else:
                        adddep(v)
        for d in deps.values():
            if d.dma_key is not None:
                k = d.dma_key
                op.dma_waits[k] = max(op.dma_waits.get(k, 0), self.dma_count[k])
            else:
                op.deps.append(d)
        for r in reads:
            rd = self.readers.setdefault(r, {})
            if dma_key is not None:
                rd.setdefault('_dma', []).append(op)
            else:
                rd[eng] = op
        for w_ in writes:
            self.last_w[w_] = [op]
            self.readers[w_] = {}
        if dma_key is not None:
            self.dma_count[dma_key] = self.dma_count.get(dma_key, 0) + 16
            op.dma_val = self.dma_count[dma_key]
        self.ops.append(op)
        return op

    def emit(self, nc, stack):
        for op in self.ops:
            for d in op.deps:
                d.signal = True
        cnt = {}
        for op in self.ops:
            if op.dma_key is None and op.signal:
                cnt[op.eng] = cnt.get(op.eng, 0) + 1
                op.sig_val = cnt[op.eng]
        esem = {}
        for e in ('pe', 'act', 'dve', 'pool', 'sp'):
            esem[e] = stack.enter_context(nc.semaphore("s_" + e))
        dsem = {}
        for k in self.dma_count:
            dsem[k] = stack.enter_context(nc.semaphore("d_%s" % k))
        byeng = {}
        for op in self.ops:
            byeng.setdefault(op.eng, []).append(op)
        block = stack.enter_context(nc.Block())
        self.n_waits = 0

        def run_engine(ename, eobj):
            waited = {}
            for op in byeng.get(ename, []):
                need = {}
                for d in op.deps:
                    key = ('e', d.eng)
                    need[key] = (esem[d.eng], max(need.get(key, (None, 0))[1], d.sig_val))
                for k, v in op.dma_waits.items():
                    key = ('d', k)
                    need[key] = (dsem[k], max(need.get(key, (None, 0))[1], v))
                for key, (s, v) in need.items():
                    if waited.get(key, 0) >= v:
                        continue
                    eobj.wait_ge(s, v)
                    self.n_waits += 1
                    waited[key] = v
                if op.meth is None:
                    continue
                ins = getattr(eobj, op.meth)(*op.args, **op.kw)
                if op.dma_key is not None:
                    ins.then_inc(dsem[op.dma_key], 16)
                elif op.signal:
                    ins.then_inc(esem[op.eng], 1)

        @block.tensor
        def _(e):
            run_engine('pe', e)

        @block.scalar
        def _(e):
            run_engine('act', e)

        @block.vector
        def _(e):
            run_engine('dve', e)

        @block.gpsimd
        def _(e):
            run_engine('pool', e)

        @block.sync
        def _(e):
            run_engine('sp', e)


def _seg(name):
    return (0, 512, 2, 256, 0) if name == 'P' else (512, 1024, 1, 1024, 1)


class _Stop(Exception):
    pass


def build_program(depth=DEPTH, stop=None):
    nc = bass.Bass("TRN2", target_bir_lowering=False)
    dt = nc.dram_tensor
    x_in = dt("x_in", [NTOK, D], F32, kind="ExternalInput").ap()
    ck = dt("ck", [DEPTH, 256, 128], F32, kind="ExternalInput").ap()
    cv = dt("cv", [DEPTH, 256, 128], F32, kind="ExternalInput").ap()
    vecpack = dt("vecpack", [V_ROWS, 128], F32, kind="ExternalInput").ap()
    sink_d = dt("sink", [DEPTH, 8], F32, kind="ExternalInput").ap()
    ropeC_d = dt("ropeC", [128, 1024], F32, kind="ExternalInput").ap()
    ropeS_d = dt("ropeS", [128, 1024], F32, kind="ExternalInput").ap()
    rperm_d = dt("rperm", [128, 128], F32, kind="ExternalInput").ap()
    ident_d = dt("ident", [128, 128], F32, kind="ExternalInput").ap()
    masks_d = dt("masks", [128, 1024], F32, kind="ExternalInput").ap()
    ecf_d = dt("ecf", [128, 256], F32, kind="ExternalInput").ap()
    w_mod = dt("w_mod", [DEPTH, D, 6 * D], F32, kind="ExternalInput").ap()
    w_in = dt("w_in", [DEPTH, D, 1280], F32, kind="ExternalInput").ap()
    w_pool = dt("w_pool", [DEPTH, 4, 128, 128], F32, kind="ExternalInput").ap()
    w_out = dt("w_out", [DEPTH, D, D], F32, kind="ExternalInput").ap()
    w_up = dt("w_up", [DEPTH, D, 5632], F32, kind="ExternalInput").ap()
    w_down = dt("w_down", [DEPTH, 2816, D], F32, kind="ExternalInput").ap()
    y_out = dt("y", [NTOK, D], F32, kind="ExternalOutput").ap()
    nk_out = dt("nk", [2, DEPTH, 256, 128], F32, kind="ExternalOutput").ap()
    nv_out = dt("nv", [2, DEPTH, 256, 128], F32, kind="ExternalOutput").ap()

    P = Prog()
    st = ExitStack()
    with st:
        def sb(n, s, d):
            return st.enter_context(nc.sbuf_tensor(n, s, d))

        xT = sb("xT", [128, 8, NTOK], F32)
        hT = sb("hT", [128, 8, NTOK], BF16)
        vecs = sb("vecs", [128, V_ROWS], F32)
        ropeC = sb("ropeCs", [128, 1024], F32)
        ropeS = sb("ropeSs", [128, 1024], F32)
        rperm = sb("rperms", [128, 128], F32)
        ident32 = sb("ident32", [128, 128], F32)
        identb = sb("identb", [128, 128], BF16)
        ones = sb("ones", [128, 128], BF16)
        maskb = sb("maskb", [128, 2, 512], BF16)
        ecf = sb("ecfs", [128, 256], F32)
        sink8 = sb("sink8", [1, 8], F32)
        sinkf8 = sb("sinkf8", [1, 8], F32)
        sinkh8 = sb("sinkh8", [1, 8], BF16)
        sinkl8 = sb("sinkl8", [1, 8], BF16)
        sinkhl = sb("sinkhl", [33, 1024], BF16)
        sinkL = [sb("sinkL%d" % i, [33, 128], BF16) for i in range(2)]
        scT = sb("scT", [128, 8, 2], BF16)
        modsT = [sb("modsT%d" % i, [128, 2, 48], F32) for i in range(2)]
        coef = sb("coef", [128, 2, 12, 2, 8], F32)
        ring = [sb("ring%d" % i, [128, 4096], BF16) for i in range(NS)]
        wpool = [sb("wpool%d" % i, [128, 4, 128], BF16) for i in range(2)]
        lnm = sb("lnm", [128, 512], F32)
        lnq = sb("lnq", [128, 512], F32)
        ybf = [sb("ybf%d" % i, [128, 512], BF16) for i in range(2)]
        ysq = [sb("ysq%d" % i, [128, 512], BF16) for i in range(2)]
        UW = 16128
        U = sb("U", [128, UW], F32)
        ps = st.enter_context(nc.psum_tensor("ps", [128, 8, 512], F32))

        cur = [0]

        def carve(nwords, dtype, shape=()):
            a = cur[0]
            cur[0] += nwords
            assert cur[0] <= UW, cur[0]
            v = U[:, a:a + nwords]
            if dtype == BF16:
                v = v.bitcast(BF16)
            if len(shape) == 2:
                return v.rearrange("p (a b) -> p a b", a=shape[0])
            if len(shape) == 3:
                return v.rearrange("p (a b c) -> p a b c", a=shape[0], b=shape[1])
            return v

        qT = carve(3072, BF16, (4, NTOK))
        kT = carve(768, BF16)
        Vb = carve(1536, BF16, (12, 256))
        poolT = carve(3072, BF16, (4, NTOK))
        kcT = carve(128, BF16)
        vcb = carve(256, BF16, (2, 256))
        p32 = carve(1040, F32)
        sA = carve(1040, F32)
        sB = carve(1040, F32)
        dTt = [carve(512, BF16) for _ in range(2)]
        q32 = [carve(512, F32) for _ in range(2)]
        t1 = carve(512, F32)
        PT = [carve(512, BF16, (2, 512)) for _ in range(3)]
        kvst = sA[:, 0:1024].rearrange("p (a b) -> p a b", a=4)
        cstage = sB[:, 0:512].rearrange("p (a b c) -> p a b c", a=2, b=2)
        rcp = [lnm, lnq]
        rcpn = ['lnm', 'lnq']
        cur[0] = 0
        mT = carve(14 * 768, BF16, (14, NTOK))
        ta = [carve(1024, F32) for _ in range(2)]
        tg = [carve(1024, F32) for _ in range(2)]
        cur[0] = 0
        xs = [carve(1024, F32) for _ in range(6)]

        MIX_RES = (['qT%d%s' % (c, s) for c in range(4) for s in 'PS'] + ['poolT%d%s' % (c, s) for c in range(4) for s in 'PS'] +
                   ['kTP', 'kTS', 'VbP', 'VbS', 'kcT', 'vcb', 'p32', 'sA', 'sB', 'dT0', 'dT1', 'q320', 'q321', 't1',
                    'PT0', 'PT1', 'PT2'])
        FFN_RES = ['mT%d%s' % (j, s) for j in range(14) for s in 'PS'] + ['ta0', 'ta1', 'tg0', 'tg1']
        IO_RES = ['xs%d' % i for i in range(6)]

        def MM(out, lhsT, rhs, start, stop, reads, writes):
            P.add('pe', 'matmul', (out, lhsT, rhs), dict(start=start, stop=stop), reads, writes)

        def TR(out, in_, ident, reads, writes):
            P.add('pe', 'transpose', (out, in_, ident), None, reads, writes)

        def ACT(out, in_, func, reads, writes, bias=None, scale=None):
            kw = {}
            if bias is not None:
                kw['bias'] = bias
            if scale is not None:
                kw['scale'] = scale
            P.add('act', 'activation', (out, in_, func), kw, reads, writes)

        def ACP(out, in_, reads, writes):
            P.add('act', 'copy', (out, in_), None, reads, writes)

        def VCP(out, in_, reads, writes):
            P.add('dve', 'tensor_copy', (out, in_), None, reads, writes)

        def TT(out, in0, in1, op, reads, writes):
            P.add('dve', 'tensor_tensor', (), dict(out=out, in0=in0, in1=in1, op=op), reads, writes)

        def TS(out, in0, s1, s2, op0, op1, reads, writes):
            kw = dict(out=out, in0=in0, scalar1=s1, scalar2=s2, op0=op0)
            if op1 is not None:
                kw['op1'] = op1
            P.add('dve', 'tensor_scalar', (), kw, reads, writes)

        def STT(out, in0, scalar, in1, op0, op1, reads, writes):
            P.add('dve', 'scalar_tensor_tensor', (), dict(out=out, in0=in0, scalar=scalar, in1=in1, op0=op0, op1=op1), reads, writes)

        def DMA(q, out, in_, reads, writes, key):
            P.add(q, 'dma_start', (), dict(out=out, in_=in_), reads, writes, dma_key=key)

        slot_ctr = [0]
        reserved = set()

        def _bank_recency(bk):
            r = 'pb%d' % bk
            m = -1
            for w in P.last_w.get(r, ()):
                m = max(m, w.idx)
            for k, v in P.readers.get(r, {}).items():
                if k == '_dma':
                    for x in v:
                        m = max(m, x.idx)
                else:
                    m = max(m, v.idx)
            return m

        def next_slot():
            best, bm = None, None
            for s in range(4):
                if s in reserved:
                    continue
                m = max(_bank_recency(2 * s), _bank_recency(2 * s + 1))
                if bm is None or m < bm:
                    best, bm = s, m
            return best

        def slot_res(s):
            return ['pb%d' % (2 * s), 'pb%d' % (2 * s + 1)]

        def slot_flat(s):
            return ps[:, 2 * s:2 * s + 2, :].rearrange("p a b -> p (a b)")

        ring_ctr = [0]

        def v3(t, a):
            return t[:].rearrange("p (a b) -> p a b", a=a)

        pinned = set()

        def ring_fill(dmas):
            while True:
                i = ring_ctr[0] % NS
                ring_ctr[0] += 1
                if i not in pinned:
                    break
            for dst_fn, src in dmas:
                DMA('pool', dst_fn(ring[i]), src, [], ['ring%d' % i], 'ring%d' % i)
            return i

        DMA('sp', ident32[:], ident_d, [], ['ident32'], 'c0')
        DMA('sp', rperm[:], rperm_d, [], ['rperm'], 'c0')
        DMA('sp', ecf[:], ecf_d, [], ['ecf'], 'c0')
        DMA('sp', ropeC[:], ropeC_d, [], ['ropeC'], 'c0')
        DMA('sp', ropeS[:], ropeS_d, [], ['ropeS'], 'c0')
        DMA('pool', identb[:], ident_d, [], ['identb'], 'c1')
        DMA('pool', maskb[:].rearrange("p a b -> p (a b)"), masks_d, [], ['maskb'], 'c1')
        P.add('dve', 'memset', (ones[:], 1.0), None, [], ['ones'])
        P.add('dve', 'memset', (sinkhl[:], 0.0), None, [], ['sinkhl'])
        for kvh in range(2):
            P.add('dve', 'memset', (sinkL[kvh][:], 0.0), None, [], ['sinkL'])
            P.add('dve', 'memset', (sinkL[kvh][:, (1 - kvh) * 64:(2 - kvh) * 64], 1.0), None, ['sinkL'], ['sinkL'])
        DMA('sp', xs[4].rearrange("p (t c) -> p t c", t=8), vecpack[0:1024, :].rearrange("(t p) c -> p t c", p=128), [], ['xs4'], 'xs4')
        DMA('sp', xs[5][:, 0:128], vecpack[1024:1152, :], [], ['xs5'], 'xs5')
        for t in range(9):
            src = xs[4][:, t * 128:(t + 1) * 128] if t < 8 else xs[5][:, 0:128]
            sn = 'xs4' if t < 8 else 'xs5'
            s = next_slot()
            TR(ps[:, 2 * s, 0:128], src, ident32[:], [sn, 'ident32'], slot_res(s))
            ACP(vecs[:, t * 128:(t + 1) * 128], ps[:, 2 * s, 0:128], slot_res(s), ['vecs'])
        for c in range(2):
            ACT(scT[:, :, c], vecs[:, V_COND + 8 * c:V_COND + 8 * c + 8], AF.Silu, ['vecs'], ['scT'])

        for ti_, tb in enumerate(list(range(4, 12)) + list(range(4))):
            b = ti_ % 4
            DMA('sp', xs[b], x_in[tb * 128:(tb + 1) * 128, :], [], ['xs%d' % b], 'xs%d' % b)
            s = next_slot()
            for c in range(8):
                TR(slot_flat(s)[:, c * 128:(c + 1) * 128], xs[b][:, c * 128:(c + 1) * 128], ident32[:], ['xs%d' % b, 'ident32'], slot_res(s))
            sg = 'P' if tb < 4 else 'S'
            ACP(xT[:, :, tb * 128:(tb + 1) * 128], slot_flat(s).rearrange("p (c t) -> p c t", c=8), slot_res(s), ['xT%d%s' % (c, sg) for c in range(8)])

        def cf(par, kind, cond, c):
            return coef[:, par, kind, cond, c:c + 1]

        K_A1, K_B1, K_G1P, K_G2P, K_A2, K_B2, K_OP1, K_OP2 = range(8)

        def mods_for_layer(l, pieces):
            par = l % 2
            for pc in pieces:
                i = ring_fill([(lambda r: v3(r, 8), w_mod[l, :, pc * 512:(pc + 1) * 512].rearrange("(k p) n -> p k n", p=128))])
                s = next_slot()
                wv = v3(ring[i], 8)
                for mm in range(4):
                    for kc in range(8):
                        MM(ps[:, 2 * s, mm * 2:mm * 2 + 2], wv[:, kc, mm * 128:(mm + 1) * 128], scT[:, kc, :], kc == 0, kc == 7, ['ring%d' % i, 'scT'], slot_res(s))
                for c in range(2):
                    bc = V_BMOD + l * 48 + pc * 4
                    TT(modsT[par][:, c, pc * 4:pc * 4 + 4], ps[:, 2 * s, c:8:2], vecs[:, bc:bc + 4], ALU.add, slot_res(s) + ['vecs'], ['mods%d_%d' % (par, pc)])

        def mres(par, j):
            return ['mods%d_%d' % (par, 2 * j), 'mods%d_%d' % (par, 2 * j + 1)]

        def coef_layer_start(l):
            par = l % 2
            for c in range(2):
                TS(coef[:, par, K_A1, c, :], modsT[par][:, c, 8:16], 1.0, None, ALU.add, None, mres(par, 1), ['cfA1_%d' % par])
                VCP(coef[:, par, K_B1, c, :], modsT[par][:, c, 0:8], mres(par, 0), ['cfB1_%d' % par])

        def coef_mid(l):
            par = l % 2
            g1 = vecs[:, V_LN1G + l * 8:V_LN1G + l * 8 + 8]
            b1 = vecs[:, V_LN1B + l * 8:V_LN1B + l * 8 + 8]
            for c in range(2):
                TS(coef[:, par, K_G1P, c, :], modsT[par][:, c, 16:24], 1.0 / ALPHA, None, ALU.mult, None, mres(par, 2), ['cfG1_%d' % par])
                TS(coef[:, par, K_G2P, c, :], modsT[par][:, c, 40:48], 1.0 / ALPHA, None, ALU.mult, None, mres(par, 5), ['cfG2_%d' % par])
                TS(coef[:, par, K_OP2, c, :], modsT[par][:, c, 32:40], 1.0, None, ALU.add, None, mres(par, 4), ['cfO2_%d' % par])
                TT(coef[:, par, K_A2, c, :], coef[:, par, K_OP2, c, :], g1, ALU.mult, ['cfO2_%d' % par, 'vecs'], ['cfA2_%d' % par])
                TT(coef[:, par, K_B2, c, :], coef[:, par, K_OP2, c, :], b1, ALU.mult, ['cfO2_%d' % par, 'vecs'], ['cfB2_%d' % par])
                TT(coef[:, par, K_B2, c, :], coef[:, par, K_B2, c, :], modsT[par][:, c, 24:32], ALU.add, ['cfB2_%d' % par] + mres(par, 3), ['cfB2_%d' % par])

        def coef_next(l):
            par = (l + 1) % 2
            g2 = vecs[:, V_LN2G + l * 8:V_LN2G + l * 8 + 8]
            b2 = vecs[:, V_LN2B + l * 8:V_LN2B + l * 8 + 8]
            for c in range(2):
                TS(coef[:, par, K_OP1, c, :], modsT[par][:, c, 8:16], 1.0, None, ALU.add, None, mres(par, 1), ['cfO1_%d' % par])
                TT(coef[:, par, K_A1, c, :], coef[:, par, K_OP1, c, :], g2, ALU.mult, ['cfO1_%d' % par, 'vecs'], ['cfA1_%d' % par])
                TT(coef[:, par, K_B1, c, :], coef[:, par, K_OP1, c, :], b2, ALU.mult, ['cfO1_%d' % par, 'vecs'], ['cfB1_%d' % par])
                TT(coef[:, par, K_B1, c, :], coef[:, par, K_B1, c, :], modsT[par][:, c, 0:8], ALU.add, ['cfB1_%d' % par] + mres(par, 0), ['cfB1_%d' % par])

        def mm_fm(s, sgn, lhs_list, rhs_fn, reads):
            c0, n, _, _, _ = _seg(sgn)
            nk = len(lhs_list)
            for k in range(nk):
                for h in range(n // 512):
                    MM(ps[:, 2 * s + h, :], lhs_list[k], rhs_fn(k, c0 + h * 512, c0 + (h + 1) * 512), k == 0, k == nk - 1, reads, slot_res(s))

        def layer_norm(l, which, last):
            par = l % 2
            for tb in range(3):
                sgn = 'P' if tb == 0 else 'S'
                cond = 0 if tb == 0 else 1
                c0 = tb * 512
                s = next_slot()
                xres = ['xT%d%s' % (c, sgn) for c in range(8)]
                for c in range(8):
                    b = c % 2
                    xv = xT[:, c, c0:c0 + 512]
                    ACP(ybf[b][:], xv, [xres[c]], ['ybf%d' % b])
                    ACT(ysq[b][:], xv, AF.Square, [xres[c]], ['ysq%d' % b])
                    MM(ps[:, 2 * s, :], ones[:], ybf[b][:], c == 0, c == 7, ['ones', 'ybf%d' % b], slot_res(s))
                    MM(ps[:, 2 * s + 1, :], ones[:], ysq[b][:], c == 0, c == 7, ['ones', 'ysq%d' % b], slot_res(s))
                ACT(lnm[:], ps[:, 2 * s, :], AF.Identity, slot_res(s), ['lnm'], scale=1.0 / D)
                TT(lnq[:], lnm[:], lnm[:], ALU.mult, ['lnm'], ['lnq'])
                STT(lnq[:], ps[:, 2 * s + 1, :], 1.0 / D, lnq[:], ALU.mult, ALU.subtract, slot_res(s) + ['lnq'], ['lnq'])
                TS(lnq[:], lnq[:], EPSP, None, ALU.add, None, ['lnq'], ['lnq'])
                ACT(lnq[:], lnq[:], AF.Ln, ['lnq'], ['lnq'])
                ACT(lnq[:], lnq[:], AF.Exp, ['lnq'], ['lnq'], scale=-0.5)
                if which == 1:
                    gcol, bcol = V_LN1G + l * 8, V_LN1B + l * 8
                    kA, kB, cpar = K_A2, K_B2, par
                    cres = ['cfA2_%d' % par, 'cfB2_%d' % par]
                else:
                    gcol, bcol = V_LN2G + l * 8, V_LN2B + l * 8
                    kA, kB, cpar = K_A1, K_B1, (l + 1) % 2
                    cres = ['cfA1_%d' % cpar, 'cfB1_%d' % cpar]
                for c in range(8):
                    xv = xT[:, c, c0:c0 + 512]
                    TT(xv, xv, lnm[:], ALU.subtract, [xres[c], 'lnm'], [xres[c]])
                    TT(xv, xv, lnq[:], ALU.mult, [xres[c], 'lnq'], [xres[c]])
                    if not last:
                        TS(hT[:, c, c0:c0 + 512], xv, cf(cpar, kA, cond, c), cf(cpar, kB, cond, c), ALU.mult, ALU.add, [xres[c]] + cres, ['hT%d%s' % (c, sgn)])
                    ACT(xv, xv, AF.Identity, [xres[c], 'vecs'], [xres[c]], bias=vecs[:, bcol + c:bcol + c + 1], scale=vecs[:, gcol + c:gcol + c + 1])

        def hrhs(k, a, b):
            return hT[:, k, a:b]

        def mixrhs(k, a, b):
            return hT[:, k, a:b] if k < 4 else poolT[:, k - 4, a:b]

        if stop == 'setup':
            depth = 0
        mods_for_layer(0, range(0, 4))
        coef_layer_start(0)
        for sgn in 'SP':
            c0, n, _, _, cond = _seg(sgn)
            for c in range(8):
                if c % 2 == 0:
                    ACT(hT[:, c, c0:c0 + n], xT[:, c, c0:c0 + n], AF.Identity, ['xT%d%s' % (c, sgn), 'cfA1_0', 'cfB1_0'], ['hT%d%s' % (c, sgn)],
                        bias=cf(0, K_B1, cond, c), scale=cf(0, K_A1, cond, c))
                else:
                    TS(hT[:, c, c0:c0 + n], xT[:, c, c0:c0 + n], cf(0, K_A1, cond, c), cf(0, K_B1, cond, c), ALU.mult, ALU.add,
                       ['xT%d%s' % (c, sgn), 'cfA1_0', 'cfB1_0'], ['hT%d%s' % (c, sgn)])
        P.transfer(IO_RES, MIX_RES)

        hres = {sgn: ['hT%d%s' % (c, sgn) for c in range(8)] for sgn in 'PS'}
        mixres = {sgn: ['hT%d%s' % (c, sgn) for c in range(4)] + ['poolT%d%s' % (c, sgn) for c in range(4)] for sgn in 'PS'}

        def run_pipeline(units, reverse=True):
            nst = max(len(u) for u in units)
            for step in range(len(units) + nst - 1):
                ks = range(nst - 1, -1, -1) if reverse else range(nst)
                for k in ks:
                    n = step - k
                    if 0 <= n < len(units) and k < len(units[n]) and units[n][k] is not None:
                        units[n][k]()

        pending = []

        def emit_pending(k):
            for _ in range(min(k, len(pending))):
                pending.pop(0)()

        def ln_tail_pieces(l, which, last, tb, sb0, sb1):
            par = l % 2
            sgn = 'P' if tb == 0 else 'S'
            cond = 0 if tb == 0 else 1
            c0 = tb * 512
            xres = ['xT%d%s' % (c, sgn) for c in range(8)]
            if which == 1:
                gcol, bcol = V_LN1G + l * 8, V_LN1B + l * 8
                kA, kB, cpar = K_A2, K_B2, par
                cres = ['cfA2_%d' % par, 'cfB2_%d' % par]
            else:
                gcol, bcol = V_LN2G + l * 8, V_LN2B + l * 8
                kA, kB, cpar = K_A1, K_B1, (l + 1) % 2
                cres = ['cfA1_%d' % cpar, 'cfB1_%d' % cpar]

            def head_a():
                ACT(lnm[:], ps[:, sb0, :], AF.Identity, ['pb%d' % sb0], ['lnm'], scale=1.0 / D)

            def head_b():
                TT(lnq[:], lnm[:], lnm[:], ALU.mult, ['lnm'], ['lnq'])
                STT(lnq[:], ps[:, sb1, :], 1.0 / D, lnq[:], ALU.mult, ALU.subtract, ['pb%d' % sb1, 'lnq'], ['lnq'])
                TS(lnq[:], lnq[:], EPSP, None, ALU.add, None, ['lnq'], ['lnq'])

            def head_c():
                ACT(lnq[:], lnq[:], AF.Ln, ['lnq'], ['lnq'])
                ACT(lnq[:], lnq[:], AF.Exp, ['lnq'], ['lnq'], scale=-0.5)

            def chunk(c):
                def f():
                    xv = xT[:, c, c0:c0 + 512]
                    TT(xv, xv, lnm[:], ALU.subtract, [xres[c], 'lnm'], [xres[c]])
                    TT(xv, xv, lnq[:], ALU.mult, [xres[c], 'lnq'], [xres[c]])
                    if not last:
                        if c % 2 == 0:
                            ACT(hT[:, c, c0:c0 + 512], xv, AF.Identity, [xres[c]] + cres, ['hT%d%s' % (c, sgn)], bias=cf(cpar, kB, cond, c), scale=cf(cpar, kA, cond, c))
                        else:
                            TS(hT[:, c, c0:c0 + 512], xv, cf(cpar, kA, cond, c), cf(cpar, kB, cond, c), ALU.mult, ALU.add, [xres[c]] + cres, ['hT%d%s' % (c, sgn)])
                    ACT(xv, xv, AF.Identity, [xres[c], 'vecs'], [xres[c]], bias=vecs[:, bcol + c:bcol + c + 1], scale=vecs[:, gcol + c:gcol + c + 1])
                return f
            return [head_a, head_b, head_c] + [chunk(c) for c in range(8)]

        def store_block(tb):
            sgn = 'P' if tb == 0 else 'S'
            for tbk in range(4 * tb, 4 * tb + 4):
                b = tbk % 2
                s = 2 + tbk % 2
                for c in range(8):
                    TR(slot_flat(s)[:, c * 128:(c + 1) * 128], xT[:, c, tbk * 128:(tbk + 1) * 128], ident32[:], ['xT%d%s' % (c, sgn), 'ident32'], slot_res(s))
                ACP(ta[b], slot_flat(s), slot_res(s), ['ta%d' % b])
                DMA('sp', y_out[tbk * 128:(tbk + 1) * 128, :], ta[b], ['ta%d' % b], ['y_out%d' % b], 'yo%d' % b)

        def filler(n):
            if n <= 0:
                return
            s = next_slot()
            for i in range(n):
                MM(ps[:, 2 * s, :], ones[:], maskb[:, 0, :], True, True, ['ones', 'maskb'], ['pb%d' % (2 * s)])

        def proj_ln(l, which, last, lhs_fn, nk, rhs_fn, rres_fn, wres, gkind, gres, nfill=0):
            par = l % 2
            reserved.update([0, 1, 2, 3])
            bctr = [0]
            prev_tb = None
            lo_rec = max(_bank_recency(bk_) for bk_ in range(0, 4))
            hi_rec = max(_bank_recency(bk_) for bk_ in range(4, 8))
            ubase, sbase = (4, 0) if hi_rec <= lo_rec or last else (0, 4)
            for ti, tb in enumerate((1, 2, 0)):
                sgn = 'P' if tb == 0 else 'S'
                cond = 0 if tb == 0 else 1
                c0 = tb * 512
                sb0, sb1 = (sbase, sbase + 1) if ti % 2 == 0 else (sbase + 2, sbase + 3)
                units = []
                for dc in range(8):
                    def s1(dc=dc, sgn=sgn, cond=cond, c0=c0):
                        bk = ubase + bctr[0] % 4
                        bctr[0] += 1
                        lhs = lhs_fn(dc)
                        for k in range(nk):
                            MM(ps[:, bk, :], lhs[k], rhs_fn(k, c0, c0 + 512), k == 0, k == nk - 1, wres + rres_fn(sgn), ['pb%d' % bk])
                        xv = xT[:, dc, c0:c0 + 512]
                        STT(xv, ps[:, bk, :], cf(par, gkind, cond, dc), xv, ALU.mult, ALU.add, ['pb%d' % bk, gres, 'xT%d%s' % (dc, sgn)], ['xT%d%s' % (dc, sgn)])
                        b = dc % 2
                        ACP(ybf[b][:], xv, ['xT%d%s' % (dc, sgn)], ['ybf%d' % b])
                        ACT(ysq[b][:], xv, AF.Square, ['xT%d%s' % (dc, sgn)], ['ysq%d' % b])
                        emit_pending((1, 1, 1, 2, 2, 2, 1, 1)[dc])

                    def s3(dc=dc, sb0=sb0, sb1=sb1):
                        b = dc % 2
                        MM(ps[:, sb0, :], ones[:], ybf[b][:], dc == 0, dc == 7, ['ones', 'ybf%d' % b], ['pb%d' % sb0])
                        MM(ps[:, sb1, :], ones[:], ysq[b][:], dc == 0, dc == 7, ['ones', 'ysq%d' % b], ['pb%d' % sb1])
                    units.append([s1, None, s3])
                run_pipeline(units)
                emit_pending(len(pending))
                if last and prev_tb is not None:
                    store_block(prev_tb)
                prev_tb = tb
                pending.extend(ln_tail_pieces(l, which, last, tb, sb0, sb1))
            reserved.difference_update([0, 1, 2, 3])
            filler(nfill)
            emit_pending(len(pending))
            if last:
                store_block(0)

        for l in range(depth):
          try:
            par = l % 2
            lastl = (l == depth - 1)
            DMA('sp', cstage[:, 0, :, :], ck[l].rearrange("(b p) d -> p b d", p=128), [], ['sB'], 'cst')
            DMA('sp', cstage[:, 1, :, :], cv[l].rearrange("(b p) d -> p b d", p=128), [], ['sB'], 'cst')
            DMA('sp', sink8[:], sink_d[l:l + 1, :], [], ['sink8'], 'snk')
            DMA('pool', wpool[par][:], w_pool[l].rearrange("g c d -> c g d"), [], ['wpool%d' % par], 'wpool%d' % par)
            ACT(sinkf8[:], sink8[:], AF.Exp, ['sink8'], ['sinkf8'])
            VCP(sinkh8[:], sinkf8[:], ['sinkf8'], ['sinkh8'])
            TT(sinkf8[:], sinkf8[:], sinkh8[:], ALU.subtract, ['sinkf8', 'sinkh8'], ['sinkf8'])
            VCP(sinkl8[:], sinkf8[:], ['sinkf8'], ['sinkl8'])
            VCP(sinkhl[0:1, :].rearrange("p (a b) -> p a b", a=8), sinkh8[0:1, :].unsqueeze(2).to_broadcast([1, 8, 128]), ['sinkh8', 'sinkhl'], ['sinkhl'])
            VCP(sinkhl[32:33, :].rearrange("p (a b) -> p a b", a=8), sinkl8[0:1, :].unsqueeze(2).to_broadcast([1, 8, 128]), ['sinkl8', 'sinkhl'], ['sinkhl'])
            P.add('dve', 'memset', (Vb[:, :, 64:192], 1.0), None, [], ['VbP', 'VbS'])
            P.add('dve', 'memset', (vcb[:, :, 64:192], 1.0), None, [], ['vcb'])
            s = next_slot()
            for b in range(2):
                TR(ps[:, 2 * s, b * 128:(b + 1) * 128], cstage[:, 0, b, :], ident32[:], ['sB', 'ident32'], slot_res(s))
            ACP(kcT, ps[:, 2 * s, 0:256], slot_res(s), ['kcT'])
            VCP(vcb[:, :, 0:64], cstage[:, 1, :, 0:64], ['sB', 'vcb'], ['vcb'])
            VCP(vcb[:, :, 192:256], cstage[:, 1, :, 64:128], ['sB', 'vcb'], ['vcb'])

            iq = ring_fill([
                ((lambda r, c=c, hh=hh: v3(r, 8)[:, :, c * 128 + hh * 64:c * 128 + hh * 64 + 64]),
                 w_in[l, :, (hh * 4 + c) * 64:(hh * 4 + c) * 64 + 64].rearrange("(k p) n -> p k n", p=128))
                for c in range(4) for hh in range(2)])
            ikv = ring_fill([(lambda r: v3(r, 8)[:, :, 0:256], w_in[l, :, 512:768].rearrange("(k p) n -> p k n", p=128))])
            ipp = ring_fill([(lambda r: v3(r, 8), w_in[l, :, 768:1280].rearrange("(k p) n -> p k n", p=128))])
            pinned.update([iq, ikv, ipp])
            wq = v3(ring[iq], 8)
            wkv = v3(ring[ikv], 8)
            wpp = v3(ring[ipp], 8)

            def proj_stage(box, sgn, lhs, rres):
                def f():
                    box['s'] = next_slot()
                    mm_fm(box['s'], sgn, lhs, hrhs, rres + hres[sgn])
                return f

            def rope_stage_a(box):
                def f():
                    s = box['s']
                    for h in range(2):
                        ACP(q32[h], ps[:, 2 * s + h, :], ['pb%d' % (2 * s + h)], ['q32%d' % h])
                return f

            def rope_stage_b(dst, dres):
                def f():
                    s2 = next_slot()
                    for h in range(2):
                        MM(ps[:, 2 * s2 + h, :], rperm[:], q32[h], True, True, ['rperm', 'q32%d' % h], ['pb%d' % (2 * s2 + h)])
                    for h in range(2):
                        cs = slice(h * 512, (h + 1) * 512)
                        TT(t1, q32[h], ropeC[:, cs], ALU.mult, ['q32%d' % h, 'ropeC'], ['t1'])
                        TT(q32[h], ps[:, 2 * s2 + h, :], ropeS[:, cs], ALU.mult, ['pb%d' % (2 * s2 + h), 'ropeS', 'q32%d' % h], ['q32%d' % h])
                        TT(dst[:, cs], t1, q32[h], ALU.add, ['t1', 'q32%d' % h], dres)
                return f

            def copy_stage(box, dst, dres):
                def f():
                    s = box['s']
                    ACP(dst, ps[:, 2 * s, :], slot_res(s), dres)
                return f

            def pool_stage2(box, g, sgn, di):
                c0, n, nseq, L, _ = _seg(sgn)
                w = (2, 4, 8, 16)[g]
                hw_ = w // 2
                LP = L + 16

                def pv(buf, lo, hi):
                    return buf[:, 0:nseq * LP].rearrange("p (a b) -> p a b", a=nseq)[:, :, lo:hi]

                def f():
                    s = box['s']
                    P.add('dve', 'memset', (pv(p32, 0, 8), 0.0), None, [], ['p32'])
                    P.add('dve', 'memset', (pv(p32, LP - 8, LP), 0.0), None, ['p32'], ['p32'])
                    P.add('act', 'copy', (pv(p32, 8, 8 + L), slot_flat(s)[:, 0:n].rearrange("p (a b) -> p a b", a=nseq)), None, slot_res(s) + ['p32'], ['p32'])
                    TT(pv(sA, 1, LP), pv(p32, 1, LP), pv(p32, 0, LP - 1), ALU.add, ['p32', 'sA'], ['sA'])
                    src, dst, sn, dn = sA, sB, 'sA', 'sB'
                    lo, hi, sh = 1, LP, 1
                    for _ in range(g):
                        TT(pv(dst, lo + sh, hi - sh), pv(src, lo + 2 * sh, hi), pv(src, lo, hi - 2 * sh), ALU.add, [sn, dn], [dn])
                        lo, hi = lo + sh, hi - sh
                        src, dst, sn, dn = dst, src, dn, sn
                        sh *= 2
                    tot, tn = src, sn
                    eb = (g * 2 + (0 if sgn == 'P' else 1)) * 32
                    ev = ecf[:, eb:eb + nseq * 8].rearrange("p (a b) -> p a b", a=nseq)
                    TT(pv(tot, 8, 8 + hw_), pv(tot, 8, 8 + hw_), ev[:, :, 0:hw_], ALU.mult, [tn, 'ecf'], [tn])
                    if hw_ > 1:
                        ev2 = ecf[:, eb + 16:eb + 16 + nseq * 8].rearrange("p (a b) -> p a b", a=nseq)
                        TT(pv(tot, 8 + L - hw_ + 1, 8 + L), pv(tot, 8 + L - hw_ + 1, 8 + L), ev2[:, :, 0:hw_ - 1], ALU.mult, [tn, 'ecf'], [tn])
                    STT(dTt[di][:, 0:n].rearrange("p (a b) -> p a b", a=nseq), pv(tot, 8, 8 + L), 1.0 / w, pv(p32, 8, 8 + L), ALU.mult, ALU.subtract, [tn, 'p32'], ['dT%d' % di])
                return f

            def pool_stage3(g, sgn, di):
                c0, n, nseq, L, _ = _seg(sgn)

                def f():
                    s = next_slot()
                    for h in range(n // 512):
                        MM(ps[:, 2 * s + h, :], wpool[par][:, g, :], dTt[di][:, h * 512:(h + 1) * 512], True, True, ['wpool%d' % par, 'dT%d' % di], ['pb%d' % (2 * s + h)])
                    psc = V_PSC + l * 4 + g
                    ACT(poolT[:, g, c0:c0 + n], slot_flat(s)[:, 0:n], AF.Identity, slot_res(s) + ['vecs'], ['poolT%d%s' % (g, sgn)], scale=vecs[:, psc:psc + 1])
                return f

            def kvp_units():
                box = {}

                def s1():
                    s = next_slot()
                    box['s'] = s
                    pv4 = slot_flat(s).rearrange("p (a b) -> p a b", a=4)
                    for tbk in range(4):
                        for k in range(8):
                            MM(pv4[:, tbk, :], hT[:, k, tbk * 128:(tbk + 1) * 128], wkv[:, k, 0:256], k == 0, k == 7, ['ring%d' % ikv] + hres['P'], slot_res(s))

                def s2():
                    s = box['s']
                    pv4 = slot_flat(s).rearrange("p (a b) -> p a b", a=4)
                    ACP(kvst, pv4, slot_res(s), ['sA'])
                    ACP(Vb[:, 0:4, 0:64], pv4[:, :, 128:192], slot_res(s) + ['VbP'], ['VbP'])
                    ACP(Vb[:, 0:4, 192:256], pv4[:, :, 192:256], slot_res(s) + ['VbP'], ['VbP'])
                    for b in range(2):
                        DMA('sp', nk_out[b, l].rearrange("(k p) d -> p k d", p=128), kvst[:, 2 * b:2 * b + 2, 0:128], ['sA'], ['nk_out'], 'kvo')
                        DMA('sp', nv_out[b, l].rearrange("(k p) d -> p k d", p=128), kvst[:, 2 * b:2 * b + 2, 128:256], ['sA'], ['nv_out'], 'kvo')
                return [s1, s2]

            def vs_units():
                box = {}

                def s1():
                    s = next_slot()
                    box['s'] = s
                    pv8 = slot_flat(s).rearrange("p (a b) -> p a b", a=8)
                    for tbk in range(8):
                        for k in range(8):
                            MM(pv8[:, tbk, :], hT[:, k, 512 + tbk * 128:512 + (tbk + 1) * 128], wkv[:, k, 128:256], k == 0, k == 7, ['ring%d' % ikv] + hres['S'], slot_res(s))

                def s2():
                    s = box['s']
                    pv8 = slot_flat(s).rearrange("p (a b) -> p a b", a=8)
                    ACP(Vb[:, 4:12, 0:64], pv8[:, :, 0:64], slot_res(s) + ['VbS'], ['VbS'])
                    ACP(Vb[:, 4:12, 192:256], pv8[:, :, 64:128], slot_res(s) + ['VbS'], ['VbS'])
                return [s1, s2]

            klhs = [wkv[:, k, 0:128] for k in range(8)]

            def qu(c, sgn):
                qlhs = [wq[:, k, c * 128:(c + 1) * 128] for k in range(8)]
                box = {}
                if sgn == 'P':
                    return [proj_stage(box, 'P', qlhs, ['ring%d' % iq]), copy_stage(box, qT[:, c, 0:512], ['qT%dP' % c])]
                return [proj_stage(box, 'S', qlhs, ['ring%d' % iq]), rope_stage_a(box), None, rope_stage_b(qT[:, c, 512:1536], ['qT%dS' % c])]

            def pu(c, sgn):
                plhs = [wpp[:, k, c * 128:(c + 1) * 128] for k in range(8)]
                box = {}
                di = 0 if sgn == 'P' else 1
                return [proj_stage(box, sgn, plhs, ['ring%d' % ipp]), pool_stage2(box, c, sgn, di), None, pool_stage3(c, sgn, di)]

            def ku(sgn):
                box = {}
                if sgn == 'P':
                    return [proj_stage(box, 'P', klhs, ['ring%d' % ikv]), copy_stage(box, kT[:, 0:512], ['kTP'])]
                return [proj_stage(box, 'S', klhs, ['ring%d' % ikv]), rope_stage_a(box), None, rope_stage_b(kT[:, 512:1536], ['kTS'])]

            def fill_stage(u, n):
                u[2] = (lambda: filler(n))
                return u

            unitsA = [ku('S'), pu(0, 'S'), qu(0, 'S'), pu(1, 'S'), qu(1, 'S'), pu(2, 'S'), qu(2, 'S'), pu(3, 'S'), qu(3, 'S'), vs_units(),
                      ku('P'), qu(0, 'P'), kvp_units(), qu(1, 'P'), pu(0, 'P'), qu(2, 'P'), pu(1, 'P'), qu(3, 'P'),
                      fill_stage(pu(2, 'P'), FILL_AB), [lambda: None], fill_stage(pu(3, 'P'), FILL_AB)]
            if l == 0:
                for i_, pc in enumerate(range(4, 12)):
                    unitsA.insert(3 + 2 * i_, [lambda pc=pc: mods_for_layer(0, [pc])])
            run_pipeline(unitsA)
            pinned.difference_update([iq, ikv, ipp])
            if stop == 'A':
                raise _Stop()

            coef_mid(l)
            aunits = []
            acc_ctr = [0]

            def add_qblock(q0, kblocks, qres, ores):
                for kvh in range(2):
                    acc = acc_ctr[0] % 4
                    acc_ctr[0] += 1
                    groups = [kblocks[g0:g0 + 2] for g0 in range(0, len(kblocks), 2)]
                    for gi, grp in enumerate(groups):
                        aunits.append(dict(q0=q0, kvh=kvh, grp=grp, first=(gi == 0), last=(gi == len(groups) - 1), acc=acc, qres=qres, ores=ores))

            def att_S(u, n):
                def f():
                    sl = 2 + n % 2
                    h0 = u['kvh'] * 64
                    q0 = u['q0']
                    for j, (kap, vap, mi, kres, vres) in enumerate(u['grp']):
                        bank = 2 * sl + j
                        if mi is not None:
                            MM(ps[:, bank, :], identb[:], maskb[:, mi, :], True, False, ['identb', 'maskb'], ['pb%d' % bank])
                        MM(ps[:, bank, :], kap[h0:h0 + 64, :], qT[h0:h0 + 64, :, q0:q0 + 128], mi is None, True, [kres] + u['qres'], ['pb%d' % bank])
                return f

            def att_PV(u, n):
                def f():
                    sl = 2 + n % 2
                    pti = n % 3
                    kvh = u['kvh']
                    h0 = kvh * 64
                    s0 = 64 - h0
                    q0 = u['q0']
                    accb = u['acc']
                    grp = u['grp']
                    ng = len(grp)
                    ACT(PT[pti][:, 0:ng, :], ps[:, 2 * sl:2 * sl + ng, :], AF.Exp, ['pb%d' % (2 * sl + j) for j in range(ng)], ['PT%d' % pti], scale=SCALE)
                    if u['first']:
                        MM(ps[:, accb, :], sinkL[kvh][:], sinkhl[:, kvh * 512:(kvh + 1) * 512], True, False, ['sinkL', 'sinkhl'], ['pb%d' % accb])
                    for j, (kap, vap, mi, kres, vres) in enumerate(grp):
                        MM(ps[:, accb, :], vap[:, kvh * 128:(kvh + 1) * 128], PT[pti][:, j, :], False, u['last'] and j == ng - 1, [vres, 'PT%d' % pti], ['pb%d' % accb])
                return f

            def att_NORM(u, n):
                if not u['last']:
                    return None

                def f():
                    kvh = u['kvh']
                    h0 = kvh * 64
                    s0 = 64 - h0
                    q0 = u['q0']
                    accb = u['acc']
                    if kvh == 0:
                        P.add('dve', 'reciprocal', (rcp[kvh][s0:s0 + 64, :], ps[s0:s0 + 64, accb, :]), None, ['pb%d' % accb], [rcpn[kvh]])
                    else:
                        ACT(rcp[kvh][s0:s0 + 64, :], ps[s0:s0 + 64, accb, :], AF.Ln, ['pb%d' % accb], [rcpn[kvh]])
                        ACT(rcp[kvh][s0:s0 + 64, :], rcp[kvh][s0:s0 + 64, :], AF.Exp, [rcpn[kvh]], [rcpn[kvh]], scale=-1.0)
                    TT(hT[h0:h0 + 64, 0:4, q0:q0 + 128], ps[h0:h0 + 64, accb, :].rearrange("p (a b) -> p a b", a=4),
                       rcp[kvh][s0:s0 + 64, :].rearrange("p (a b) -> p a b", a=4), ALU.mult, ['pb%d' % accb, rcpn[kvh]], u['ores'])
                return f

            for i in range(8):
                kbl = [(kcT[:, kb * 128:(kb + 1) * 128], vcb[:, kb, :], None, 'kcT', 'vcb') for kb in range(2)]
                for j in (i - 1, i, i + 1):
                    if j < 0 or j > 7:
                        continue
                    mi = None if j == i else (0 if j == i - 1 else 1)
                    kbl.append((kT[:, 512 + j * 128:512 + (j + 1) * 128], Vb[:, 4 + j, :], mi, 'kTS', 'VbS'))
                add_qblock(512 + i * 128, kbl, ['qT%dS' % c for c in range(4)], ['hT%dS' % c for c in range(4)])
            for b in range(2):
                kbl = [(kT[:, b * 256 + kb * 128:b * 256 + (kb + 1) * 128], Vb[:, 2 * b + kb, :], None, 'kTP', 'VbP') for kb in range(2)]
                for qb in range(2):
                    add_qblock(b * 256 + qb * 128, kbl, ['qT%dP' % c for c in range(4)], ['hT%dP' % c for c in range(4)])
            def att_warm():
                for i in range(FILL_ATT):
                    MM(ps[:, 3, :], ones[:], maskb[:, 0, :], True, True, ['ones', 'maskb'], ['pb3'])

            att_stages = [[att_S(u, n), att_PV(u, n), None, att_NORM(u, n)] for n, u in enumerate(aunits)]
            s0_ = att_stages[0][0]
            att_stages[0][0] = (lambda: (s0_(), att_warm()))
            run_pipeline(att_stages, reverse=False)
            if stop == 'B':
                raise _Stop()

            filler(FILL_BC)

            ios = []
            for half in range(2):
                c_a, c_b = half * 512, (half + 1) * 512
                ios.append(ring_fill([
                    (lambda r: v3(r, 8)[0:64, 0:4, :], w_out[l, 0:256, c_a:c_b].rearrange("(c p) n -> p c n", p=64)),
                    (lambda r: v3(r, 8)[64:128, 0:4, :], w_out[l, 256:512, c_a:c_b].rearrange("(c p) n -> p c n", p=64)),
                    (lambda r: v3(r, 8)[:, 4:8, :], w_out[l, 512:1024, c_a:c_b].rearrange("(c p) n -> p c n", p=128)),
                ]))
            wos = [v3(ring[i], 8) for i in ios]
            proj_ln(l, 1, False, lambda dc: [wos[dc // 4][:, k, (dc % 4) * 128:(dc % 4 + 1) * 128] for k in range(8)], 8,
                    mixrhs, lambda sgn: mixres[sgn], ['ring%d' % i for i in ios], K_G1P, 'cfG1_%d' % par)
            if stop == 'C':
                raise _Stop()
            P.transfer(MIX_RES, FFN_RES)

            nmod_done = [0]

            def next_mod_piece():
                if (not lastl) and nmod_done[0] < 12:
                    mods_for_layer(l + 1, [nmod_done[0]])
                    nmod_done[0] += 1

            def cw(tap, ch):
                cc = V_CW + (l * 3 + tap) * 44 + ch
                return vecs[:, cc:cc + 1]

            def cb(ch):
                cc = V_CB + l * 44 + ch
                return vecs[:, cc:cc + 1]

            tb_ctr = [0]
            j0 = 0
            for hf in range(2):
                npairs = HALF_PAIRS[hf]
                for jp in range(npairs // 2):
                    jA = j0 + 2 * jp
                    iu = ring_fill([
                        (lambda r: v3(r, 8)[:, :, 0:256], w_up[l, :, jA * 128:(jA + 2) * 128].rearrange("(k p) n -> p k n", p=128)),
                        (lambda r: v3(r, 8)[:, :, 256:512], w_up[l, :, 2816 + jA * 128:2816 + (jA + 2) * 128].rearrange("(k p) n -> p k n", p=128)),
                    ])
                    wu = v3(ring[iu], 8)
                    first_slot = (hf == 0 and jp == 0)
                    last_slot = (jp == npairs // 2 - 1)
                    order = [(0, 'S'), (1, 'S'), (0, 'P'), (1, 'P')] if (first_slot or last_slot) else [(0, 'P'), (0, 'S'), (1, 'P'), (1, 'S')]
                    for oi, (jj, sgn) in enumerate(order):
                        j = jA + jj
                        jm = j - j0
                        la = [wu[:, k, jj * 128:(jj + 1) * 128] for k in range(8)]
                        lg = [wu[:, k, 256 + jj * 128:256 + (jj + 1) * 128] for k in range(8)]
                        for sgn in (sgn,):
                            c0, n, nseq, L, _ = _seg(sgn)
                            tbi = tb_ctr[0] % 2
                            tb_ctr[0] += 1
                            rr = ['ring%d' % iu] + hres[sgn]
                            if sgn == 'P':
                                s = next_slot()
                                for k in range(8):
                                    MM(ps[:, 2 * s, :], la[k], hT[:, k, 0:512], k == 0, k == 7, rr, ['pb%d' % (2 * s)])
                                for k in range(8):
                                    MM(ps[:, 2 * s + 1, :], lg[k], hT[:, k, 0:512], k == 0, k == 7, rr, ['pb%d' % (2 * s + 1)])
                                srcs = [(ps[:, 2 * s, :], ['pb%d' % (2 * s)], ta[tbi], 'ta%d' % tbi, j), (ps[:, 2 * s + 1, :], ['pb%d' % (2 * s + 1)], tg[tbi], 'tg%d' % tbi, 22 + j)]
                            else:
                                s = next_slot()
                                mm_fm(s, 'S', la, hrhs, rr)
                                s2 = next_slot()
                                mm_fm(s2, 'S', lg, hrhs, rr)
                                srcs = [(slot_flat(s), slot_res(s), ta[tbi], 'ta%d' % tbi, j), (slot_flat(s2), slot_res(s2), tg[tbi], 'tg%d' % tbi, 22 + j)]
                            for (src, sres, tbuf, tname, ch) in srcs:
                                sv = src.rearrange("p (a b) -> p a b", a=nseq)
                                tv = tbuf[:, 0:n].rearrange("p (a b) -> p a b", a=nseq)
                                ACT(tv, sv, AF.Identity, sres + ['vecs'], [tname], bias=cb(ch), scale=cw(1, ch))
                                STT(tv[:, :, 1:L], sv[:, :, 0:L - 1], cw(0, ch), tv[:, :, 1:L], ALU.mult, ALU.add, sres + ['vecs', tname], [tname])
                                STT(tv[:, :, 0:L - 1], sv[:, :, 1:L], cw(2, ch), tv[:, :, 0:L - 1], ALU.mult, ALU.add, sres + ['vecs', tname], [tname])
                            ACT(tg[tbi][:, 0:n], tg[tbi][:, 0:n], AF.Silu, ['tg%d' % tbi], ['tg%d' % tbi])
                            TT(mT[:, jm, c0:c0 + n], ta[tbi][:, 0:n], tg[tbi][:, 0:n], ALU.mult, ['ta%d' % tbi, 'tg%d' % tbi], ['mT%d%s' % (jm, sgn)])
                    next_mod_piece()
                if hf == 0:
                    for dcp in range(4):
                        idn = ring_fill([(lambda r: r[:, 0:npairs * 256].rearrange("p (a b) -> p a b", a=npairs),
                                          w_down[l, j0 * 128:(j0 + npairs) * 128, dcp * 256:(dcp + 1) * 256].rearrange("(k p) n -> p k n", p=128))])
                        wd = ring[idn][:, 0:npairs * 256].rearrange("p (a b) -> p a b", a=npairs)
                        for d2, sgn in ((0, 'S'), (1, 'S'), (0, 'P'), (1, 'P')):
                            dc = dcp * 2 + d2
                            lhs = [wd[:, k, d2 * 128:(d2 + 1) * 128] for k in range(npairs)]
                            for sgn in (sgn,):
                                c0, n, _, _, cond = _seg(sgn)
                                s = next_slot()
                                mm_fm(s, sgn, lhs, lambda k, a, b: mT[:, k, a:b], ['ring%d' % idn] + ['mT%d%s' % (k, sgn) for k in range(npairs)])
                                xv = xT[:, dc, c0:c0 + n]
                                STT(xv, slot_flat(s)[:, 0:n], cf(par, K_G2P, cond, dc), xv, ALU.mult, ALU.add, slot_res(s) + ['cfG2_%d' % par, 'xT%d%s' % (dc, sgn)], ['xT%d%s' % (dc, sgn)])
                    next_mod_piece()
                else:
                    while (not lastl) and nmod_done[0] < 12:
                        next_mod_piece()
                    if not lastl:
                        coef_next(l)
                    idns = []
                    for dq in range(2):
                        idns.append(ring_fill([(lambda r: r[:, 0:npairs * 512].rearrange("p (a b) -> p a b", a=npairs),
                                                w_down[l, j0 * 128:(j0 + npairs) * 128, dq * 512:(dq + 1) * 512].rearrange("(k p) n -> p k n", p=128))]))
                    wds = [ring[i][:, 0:npairs * 512].rearrange("p (a b) -> p a b", a=npairs) for i in idns]
                    proj_ln(l, 2, lastl, lambda dc: [wds[dc // 4][:, k, (dc % 4) * 128:(dc % 4 + 1) * 128] for k in range(npairs)], npairs,
                            lambda k, a, b: mT[:, k, a:b], lambda sgn: ['mT%d%s' % (k, sgn) for k in range(npairs)], ['ring%d' % i for i in idns], K_G2P, 'cfG2_%d' % par, nfill=0)
                j0 += npairs
            P.transfer(FFN_RES, IO_RES if lastl else MIX_RES)
          except _Stop:
            P.transfer(MIX_RES + FFN_RES, IO_RES)
            break

        for tb in (range(12) if (stop is not None or depth == 0) else []):
            b = tb % 2
            sgn = 'P' if tb < 4 else 'S'
            s = next_slot()
            for c in range(8):
                TR(slot_flat(s)[:, c * 128:(c + 1) * 128], xT[:, c, tb * 128:(tb + 1) * 128], ident32[:], ['xT%d%s' % (c, sgn), 'ident32'], slot_res(s))
            if tb % 2 == 0:
                ACP(xs[b], slot_flat(s), slot_res(s), ['xs%d' % b])
            else:
                VCP(xs[b], slot_flat(s), slot_res(s), ['xs%d' % b])
            DMA('sp', y_out[tb * 128:(tb + 1) * 128, :], xs[b], ['xs%d' % b], ['y_out%d' % b], 'yo%d' % b)
        P.add('sp', None, (), None, ['y_out0', 'y_out1', 'nk_out', 'nv_out'], [])
        P.emit(nc, st)
        nc._prog_stats = (len(P.ops), P.n_waits)
    return nc


def _constants():
    half = 32
    inv = (10000.0 ** (-np.arange(0, half, 2, dtype=np.float32) / half)).astype(np.float32)
    t = np.arange(1024)
    row = (t // 64).astype(np.float32)
    col = (t % 64).astype(np.float32)
    C = np.zeros((128, 1024), np.float32)
    S = np.zeros((128, 1024), np.float32)
    for p in range(128):
        d = p % 64
        pos = row if d < 32 else col
        f = inv[(d % 32) % 16]
        ang = (pos * f).astype(np.float32)
        C[p] = np.cos(ang)
        S[p] = np.sin(ang)
    R = np.zeros((128, 128), np.float32)
    for m in range(128):
        if (m % 32) < 16:
            R[m + 16, m] = -1.0
        else:
            R[m - 16, m] = 1.0
    ident = np.eye(128, dtype=np.float32)
    cc = np.arange(128)[:, None]
    rr = np.arange(128)[None, :]
    mA = np.where(cc >= rr, 0.0, NEG).astype(np.float32)
    mB = np.where(cc <= rr, 0.0, NEG).astype(np.float32)
    masks = np.concatenate([np.tile(mA, (1, 4)), np.tile(mB, (1, 4))], axis=1)
    ecf = np.ones((128, 256), np.float32)
    for g, w in enumerate((2, 4, 8, 16)):
        hw = w // 2
        for si in range(2):
            base = (g * 2 + si) * 32
            for sq in range(2):
                for i in range(hw):
                    ecf[:, base + sq * 8 + i] = w / float(i + hw)
                for i in range(hw - 1):
                    ecf[:, base + 16 + sq * 8 + i] = w / float(2 * hw - 1 - i)
    return C, S, R, ident, masks, ecf


def _run(inputs, depth=DEPTH, trace=False, stop=None):
    f = lambda a: np.ascontiguousarray(np.asarray(a, dtype=np.float32))
    x_prompt, x_sample = f(inputs['x_prompt']), f(inputs['x_sample'])
    cache_k, cache_v = f(inputs['cache_k']), f(inputs['cache_v'])
    c, c_ctx = f(inputs['c']), f(inputs['c_ctx'])
    C, S, R, ident, masks, ecf = _constants()
    common = np.concatenate([
        f(inputs['b_mod']).reshape(-1, 128), f(inputs['pool_scale']).reshape(-1, 128),
        f(inputs['ln1_g']).reshape(-1, 128), f(inputs['ln1_b']).reshape(-1, 128),
        f(inputs['ln2_g']).reshape(-1, 128), f(inputs['ln2_b']).reshape(-1, 128),
        f(inputs['conv_w']).reshape(-1, 128), f(inputs['conv_b']).reshape(-1, 128),
        c_ctx.reshape(8, 128)], axis=0)
    assert common.shape[0] == V_COND + 8
    shared = {
        'ropeC': C, 'ropeS': S, 'rperm': R, 'ident': ident, 'masks': masks, 'ecf': ecf,
        'sink': f(inputs['attn_sink']),
        'w_mod': f(inputs['w_mod']), 'w_in': f(inputs['w_in']), 'w_pool': f(inputs['w_pool']),
        'w_out': f(inputs['w_out']), 'w_up': f(inputs['w_up']), 'w_down': f(inputs['w_down']),
    }
    in_maps = []
    for core in range(8):
        vp = np.zeros((V_ROWS, 128), np.float32)
        vp[:common.shape[0]] = common
        vp[V_COND + 8:V_COND + 16] = c[core].reshape(8, 128)
        m = dict(shared)
        m['x_in'] = np.ascontiguousarray(np.concatenate([x_prompt[2 * core], x_prompt[2 * core + 1], x_sample[core]], axis=0))
        m['ck'] = np.ascontiguousarray(cache_k[core].reshape(DEPTH, 256, 128))
        m['cv'] = np.ascontiguousarray(cache_v[core].reshape(DEPTH, 256, 128))
        m['vecpack'] = vp
        in_maps.append(m)
    nc = build_program(depth, stop)
    res = run_bass_kernel_spmd(nc, in_maps, core_ids=list(range(8)), trace=trace)
    y_prompt = np.zeros((16, 256, D), np.float32)
    y_sample = np.zeros((8, 1024, D), np.float32)
    nk = np.zeros((16, DEPTH, 256, 2, 64), np.float32)
    nv = np.zeros((16, DEPTH, 256, 2, 64), np.float32)
    for core in range(8):
        r = res.results[core]
        y = np.asarray(r['y'])
        y_prompt[2 * core] = y[0:256]
        y_prompt[2 * core + 1] = y[256:512]
        y_sample[core] = y[512:1536]
        nk[2 * core:2 * core + 2] = np.asarray(r['nk']).reshape(2, DEPTH, 256, 2, 64)
        nv[2 * core:2 * core + 2] = np.asarray(r['nv']).reshape(2, DEPTH, 256, 2, 64)
    return (y_prompt, y_sample, nk, nv), res


def kernel(**inputs):
    outs, _ = _run(inputs)
    return outs
```

```python
from contextlib import ExitStack
import numpy as np
import concourse.bass as bass
import concourse.mybir as mybir
from concourse.bass_utils import run_bass_kernel_spmd

F32 = mybir.dt.float32
BF16 = mybir.dt.bfloat16
AF = mybir.ActivationFunctionType
ALU = mybir.AluOpType

DEPTH = 4
D = 1024
NTOK = 1536
ALPHA = (2 * DEPTH) ** 0.25
EPSP = 1e-5 / (ALPHA * ALPHA)
SCALE = 64 ** -0.5
NEG = -30000.0
NS = 5
HALF_PAIRS = (14, 8)
FILL_AB, FILL_BC, FILL_CD, FILL_EA = 16, 10, 30, 0
FILL_ATT = 8
FILL_LN = 16

V_BMOD = 0
V_PSC = 192
V_LN1G = 208
V_LN1B = 240
V_LN2G = 272
V_LN2B = 304
V_CW = 336
V_CB = 864
V_COND = 1040
V_ROWS = 1152


class _Op:
    __slots__ = ('eng', 'meth', 'args', 'kw', 'deps', 'signal', 'sig_val', 'dma_key', 'dma_val', 'dma_waits', 'idx')


class Prog:
    def __init__(self):
        self.ops = []
        self.last_w = {}
        self.readers = {}
        self.dma_count = {}

    def transfer(self, old, new):
        comb = {}
        dl = []
        for r in old:
            for w in self.last_w.get(r, ()):
                if w.dma_key is not None:
                    dl.append(w)
                elif w.eng not in comb or comb[w.eng].idx < w.idx:
                    comb[w.eng] = w
            for k, v in self.readers.get(r, {}).items():
                if k == '_dma':
                    dl.extend(v)
                elif k not in comb or comb[k].idx < v.idx:
                    comb[k] = v
        for n in new:
            self.last_w[n] = []
            rd = dict(comb)
            if dl:
                rd['_dma'] = list(dl)
            self.readers[n] = rd

    def add(self, eng, meth, args=(), kw=None, reads=(), writes=(), dma_key=None):
        op = _Op()
        op.eng = eng
        op.meth = meth
        op.args = args
        op.kw = kw or {}
        op.dma_key = dma_key
        op.deps = []
        op.signal = False
        op.sig_val = None
        op.dma_waits = {}
        op.idx = len(self.ops)
        deps = {}

        def adddep(d):
            if d is None or d is op:
                return
            if eng == 'pe' and d.eng == 'pe' and d.dma_key is None:
                return
            deps[d.idx] = d

        for r in reads:
            for w in self.last_w.get(r, ()):
                adddep(w)
            if r.startswith('pb'):
                for k, v in self.readers.get(r, {}).items():
                    if k != eng and k != '_dma':
                        adddep(v)
        for w_ in writes:
            for w in self.last_w.get(w_, ()):
                adddep(w)
            rd = self.readers.get(w_)
            if rd:
                for k, v in rd.items():
                    if k == '_dma':
                        for x in v:
                            adddep(x)
                    else:
                        adddep(v)
        for d in deps.values():
            if d.dma_key is not None:
                k = d.dma_key
                op.dma_waits[k] = max(op.dma_waits.get(k, 0), self.dma_count[k])
            else:
                op.deps.append(d)
        for r in reads:
            rd = self.readers.setdefault(r, {})
            if dma_key is not None:
                rd.setdefault('_dma', []).append(op)
            else:
                rd[eng] = op
        for w_ in writes:
            self.last_w[w_] = [op]
            self.readers[w_] = {}
        if dma_key is not None:
            self.dma_count[dma_key] = self.dma_count.get(dma_key, 0) + 16
            op.dma_val = self.dma_count[dma_key]
        self.ops.append(op)
        return op

    def emit(self, nc, stack):
        for op in self.ops:
            for d in op.deps:
                d.signal = True
        cnt = {}
        for op in self.ops:
            if op.dma_key is None and op.signal:
                cnt[op.eng] = cnt.get(op.eng, 0) + 1
                op.sig_val = cnt[op.eng]
        esem = {}
        for e in ('pe', 'act', 'dve', 'pool', 'sp'):
            esem[e] = stack.enter_context(nc.semaphore("s_" + e))
        dsem = {}
        for k in self.dma_count:
            dsem[k] = stack.enter_context(nc.semaphore("d_%s" % k))
        byeng = {}
        for op in self.ops:
            byeng.setdefault(op.eng, []).append(op)
        block = stack.enter_context(nc.Block())
        self.n_waits = 0

        def run_engine(ename, eobj):
            waited = {}
            for op in byeng.get(ename, []):
                need = {}
                for d in op.deps:
                    key = ('e', d.eng)
                    need[key] = (esem[d.eng], max(need.get(key, (None, 0))[1], d.sig_val))
                for k, v in op.dma_waits.items():
                    key = ('d', k)
                    need[key] = (dsem[k], max(need.get(key, (None, 0))[1], v))
                for key, (s, v) in need.items():
                    if waited.get(key, 0) >= v:
                        continue
                    eobj.wait_ge(s, v)
                    self.n_waits += 1
                    waited[key] = v
                if op.meth is None:
                    continue
                ins = getattr(eobj, op.meth)(*op.args, **op.kw)
                if op.dma_key is not None:
                    ins.then_inc(dsem[op.dma_key], 16)
                elif op.signal:
                    ins.then_inc(esem[op.eng], 1)

        @block.tensor
        def _(e):
            run_engine('pe', e)

        @block.scalar
        def _(e):
            run_engine('act', e)

        @block.vector
        def _(e):
            run_engine('dve', e)

        @block.gpsimd
        def _(e):
            run_engine('pool', e)

        @block.sync
        def _(e):
            run_engine('sp', e)


def _seg(name):
    return (0, 512, 2, 256, 0) if name == 'P' else (512, 1024, 1, 1024, 1)


class _Stop(Exception):
    pass


def build_program(depth=DEPTH, stop=None):
    nc = bass.Bass("TRN2", target_bir_lowering=False)
    dt = nc.dram_tensor
    x_in = dt("x_in", [NTOK, D], F32, kind="ExternalInput").ap()
    ck = dt("ck", [DEPTH, 256, 128], F32, kind="ExternalInput").ap()
    cv = dt("cv", [DEPTH, 256, 128], F32, kind="ExternalInput").ap()
    vecpack = dt("vecpack", [V_ROWS, 128], F32, kind="ExternalInput").ap()
    sink_d = dt("sink", [DEPTH, 8], F32, kind="ExternalInput").ap()
    ropeC_d = dt("ropeC", [128, 1024], F32, kind="ExternalInput").ap()
    ropeS_d = dt("ropeS", [128, 1024], F32, kind="ExternalInput").ap()
    rperm_d = dt("rperm", [128, 128], F32, kind="ExternalInput").ap()
    ident_d = dt("ident", [128, 128], F32, kind="ExternalInput").ap()
    masks_d = dt("masks", [128, 1024], F32, kind="ExternalInput").ap()
    ecf_d = dt("ecf", [128, 256], F32, kind="ExternalInput").ap()
    w_mod = dt("w_mod", [DEPTH, D, 6 * D], F32, kind="ExternalInput").ap()
    w_in = dt("w_in", [DEPTH, D, 1280], F32, kind="ExternalInput").ap()
    w_pool = dt("w_pool", [DEPTH, 4, 128, 128], F32, kind="ExternalInput").ap()
    w_out = dt("w_out", [DEPTH, D, D], F32, kind="ExternalInput").ap()
    w_up = dt("w_up", [DEPTH, D, 5632], F32, kind="ExternalInput").ap()
    w_down = dt("w_down", [DEPTH, 2816, D], F32, kind="ExternalInput").ap()
    y_out = dt("y", [NTOK, D], F32, kind="ExternalOutput").ap()
    nk_out = dt("nk", [2, DEPTH, 256, 128], F32, kind="ExternalOutput").ap()
    nv_out = dt("nv", [2, DEPTH, 256, 128], F32, kind="ExternalOutput").ap()

    P = Prog()
    st = ExitStack()
    with st:
        def sb(n, s, d):
            return st.enter_context(nc.sbuf_tensor(n, s, d))

        xT = sb("xT", [128, 8, NTOK], F32)
        hT = sb("hT", [128, 8, NTOK], BF16)
        vecs = sb("vecs", [128, V_ROWS], F32)
        ropeC = sb("ropeCs", [128, 1024], F32)
        ropeS = sb("ropeSs", [128, 1024], F32)
        rperm = sb("rperms", [128, 128], F32)
        ident32 = sb("ident32", [128, 128], F32)
        identb = sb("identb", [128, 128], BF16)
        ones = sb("ones", [128, 128], BF16)
        maskb = sb("maskb", [128, 2, 512], BF16)
        ecf = sb("ecfs", [128, 256], F32)
        sink8 = sb("sink8", [1, 8], F32)
        sinkf8 = sb("sinkf8", [1, 8], F32)
        sinkh8 = sb("sinkh8", [1, 8], BF16)
        sinkl8 = sb("sinkl8", [1, 8], BF16)
        sinkhl = sb("sinkhl", [33, 1024], BF16)
        sinkL = [sb("sinkL%d" % i, [33, 128], BF16) for i in range(2)]
        scT = sb("scT", [128, 8, 2], BF16)
        modsT = [sb("modsT%d" % i, [128, 2, 48], F32) for i in range(2)]
        coef = sb("coef", [128, 2, 12, 2, 8], F32)
        ring = [sb("ring%d" % i, [128, 4096], BF16) for i in range(NS)]
        wpool = [sb("wpool%d" % i, [128, 4, 128], BF16) for i in range(2)]
        lnm = sb("lnm", [128, 512], F32)
        lnq = sb("lnq", [128, 512], F32)
        ybf = [sb("ybf%d" % i, [128, 512], BF16) for i in range(2)]
        ysq = [sb("ysq%d" % i, [128, 512], BF16) for i in range(2)]
        UW = 16128
        U = sb("U", [128, UW], F32)
        ps = st.enter_context(nc.psum_tensor("ps", [128, 8, 512], F32))

        cur = [0]

        def carve(nwords, dtype, shape=()):
            a = cur[0]
            cur[0] += nwords
            assert cur[0] <= UW, cur[0]
            v = U[:, a:a + nwords]
            if dtype == BF16:
                v = v.bitcast(BF16)
            if len(shape) == 2:
                return v.rearrange("p (a b) -> p a b", a=shape[0])
            if len(shape) == 3:
                return v.rearrange("p (a b c) -> p a b c", a=shape[0], b=shape[1])
            return v

        qT = carve(3072, BF16, (4, NTOK))
        kT = carve(768, BF16)
        Vb = carve(1536, BF16, (12, 256))
        poolT = carve(3072, BF16, (4, NTOK))
        kcT = carve(128, BF16)
        vcb = carve(256, BF16, (2, 256))
        p32 = carve(1040, F32)
        sA = carve(1040, F32)
        sB = carve(1040, F32)
        dTt = [carve(512, BF16) for _ in range(2)]
        q32 = [carve(512, F32) for _ in range(2)]
        t1 = carve(512, F32)
        PT = [carve(512, BF16, (2, 512)) for _ in range(3)]
        kvst = sA[:, 0:1024].rearrange("p (a b) -> p a b", a=4)
        cstage = sB[:, 0:512].rearrange("p (a b c) -> p a b c", a=2, b=2)
        rcp = [lnm, lnq]
        rcpn = ['lnm', 'lnq']
        cur[0] = 0
        mT = carve(14 * 768, BF16, (14, NTOK))
        ta = [carve(1024, F32) for _ in range(2)]
        tg = [carve(1024, F32) for _ in range(2)]
        cur[0] = 0
        xs = [carve(1024, F32) for _ in range(6)]

        MIX_RES = (['qT%d%s' % (c, s) for c in range(4) for s in 'PS'] + ['poolT%d%s' % (c, s) for c in range(4) for s in 'PS'] +
                   ['kTP', 'kTS', 'VbP', 'VbS', 'kcT', 'vcb', 'p32', 'sA', 'sB', 'dT0', 'dT1', 'q320', 'q321', 't1',
                    'PT0', 'PT1', 'PT2'])
        FFN_RES = ['mT%d%s' % (j, s) for j in range(14) for s in 'PS'] + ['ta0', 'ta1', 'tg0', 'tg1']
        IO_RES = ['xs%d' % i for i in range(6)]

        def MM(out, lhsT, rhs, start, stop, reads, writes):
            P.add('pe', 'matmul', (out, lhsT, rhs), dict(start=start, stop=stop), reads, writes)

        def TR(out, in_, ident, reads, writes):
            P.add('pe', 'transpose', (out, in_, ident), None, reads, writes)

        def ACT(out, in_, func, reads, writes, bias=None, scale=None):
            kw = {}
            if bias is not None:
                kw['bias'] = bias
            if scale is not None:
                kw['scale'] = scale
            P.add('act', 'activation', (out, in_, func), kw, reads, writes)

        def ACP(out, in_, reads, writes):
            P.add('act', 'copy', (out, in_), None, reads, writes)

        def VCP(out, in_, reads, writes):
            P.add('dve', 'tensor_copy', (out, in_), None, reads, writes)

        def TT(out, in0, in1, op, reads, writes):
            P.add('dve', 'tensor_tensor', (), dict(out=out, in0=in0, in1=in1, op=op), reads, writes)

        def TS(out, in0, s1, s2, op0, op1, reads, writes):
            kw = dict(out=out, in0=in0, scalar1=s1, scalar2=s2, op0=op0)
            if op1 is not None:
                kw['op1'] = op1
            P.add('dve', 'tensor_scalar', (), kw, reads, writes)

        def STT(out, in0, scalar, in1, op0, op1, reads, writes):
            P.add('dve', 'scalar_tensor_tensor', (), dict(out=out, in0=in0, scalar=scalar, in1=in1, op0=op0, op1=op1), reads, writes)

        def DMA(q, out, in_, reads, writes, key):
            P.add(q, 'dma_start', (), dict(out=out, in_=in_), reads, writes, dma_key=key)

        slot_ctr = [0]
        reserved = set()

        def _bank_recency(bk):
            r = 'pb%d' % bk
            m = -1
            for w in P.last_w.get(r, ()):
                m = max(m, w.idx)
            for k, v in P.readers.get(r, {}).items():
                if k == '_dma':
                    for x in v:
                        m = max(m, x.idx)
                else:
                    m = max(m, v.idx)
            return m

        def next_slot():
            best, bm = None, None
            for s in range(4):
                if s in reserved:
                    continue
                m = max(_bank_recency(2 * s), _bank_recency(2 * s + 1))
                if bm is None or m < bm:
                    best, bm = s, m
            return best

        def slot_res(s):
            return ['pb%d' % (2 * s), 'pb%d' % (2 * s + 1)]

        def slot_flat(s):
            return ps[:, 2 * s:2 * s + 2, :].rearrange("p a b -> p (a b)")

        ring_ctr = [0]

        def v3(t, a):
            return t[:].rearrange("p (a b) -> p a b", a=a)

        pinned = set()

        def ring_fill(dmas):
            while True:
                i = ring_ctr[0] % NS
                ring_ctr[0] += 1
                if i not in pinned:
                    break
            for dst_fn, src in dmas:
                DMA('pool', dst_fn(ring[i]), src, [], ['ring%d' % i], 'ring%d' % i)
            return i

        DMA('sp', ident32[:], ident_d, [], ['ident32'], 'c0')
        DMA('sp', rperm[:], rperm_d, [], ['rperm'], 'c0')
        DMA('sp', ecf[:], ecf_d, [], ['ecf'], 'c0')
        DMA('sp', ropeC[:], ropeC_d, [], ['ropeC'], 'c0')
        DMA('sp', ropeS[:], ropeS_d, [], ['ropeS'], 'c0')
        DMA('pool', identb[:], ident_d, [], ['identb'], 'c1')
        DMA('pool', maskb[:].rearrange("p a b -> p (a b)"), masks_d, [], ['maskb'], 'c1')
        P.add('dve', 'memset', (ones[:], 1.0), None, [], ['ones'])
        P.add('dve', 'memset', (sinkhl[:], 0.0), None, [], ['sinkhl'])
        for kvh in range(2):
            P.add('dve', 'memset', (sinkL[kvh][:], 0.0), None, [], ['sinkL'])
            P.add('dve', 'memset', (sinkL[kvh][:, (1 - kvh) * 64:(2 - kvh) * 64], 1.0), None, ['sinkL'], ['sinkL'])
        DMA('sp', xs[4].rearrange("p (t c) -> p t c", t=8), vecpack[0:1024, :].rearrange("(t p) c -> p t c", p=128), [], ['xs4'], 'xs4')
        DMA('sp', xs[5][:, 0:128], vecpack[1024:1152, :], [], ['xs5'], 'xs5')
        for t in range(9):
            src = xs[4][:, t * 128:(t + 1) * 128] if t < 8 else xs[5][:, 0:128]
            sn = 'xs4' if t < 8 else 'xs5'
            s = next_slot()
            TR(ps[:, 2 * s, 0:128], src, ident32[:], [sn, 'ident32'], slot_res(s))
            ACP(vecs[:, t * 128:(t + 1) * 128], ps[:, 2 * s, 0:128], slot_res(s), ['vecs'])
        for c in range(2):
            ACT(scT[:, :, c], vecs[:, V_COND + 8 * c:V_COND + 8 * c + 8], AF.Silu, ['vecs'], ['scT'])

        for ti_, tb in enumerate(list(range(4, 12)) + list(range(4))):
            b = ti_ % 4
            DMA('sp', xs[b], x_in[tb * 128:(tb + 1) * 128, :], [], ['xs%d' % b], 'xs%d' % b)
            s = next_slot()
            for c in range(8):
                TR(slot_flat(s)[:, c * 128:(c + 1) * 128], xs[b][:, c * 128:(c + 1) * 128], ident32[:], ['xs%d' % b, 'ident32'], slot_res(s))
            sg = 'P' if tb < 4 else 'S'
            ACP(xT[:, :, tb * 128:(tb + 1) * 128], slot_flat(s).rearrange("p (c t) -> p c t", c=8), slot_res(s), ['xT%d%s' % (c, sg) for c in range(8)])

        def cf(par, kind, cond, c):
            return coef[:, par, kind, cond, c:c + 1]

        K_A1, K_B1, K_G1P, K_G2P, K_A2, K_B2, K_OP1, K_OP2 = range(8)

        def mods_for_layer(l, pieces):
            par = l % 2
            for pc in pieces:
                i = ring_fill([(lambda r: v3(r, 8), w_mod[l, :, pc * 512:(pc + 1) * 512].rearrange("(k p) n -> p k n", p=128))])
                s = next_slot()
                wv = v3(ring[i], 8)
                for mm in range(4):
                    for kc in range(8):
                        MM(ps[:, 2 * s, mm * 2:mm * 2 + 2], wv[:, kc, mm * 128:(mm + 1) * 128], scT[:, kc, :], kc == 0, kc == 7, ['ring%d' % i, 'scT'], slot_res(s))
                for c in range(2):
                    bc = V_BMOD + l * 48 + pc * 4
                    TT(modsT[par][:, c, pc * 4:pc * 4 + 4], ps[:, 2 * s, c:8:2], vecs[:, bc:bc + 4], ALU.add, slot_res(s) + ['vecs'], ['mods%d_%d' % (par, pc)])

        def mres(par, j):
            return ['mods%d_%d' % (par, 2 * j), 'mods%d_%d' % (par, 2 * j + 1)]

        def coef_layer_start(l):
            par = l % 2
            for c in range(2):
                TS(coef[:, par, K_A1, c, :], modsT[par][:, c, 8:16], 1.0, None, ALU.add, None, mres(par, 1), ['cfA1_%d' % par])
                VCP(coef[:, par, K_B1, c, :], modsT[par][:, c, 0:8], mres(par, 0), ['cfB1_%d' % par])

        def coef_mid(l):
            par = l % 2
            g1 = vecs[:, V_LN1G + l * 8:V_LN1G + l * 8 + 8]
            b1 = vecs[:, V_LN1B + l * 8:V_LN1B + l * 8 + 8]
            for c in range(2):
                TS(coef[:, par, K_G1P, c, :], modsT[par][:, c, 16:24], 1.0 / ALPHA, None, ALU.mult, None, mres(par, 2), ['cfG1_%d' % par])
                TS(coef[:, par, K_G2P, c, :], modsT[par][:, c, 40:48], 1.0 / ALPHA, None, ALU.mult, None, mres(par, 5), ['cfG2_%d' % par])
                TS(coef[:, par, K_OP2, c, :], modsT[par][:, c, 32:40], 1.0, None, ALU.add, None, mres(par, 4), ['cfO2_%d' % par])
                TT(coef[:, par, K_A2, c, :], coef[:, par, K_OP2, c, :], g1, ALU.mult, ['cfO2_%d' % par, 'vecs'], ['cfA2_%d' % par])
                TT(coef[:, par, K_B2, c, :], coef[:, par, K_OP2, c, :], b1, ALU.mult, ['cfO2_%d' % par, 'vecs'], ['cfB2_%d' % par])
                TT(coef[:, par, K_B2, c, :], coef[:, par, K_B2, c, :], modsT[par][:, c, 24:32], ALU.add, ['cfB2_%d' % par] + mres(par, 3), ['cfB2_%d' % par])

        def coef_next(l):
            par = (l + 1) % 2
            g2 = vecs[:, V_LN2G + l * 8:V_LN2G + l * 8 + 8]
            b2 = vecs[:, V_LN2B + l * 8:V_LN2B + l * 8 + 8]
            for c in range(2):
                TS(coef[:, par, K_OP1, c, :], modsT[par][:, c, 8:16], 1.0, None, ALU.add, None, mres(par, 1), ['cfO1_%d' % par])
                TT(coef[:, par, K_A1, c, :], coef[:, par, K_OP1, c, :], g2, ALU.mult, ['cfO1_%d' % par, 'vecs'], ['cfA1_%d' % par])
                TT(coef[:, par, K_B1, c, :], coef[:, par, K_OP1, c, :], b2, ALU.mult, ['cfO1_%d' % par, 'vecs'], ['cfB1_%d' % par])
                TT(coef[:, par, K_B1, c, :], coef[:, par, K_B1, c, :], modsT[par][:, c, 0:8], ALU.add, ['cfB1_%d' % par] + mres(par, 0), ['cfB1_%d' % par])

        def mm_fm(s, sgn, lhs_list, rhs_fn, reads):
            c0, n, _, _, _ = _seg(sgn)
            nk = len(lhs_list)
            for k in range(nk):
                for h in range(n // 512):
                    MM(ps[:, 2 * s + h, :], lhs_list[k], rhs_fn(k, c0 + h * 512, c0 + (h + 1) * 512), k == 0, k == nk - 1, reads, slot_res(s))

        def layer_norm(l, which, last):
            par = l % 2
            for tb in range(3):
                sgn = 'P' if tb == 0 else 'S'
                cond = 0 if tb == 0 else 1
                c0 = tb * 512
                s = next_slot()
                xres = ['xT%d%s' % (c, sgn) for c in range(8)]
                for c in range(8):
                    b = c % 2
                    xv = xT[:, c, c0:c0 + 512]
                    ACP(ybf[b][:], xv, [xres[c]], ['ybf%d' % b])
                    ACT(ysq[b][:], xv, AF.Square, [xres[c]], ['ysq%d' % b])
                    MM(ps[:, 2 * s, :], ones[:], ybf[b][:], c == 0, c == 7, ['ones', 'ybf%d' % b], slot_res(s))
                    MM(ps[:, 2 * s + 1, :], ones[:], ysq[b][:], c == 0, c == 7, ['ones', 'ysq%d' % b], slot_res(s))
                ACT(lnm[:], ps[:, 2 * s, :], AF.Identity, slot_res(s), ['lnm'], scale=1.0 / D)
                TT(lnq[:], lnm[:], lnm[:], ALU.mult, ['lnm'], ['lnq'])
                STT(lnq[:], ps[:, 2 * s + 1, :], 1.0 / D, lnq[:], ALU.mult, ALU.subtract, slot_res(s) + ['lnq'], ['lnq'])
                TS(lnq[:], lnq[:], EPSP, None, ALU.add, None, ['lnq'], ['lnq'])
                ACT(lnq[:], lnq[:], AF.Ln, ['lnq'], ['lnq'])
                ACT(lnq[:], lnq[:], AF.Exp, ['lnq'], ['lnq'], scale=-0.5)
                if which == 1:
                    gcol, bcol = V_LN1G + l * 8, V_LN1B + l * 8
                    kA, kB, cpar = K_A2, K_B2, par
                    cres = ['cfA2_%d' % par, 'cfB2_%d' % par]
                else:
                    gcol, bcol = V_LN2G + l * 8, V_LN2B + l * 8
                    kA, kB, cpar = K_A1, K_B1, (l + 1) % 2
                    cres = ['cfA1_%d' % cpar, 'cfB1_%d' % cpar]
                for c in range(8):
                    xv = xT[:, c, c0:c0 + 512]
                    TT(xv, xv, lnm[:], ALU.subtract, [xres[c], 'lnm'], [xres[c]])
                    TT(xv, xv, lnq[:], ALU.mult, [xres[c], 'lnq'], [xres[c]])
                    if not last:
                        TS(hT[:, c, c0:c0 + 512], xv, cf(cpar, kA, cond, c), cf(cpar, kB, cond, c), ALU.mult, ALU.add, [xres[c]] + cres, ['hT%d%s' % (c, sgn)])
                    ACT(xv, xv, AF.Identity, [xres[c], 'vecs'], [xres[c]], bias=vecs[:, bcol + c:bcol + c + 1], scale=vecs[:, gcol + c:gcol + c + 1])

        def hrhs(k, a, b):
            return hT[:, k, a:b]

        def mixrhs(k, a, b):
            return hT[:, k, a:b] if k < 4 else poolT[:, k - 4, a:b]

        if stop == 'setup':
            depth = 0
        mods_for_layer(0, range(0, 4))
        coef_layer_start(0)
        for sgn in 'SP':
            c0, n, _, _, cond = _seg(sgn)
            for c in range(8):
                if c % 2 == 0:
                    ACT(hT[:, c, c0:c0 + n], xT[:, c, c0:c0 + n], AF.Identity, ['xT%d%s' % (c, sgn), 'cfA1_0', 'cfB1_0'], ['hT%d%s' % (c, sgn)],
                        bias=cf(0, K_B1, cond, c), scale=cf(0, K_A1, cond, c))
                else:
                    TS(hT[:, c, c0:c0 + n], xT[:, c, c0:c0 + n], cf(0, K_A1, cond, c), cf(0, K_B1, cond, c), ALU.mult, ALU.add,
                       ['xT%d%s' % (c, sgn), 'cfA1_0', 'cfB1_0'], ['hT%d%s' % (c, sgn)])
        P.transfer(IO_RES, MIX_RES)

        hres = {sgn: ['hT%d%s' % (c, sgn) for c in range(8)] for sgn in 'PS'}
        mixres = {sgn: ['hT%d%s' % (c, sgn) for c in range(4)] + ['poolT%d%s' % (c, sgn) for c in range(4)] for sgn in 'PS'}

        def run_pipeline(units, reverse=True):
            nst = max(len(u) for u in units)
            for step in range(len(units) + nst - 1):
                ks = range(nst - 1, -1, -1) if reverse else range(nst)
                for k in ks:
                    n = step - k
                    if 0 <= n < len(units) and k < len(units[n]) and units[n][k] is not None:
                        units[n][k]()

        pending = []

        def emit_pending(k):
            for _ in range(min(k, len(pending))):
                pending.pop(0)()

        def ln_tail_pieces(l, which, last, tb, sb0, sb1):
            par = l % 2
            sgn = 'P' if tb == 0 else 'S'
            cond = 0 if tb == 0 else 1
            c0 = tb * 512
            xres = ['xT%d%s' % (c, sgn) for c in range(8)]
            if which == 1:
                gcol, bcol = V_LN1G + l * 8, V_LN1B + l * 8
                kA, kB, cpar = K_A2, K_B2, par
                cres = ['cfA2_%d' % par, 'cfB2_%d' % par]
            else:
                gcol, bcol = V_LN2G + l * 8, V_LN2B + l * 8
                kA, kB, cpar = K_A1, K_B1, (l + 1) % 2
                cres = ['cfA1_%d' % cpar, 'cfB1_%d' % cpar]

            def head_a():
                ACT(lnm[:], ps[:, sb0, :], AF.Identity, ['pb%d' % sb0], ['lnm'], scale=1.0 / D)

            def head_b():
                TT(lnq[:], lnm[:], lnm[:], ALU.mult, ['lnm'], ['lnq'])
                STT(lnq[:], ps[:, sb1, :], 1.0 / D, lnq[:], ALU.mult, ALU.subtract, ['pb%d' % sb1, 'lnq'], ['lnq'])
                TS(lnq[:], lnq[:], EPSP, None, ALU.add, None, ['lnq'], ['lnq'])

            def head_c():
                ACT(lnq[:], lnq[:], AF.Ln, ['lnq'], ['lnq'])
                ACT(lnq[:], lnq[:], AF.Exp, ['lnq'], ['lnq'], scale=-0.5)

            def chunk(c):
                def f():
                    xv = xT[:, c, c0:c0 + 512]
                    TT(xv, xv, lnm[:], ALU.subtract, [xres[c], 'lnm'], [xres[c]])
                    TT(xv, xv, lnq[:], ALU.mult, [xres[c], 'lnq'], [xres[c]])
                    if not last:
                        if c % 2 == 0:
                            ACT(hT[:, c, c0:c0 + 512], xv, AF.Identity, [xres[c]] + cres, ['hT%d%s' % (c, sgn)], bias=cf(cpar, kB, cond, c), scale=cf(cpar, kA, cond, c))
                        else:
                            TS(hT[:, c, c0:c0 + 512], xv, cf(cpar, kA, cond, c), cf(cpar, kB, cond, c), ALU.mult, ALU.add, [xres[c]] + cres, ['hT%d%s' % (c, sgn)])
                    ACT(xv, xv, AF.Identity, [xres[c], 'vecs'], [xres[c]], bias=vecs[:, bcol + c:bcol + c + 1], scale=vecs[:, gcol + c:gcol + c + 1])
                return f
            return [head_a, head_b, head_c] + [chunk(c) for c in range(8)]

        def store_block(tb):
            sgn = 'P' if tb == 0 else 'S'
            for tbk in range(4 * tb, 4 * tb + 4):
                b = tbk % 2
                s = 2 + tbk % 2
                for c in range(8):
                    TR(slot_flat(s)[:, c * 128:(c + 1) * 128], xT[:, c, tbk * 128:(tbk + 1) * 128], ident32[:], ['xT%d%s' % (c, sgn), 'ident32'], slot_res(s))
                ACP(ta[b], slot_flat(s), slot_res(s), ['ta%d' % b])
                DMA('sp', y_out[tbk * 128:(tbk + 1) * 128, :], ta[b], ['ta%d' % b], ['y_out%d' % b], 'yo%d' % b)

        def filler(n):
            if n <= 0:
                return
            s = next_slot()
            for i in range(n):
                MM(ps[:, 2 * s, :], ones[:], maskb[:, 0, :], True, True, ['ones', 'maskb'], ['pb%d' % (2 * s)])

        def proj_ln(l, which, last, lhs_fn, nk, rhs_fn, rres_fn, wres, gkind, gres, nfill=0):
            par = l % 2
            reserved.update([0, 1, 2, 3])
            bctr = [0]
            prev_tb = None
            lo_rec = max(_bank_recency(bk_) for bk_ in range(0, 4))
            hi_rec = max(_bank_recency(bk_) for bk_ in range(4, 8))
            ubase, sbase = (4, 0) if hi_rec <= lo_rec or last else (0, 4)
            for ti, tb in enumerate((1, 2, 0)):
                sgn = 'P' if tb == 0 else 'S'
                cond = 0 if tb == 0 else 1
                c0 = tb * 512
                sb0, sb1 = (sbase, sbase + 1) if ti % 2 == 0 else (sbase + 2, sbase + 3)
                units = []
                for dc in range(8):
                    def s1(dc=dc, sgn=sgn, cond=cond, c0=c0):
                        bk = ubase + bctr[0] % 4
                        bctr[0] += 1
                        lhs = lhs_fn(dc)
                        for k in range(nk):
                            MM(ps[:, bk, :], lhs[k], rhs_fn(k, c0, c0 + 512), k == 0, k == nk - 1, wres + rres_fn(sgn), ['pb%d' % bk])
                        xv = xT[:, dc, c0:c0 + 512]
                        STT(xv, ps[:, bk, :], cf(par, gkind, cond, dc), xv, ALU.mult, ALU.add, ['pb%d' % bk, gres, 'xT%d%s' % (dc, sgn)], ['xT%d%s' % (dc, sgn)])
                        b = dc % 2
                        ACP(ybf[b][:], xv, ['xT%d%s' % (dc, sgn)], ['ybf%d' % b])
                        ACT(ysq[b][:], xv, AF.Square, ['xT%d%s' % (dc, sgn)], ['ysq%d' % b])
                        emit_pending((1, 1, 1, 2, 2, 2, 1, 1)[dc])

                    def s3(dc=dc, sb0=sb0, sb1=sb1):
                        b = dc % 2
                        MM(ps[:, sb0, :], ones[:], ybf[b][:], dc == 0, dc == 7, ['ones', 'ybf%d' % b], ['pb%d' % sb0])
                        MM(ps[:, sb1, :], ones[:], ysq[b][:], dc == 0, dc == 7, ['ones', 'ysq%d' % b], ['pb%d' % sb1])
                    units.append([s1, None, s3])
                run_pipeline(units)
                emit_pending(len(pending))
                if last and prev_tb is not None:
                    store_block(prev_tb)
                prev_tb = tb
                pending.extend(ln_tail_pieces(l, which, last, tb, sb0, sb1))
            reserved.difference_update([0, 1, 2, 3])
            filler(nfill)
            emit_pending(len(pending))
            if last:
                store_block(0)

        for l in range(depth):
          try:
            par = l % 2
            lastl = (l == depth - 1)
            DMA('sp', cstage[:, 0, :, :], ck[l].rearrange("(b p) d -> p b d", p=128), [], ['sB'], 'cst')
            DMA('sp', cstage[:, 1, :, :], cv[l].rearrange("(b p) d -> p b d", p=128), [], ['sB'], 'cst')
            DMA('sp', sink8[:], sink_d[l:l + 1, :], [], ['sink8'], 'snk')
            DMA('pool', wpool[par][:], w_pool[l].rearrange("g c d -> c g d"), [], ['wpool%d' % par], 'wpool%d' % par)
            ACT(sinkf8[:], sink8[:], AF.Exp, ['sink8'], ['sinkf8'])
            VCP(sinkh8[:], sinkf8[:], ['sinkf8'], ['sinkh8'])
            TT(sinkf8[:], sinkf8[:], sinkh8[:], ALU.subtract, ['sinkf8', 'sinkh8'], ['sinkf8'])
            VCP(sinkl8[:], sinkf8[:], ['sinkf8'], ['sinkl8'])
            VCP(sinkhl[0:1, :].rearrange("p (a b) -> p a b", a=8), sinkh8[0:1, :].unsqueeze(2).to_broadcast([1, 8, 128]), ['sinkh8', 'sinkhl'], ['sinkhl'])
            VCP(sinkhl[32:33, :].rearrange("p (a b) -> p a b", a=8), sinkl8[0:1, :].unsqueeze(2).to_broadcast([1, 8, 128]), ['sinkl8', 'sinkhl'], ['sinkhl'])
            P.add('dve', 'memset', (Vb[:, :, 64:192], 1.0), None, [], ['VbP', 'VbS'])
            P.add('dve', 'memset', (vcb[:, :, 64:192], 1.0), None, [], ['vcb'])
            s = next_slot()
            for b in range(2):
                TR(ps[:, 2 * s, b * 128:(b + 1) * 128], cstage[:, 0, b, :], ident32[:], ['sB', 'ident32'], slot_res(s))
            ACP(kcT, ps[:, 2 * s, 0:256], slot_res(s), ['kcT'])
            VCP(vcb[:, :, 0:64], cstage[:, 1, :, 0:64], ['sB', 'vcb'], ['vcb'])
            VCP(vcb[:, :, 192:256], cstage[:, 1, :, 64:128], ['sB', 'vcb'], ['vcb'])

            iq = ring_fill([
                ((lambda r, c=c, hh=hh: v3(r, 8)[:, :, c * 128 + hh * 64:c * 128 + hh * 64 + 64]),
                 w_in[l, :, (hh * 4 + c) * 64:(hh * 4 + c) * 64 + 64].rearrange("(k p) n -> p k n", p=128))
                for c in range(4) for hh in range(2)])
            ikv = ring_fill([(lambda r: v3(r, 8)[:, :, 0:256], w_in[l, :, 512:768].rearrange("(k p) n -> p k n", p=128))])
            ipp = ring_fill([(lambda r: v3(r, 8), w_in[l, :, 768:1280].rearrange("(k p) n -> p k n", p=128))])
            pinned.update([iq, ikv, ipp])
            wq = v3(ring[iq], 8)
            wkv = v3(ring[ikv], 8)
            wpp = v3(ring[ipp], 8)

            def proj_stage(box, sgn, lhs, rres):
                def f():
                    box['s'] = next_slot()
                    mm_fm(box['s'], sgn, lhs, hrhs, rres + hres[sgn])
                return f

            def rope_stage_a(box):
                def f():
                    s = box['s']
                    for h in range(2):
                        ACP(q32[h], ps[:, 2 * s + h, :], ['pb%d' % (2 * s + h)], ['q32%d' % h])
                return f

            def rope_stage_b(dst, dres):
                def f():
                    s2 = next_slot()
                    for h in range(2):
                        MM(ps[:, 2 * s2 + h, :], rperm[:], q32[h], True, True, ['rperm', 'q32%d' % h], ['pb%d' % (2 * s2 + h)])
                    for h in range(2):
                        cs = slice(h * 512, (h + 1) * 512)
                        TT(t1, q32[h], ropeC[:, cs], ALU.mult, ['q32%d' % h, 'ropeC'], ['t1'])
                        TT(q32[h], ps[:, 2 * s2 + h, :], ropeS[:, cs], ALU.mult, ['pb%d' % (2 * s2 + h), 'ropeS', 'q32%d' % h], ['q32%d' % h])
                        TT(dst[:, cs], t1, q32[h], ALU.add, ['t1', 'q32%d' % h], dres)
                return f

            def copy_stage(box, dst, dres):
                def f():
                    s = box['s']
                    ACP(dst, ps[:, 2 * s, :], slot_res(s), dres)
                return f

            def pool_stage2(box, g, sgn, di):
                c0, n, nseq, L, _ = _seg(sgn)
                w = (2, 4, 8, 16)[g]
                hw_ = w // 2
                LP = L + 16

                def pv(buf, lo, hi):
                    return buf[:, 0:nseq * LP].rearrange("p (a b) -> p a b", a=nseq)[:, :, lo:hi]

                def f():
                    s = box['s']
                    P.add('dve', 'memset', (pv(p32, 0, 8), 0.0), None, [], ['p32'])
                    P.add('dve', 'memset', (pv(p32, LP - 8, LP), 0.0), None, ['p32'], ['p32'])
                    P.add('act', 'copy', (pv(p32, 8, 8 + L), slot_flat(s)[:, 0:n].rearrange("p (a b) -> p a b", a=nseq)), None, slot_res(s) + ['p32'], ['p32'])
                    TT(pv(sA, 1, LP), pv(p32, 1, LP), pv(p32, 0, LP - 1), ALU.add, ['p32', 'sA'], ['sA'])
                    src, dst, sn, dn = sA, sB, 'sA', 'sB'
                    lo, hi, sh = 1, LP, 1
                    for _ in range(g):
                        TT(pv(dst, lo + sh, hi - sh), pv(src, lo + 2 * sh, hi), pv(src, lo, hi - 2 * sh), ALU.add, [sn, dn], [dn])
                        lo, hi = lo + sh, hi - sh
                        src, dst, sn, dn = dst, src, dn, sn
                        sh *= 2
                    tot, tn = src, sn
                    eb = (g * 2 + (0 if sgn == 'P' else 1)) * 32
                    ev = ecf[:, eb:eb + nseq * 8].rearrange("p (a b) -> p a b", a=nseq)
                    TT(pv(tot, 8, 8 + hw_), pv(tot, 8, 8 + hw_), ev[:, :, 0:hw_], ALU.mult, [tn, 'ecf'], [tn])
                    if hw_ > 1:
                        ev2 = ecf[:, eb + 16:eb + 16 + nseq * 8].rearrange("p (a b) -> p a b", a=nseq)
                        TT(pv(tot, 8 + L - hw_ + 1, 8 + L), pv(tot, 8 + L - hw_ + 1, 8 + L), ev2[:, :, 0:hw_ - 1], ALU.mult, [tn, 'ecf'], [tn])
                    STT(dTt[di][:, 0:n].rearrange("p (a b) -> p a b", a=nseq), pv(tot, 8, 8 + L), 1.0 / w, pv(p32, 8, 8 + L), ALU.mult, ALU.subtract, [tn, 'p32'], ['dT%d' % di])
                return f

            def pool_stage3(g, sgn, di):
                c0, n, nseq, L, _ = _seg(sgn)

                def f():
                    s = next_slot()
                    for h in range(n // 512):
                        MM(ps[:, 2 * s + h, :], wpool[par][:, g, :], dTt[di][:, h * 512:(h + 1) * 512], True, True, ['wpool%d' % par, 'dT%d' % di], ['pb%d' % (2 * s + h)])
                    psc = V_PSC + l * 4 + g
                    ACT(poolT[:, g, c0:c0 + n], slot_flat(s)[:, 0:n], AF.Identity, slot_res(s) + ['vecs'], ['poolT%d%s' % (g, sgn)], scale=vecs[:, psc:psc + 1])
                return f

            def kvp_units():
                box = {}

                def s1():
                    s = next_slot()
                    box['s'] = s
                    pv4 = slot_flat(s).rearrange("p (a b) -> p a b", a=4)
                    for tbk in range(4):
                        for k in range(8):
                            MM(pv4[:, tbk, :], hT[:, k, tbk * 128:(tbk + 1) * 128], wkv[:, k, 0:256], k == 0, k == 7, ['ring%d' % ikv] + hres['P'], slot_res(s))

                def s2():
                    s = box['s']
                    pv4 = slot_flat(s).rearrange("p (a b) -> p a b", a=4)
                    ACP(kvst, pv4, slot_res(s), ['sA'])
                    ACP(Vb[:, 0:4, 0:64], pv4[:, :, 128:192], slot_res(s) + ['VbP'], ['VbP'])
                    ACP(Vb[:, 0:4, 192:256], pv4[:, :, 192:256], slot_res(s) + ['VbP'], ['VbP'])
                    for b in range(2):
                        DMA('sp', nk_out[b, l].rearrange("(k p) d -> p k d", p=128), kvst[:, 2 * b:2 * b + 2, 0:128], ['sA'], ['nk_out'], 'kvo')
                        DMA('sp', nv_out[b, l].rearrange("(k p) d -> p k d", p=128), kvst[:, 2 * b:2 * b + 2, 128:256], ['sA'], ['nv_out'], 'kvo')
                return [s1, s2]

            def vs_units():
                box = {}

                def s1():
                    s = next_slot()
                    box['s'] = s
                    pv8 = slot_flat(s).rearrange("p (a b) -> p a b", a=8)
                    for tbk in range(8):
                        for k in range(8):
                            MM(pv8[:, tbk, :], hT[:, k, 512 + tbk * 128:512 + (tbk + 1) * 128], wkv[:, k, 128:256], k == 0, k == 7, ['ring%d' % ikv] + hres['S'], slot_res(s))

                def s2():
                    s = box['s']
                    pv8 = slot_flat(s).rearrange("p (a b) -> p a b", a=8)
                    ACP(Vb[:, 4:12, 0:64], pv8[:, :, 0:64], slot_res(s) + ['VbS'], ['VbS'])
                    ACP(Vb[:, 4:12, 192:256], pv8[:, :, 64:128], slot_res(s) + ['VbS'], ['VbS'])
                return [s1, s2]

            klhs = [wkv[:, k, 0:128] for k in range(8)]

            def qu(c, sgn):
                qlhs = [wq[:, k, c * 128:(c + 1) * 128] for k in range(8)]
                box = {}
                if sgn == 'P':
                    return [proj_stage(box, 'P', qlhs, ['ring%d' % iq]), copy_stage(box, qT[:, c, 0:512], ['qT%dP' % c])]
                return [proj_stage(box, 'S', qlhs, ['ring%d' % iq]), rope_stage_a(box), None, rope_stage_b(qT[:, c, 512:1536], ['qT%dS' % c])]

            def pu(c, sgn):
                plhs = [wpp[:, k, c * 128:(c + 1) * 128] for k in range(8)]
                box = {}
                di = 0 if sgn == 'P' else 1
                return [proj_stage(box, sgn, plhs, ['ring%d' % ipp]), pool_stage2(box, c, sgn, di), None, pool_stage3(c, sgn, di)]

            def ku(sgn):
                box = {}
                if sgn == 'P':
                    return [proj_stage(box, 'P', klhs, ['ring%d' % ikv]), copy_stage(box, kT[:, 0:512], ['kTP'])]
                return [proj_stage(box, 'S', klhs, ['ring%d' % ikv]), rope_stage_a(box), None, rope_stage_b(kT[:, 512:1536], ['kTS'])]

            def fill_stage(u, n):
                u[2] = (lambda: filler(n))
                return u

            unitsA = [ku('S'), pu(0, 'S'), qu(0, 'S'), pu(1, 'S'), qu(1, 'S'), pu(2, 'S'), qu(2, 'S'), pu(3, 'S'), qu(3, 'S'), vs_units(),
                      ku('P'), qu(0, 'P'), kvp_units(), qu(1, 'P'), pu(0, 'P'), qu(2, 'P'), pu(1, 'P'), qu(3, 'P'),
                      fill_stage(pu(2, 'P'), FILL_AB), [lambda: None], fill_stage(pu(3, 'P'), FILL_AB)]
            if l == 0:
                for i_, pc in enumerate(range(4, 12)):
                    unitsA.insert(3 + 2 * i_, [lambda pc=pc: mods_for_layer(0, [pc])])
            run_pipeline(unitsA)
            pinned.difference_update([iq, ikv, ipp])
            if stop == 'A':
                raise _Stop()

            coef_mid(l)
            aunits = []
            acc_ctr = [0]

            def add_qblock(q0, kblocks, qres, ores):
                for kvh in range(2):
                    acc = acc_ctr[0] % 4
                    acc_ctr[0] += 1
                    groups = [kblocks[g0:g0 + 2] for g0 in range(0, len(kblocks), 2)]
                    for gi, grp in enumerate(groups):
                        aunits.append(dict(q0=q0, kvh=kvh, grp=grp, first=(gi == 0), last=(gi == len(groups) - 1), acc=acc, qres=qres, ores=ores))

            def att_S(u, n):
                def f():
                    sl = 2 + n % 2
                    h0 = u['kvh'] * 64
                    q0 = u['q0']
                    for j, (kap, vap, mi, kres, vres) in enumerate(u['grp']):
                        bank = 2 * sl + j
                        if mi is not None:
                            MM(ps[:, bank, :], identb[:], maskb[:, mi, :], True, False, ['identb', 'maskb'], ['pb%d' % bank])
                        MM(ps[:, bank, :], kap[h0:h0 + 64, :], qT[h0:h0 + 64, :, q0:q0 + 128], mi is None, True, [kres] + u['qres'], ['pb%d' % bank])
                return f

            def att_PV(u, n):
                def f():
                    sl = 2 + n % 2
                    pti = n % 3
                    kvh = u['kvh']
                    h0 = kvh * 64
                    s0 = 64 - h0
                    q0 = u['q0']
                    accb = u['acc']
                    grp = u['grp']
                    ng = len(grp)
                    ACT(PT[pti][:, 0:ng, :], ps[:, 2 * sl:2 * sl + ng, :], AF.Exp, ['pb%d' % (2 * sl + j) for j in range(ng)], ['PT%d' % pti], scale=SCALE)
                    if u['first']:
                        MM(ps[:, accb, :], sinkL[kvh][:], sinkhl[:, kvh * 512:(kvh + 1) * 512], True, False, ['sinkL', 'sinkhl'], ['pb%d' % accb])
                    for j, (kap, vap, mi, kres, vres) in enumerate(grp):
                        MM(ps[:, accb, :], vap[:, kvh * 128:(kvh + 1) * 128], PT[pti][:, j, :], False, u['last'] and j == ng - 1, [vres, 'PT%d' % pti], ['pb%d' % accb])
                return f

            def att_NORM(u, n):
                if not u['last']:
                    return None

                def f():
                    kvh = u['kvh']
                    h0 = kvh * 64
                    s0 = 64 - h0
                    q0 = u['q0']
                    accb = u['acc']
                    if kvh == 0:
                        P.add('dve', 'reciprocal', (rcp[kvh][s0:s0 + 64, :], ps[s0:s0 + 64, accb, :]), None, ['pb%d' % accb], [rcpn[kvh]])
                    else:
                        ACT(rcp[kvh][s0:s0 + 64, :], ps[s0:s0 + 64, accb, :], AF.Ln, ['pb%d' % accb], [rcpn[kvh]])
                        ACT(rcp[kvh][s0:s0 + 64, :], rcp[kvh][s0:s0 + 64, :], AF.Exp, [rcpn[kvh]], [rcpn[kvh]], scale=-1.0)
                    TT(hT[h0:h0 + 64, 0:4, q0:q0 + 128], ps[h0:h0 + 64, accb, :].rearrange("p (a b) -> p a b", a=4),
                       rcp[kvh][s0:s0 + 64, :].rearrange("p (a b) -> p a b", a=4), ALU.mult, ['pb%d' % accb, rcpn[kvh]], u['ores'])
                return f

            for i in range(8):
                kbl = [(kcT[:, kb * 128:(kb + 1) * 128], vcb[:, kb, :], None, 'kcT', 'vcb') for kb in range(2)]
                for j in (i - 1, i, i + 1):
                    if j < 0 or j > 7:
                        continue
                    mi = None if j == i else (0 if j == i - 1 else 1)
                    kbl.append((kT[:, 512 + j * 128:512 + (j + 1) * 128], Vb[:, 4 + j, :], mi, 'kTS', 'VbS'))
                add_qblock(512 + i * 128, kbl, ['qT%dS' % c for c in range(4)], ['hT%dS' % c for c in range(4)])
            for b in range(2):
                kbl = [(kT[:, b * 256 + kb * 128:b * 256 + (kb + 1) * 128], Vb[:, 2 * b + kb, :], None, 'kTP', 'VbP') for kb in range(2)]
                for qb in range(2):
                    add_qblock(b * 256 + qb * 128, kbl, ['qT%dP' % c for c in range(4)], ['hT%dP' % c for c in range(4)])
            def att_warm():
                for i in range(FILL_ATT):
                    MM(ps[:, 3, :], ones[:], maskb[:, 0, :], True, True, ['ones', 'maskb'], ['pb3'])

            att_stages = [[att_S(u, n), att_PV(u, n), None, att_NORM(u, n)] for n, u in enumerate(aunits)]
            s0_ = att_stages[0][0]
            att_stages[0][0] = (lambda: (s0_(), att_warm()))
            run_pipeline(att_stages, reverse=False)
            if stop == 'B':
                raise _Stop()

            filler(FILL_BC)

            ios = []
            for half in range(2):
                c_a, c_b = half * 512, (half + 1) * 512
                ios.append(ring_fill([
                    (lambda r: v3(r, 8)[0:64, 0:4, :], w_out[l, 0:256, c_a:c_b].rearrange("(c p) n -> p c n", p=64)),
                    (lambda r: v3(r, 8)[64:128, 0:4, :], w_out[l, 256:512, c_a:c_b].rearrange("(c p) n -> p c n", p=64)),
                    (lambda r: v3(r, 8)[:, 4:8, :], w_out[l, 512:1024, c_a:c_b].rearrange("(c p) n -> p c n", p=128)),
                ]))
            wos = [v3(ring[i], 8) for i in ios]
            proj_ln(l, 1, False, lambda dc: [wos[dc // 4][:, k, (dc % 4) * 128:(dc % 4 + 1) * 128] for k in range(8)], 8,
                    mixrhs, lambda sgn: mixres[sgn], ['ring%d' % i for i in ios], K_G1P, 'cfG1_%d' % par)
            if stop == 'C':
                raise _Stop()
            P.transfer(MIX_RES, FFN_RES)

            nmod_done = [0]

            def next_mod_piece():
                if (not lastl) and nmod_done[0] < 12:
                    mods_for_layer(l + 1, [nmod_done[0]])
                    nmod_done[0] += 1

            def cw(tap, ch):
                cc = V_CW + (l * 3 + tap) * 44 + ch
                return vecs[:, cc:cc + 1]

            def cb(ch):
                cc = V_CB + l * 44 + ch
                return vecs[:, cc:cc + 1]

            tb_ctr = [0]
            j0 = 0
            for hf in range(2):
                npairs = HALF_PAIRS[hf]
                for jp in range(npairs // 2):
                    jA = j0 + 2 * jp
                    iu = ring_fill([
                        (lambda r: v3(r, 8)[:, :, 0:256], w_up[l, :, jA * 128:(jA + 2) * 128].rearrange("(k p) n -> p k n", p=128)),
                        (lambda r: v3(r, 8)[:, :, 256:512], w_up[l, :, 2816 + jA * 128:2816 + (jA + 2) * 128].rearrange("(k p) n -> p k n", p=128)),
                    ])
                    wu = v3(ring[iu], 8)
                    first_slot = (hf == 0 and jp == 0)
                    last_slot = (jp == npairs // 2 - 1)
                    order = [(0, 'S'), (1, 'S'), (0, 'P'), (1, 'P')] if (first_slot or last_slot) else [(0, 'P'), (0, 'S'), (1, 'P'), (1, 'S')]
                    for oi, (jj, sgn) in enumerate(order):
                        j = jA + jj
                        jm = j - j0
                        la = [wu[:, k, jj * 128:(jj + 1) * 128] for k in range(8)]
                        lg = [wu[:, k, 256 + jj * 128:256 + (jj + 1) * 128] for k in range(8)]
                        for sgn in (sgn,):
                            c0, n, nseq, L, _ = _seg(sgn)
                            tbi = tb_ctr[0] % 2
                            tb_ctr[0] += 1
                            rr = ['ring%d' % iu] + hres[sgn]
                            if sgn == 'P':
                                s = next_slot()
                                for k in range(8):
                                    MM(ps[:, 2 * s, :], la[k], hT[:, k, 0:512], k == 0, k == 7, rr, ['pb%d' % (2 * s)])
                                for k in range(8):
                                    MM(ps[:, 2 * s + 1, :], lg[k], hT[:, k, 0:512], k == 0, k == 7, rr, ['pb%d' % (2 * s + 1)])
                                srcs = [(ps[:, 2 * s, :], ['pb%d' % (2 * s)], ta[tbi], 'ta%d' % tbi, j), (ps[:, 2 * s + 1, :], ['pb%d' % (2 * s + 1)], tg[tbi], 'tg%d' % tbi, 22 + j)]
                            else:
                                s = next_slot()
                                mm_fm(s, 'S', la, hrhs, rr)
                                s2 = next_slot()
                                mm_fm(s2, 'S', lg, hrhs, rr)
                                srcs = [(slot_flat(s), slot_res(s), ta[tbi], 'ta%d' % tbi, j), (slot_flat(s2), slot_res(s2), tg[tbi], 'tg%d' % tbi, 22 + j)]
                            for (src, sres, tbuf, tname, ch) in srcs:
                                sv = src.rearrange("p (a b) -> p a b", a=nseq)
                                tv = tbuf[:, 0:n].rearrange("p (a b) -> p a b", a=nseq)
                                ACT(tv, sv, AF.Identity, sres + ['vecs'], [tname], bias=cb(ch), scale=cw(1, ch))
                                STT(tv[:, :, 1:L], sv[:, :, 0:L - 1], cw(0, ch), tv[:, :, 1:L], ALU.mult, ALU.add, sres + ['vecs', tname], [tname])
                                STT(tv[:, :, 0:L - 1], sv[:, :, 1:L], cw(2, ch), tv[:, :, 0:L - 1], ALU.mult, ALU.add, sres + ['vecs', tname], [tname])
                            ACT(tg[tbi][:, 0:n], tg[tbi][:, 0:n], AF.Silu, ['tg%d' % tbi], ['tg%d' % tbi])
                            TT(mT[:, jm, c0:c0 + n], ta[tbi][:, 0:n], tg[tbi][:, 0:n], ALU.mult, ['ta%d' % tbi, 'tg%d' % tbi], ['mT%d%s' % (jm, sgn)])
                    next_mod_piece()
                if hf == 0:
                    for dcp in range(4):
                        idn = ring_fill([(lambda r: r[:, 0:npairs * 256].rearrange("p (a b) -> p a b", a=npairs),
                                          w_down[l, j0 * 128:(j0 + npairs) * 128, dcp * 256:(dcp + 1) * 256].rearrange("(k p) n -> p k n", p=128))])
                        wd = ring[idn][:, 0:npairs * 256].rearrange("p (a b) -> p a b", a=npairs)
                        for d2, sgn in ((0, 'S'), (1, 'S'), (0, 'P'), (1, 'P')):
                            dc = dcp * 2 + d2
                            lhs = [wd[:, k, d2 * 128:(d2 + 1) * 128] for k in range(npairs)]
                            for sgn in (sgn,):
                                c0, n, _, _, cond = _seg(sgn)
                                s = next_slot()
                                mm_fm(s, sgn, lhs, lambda k, a, b: mT[:, k, a:b], ['ring%d' % idn] + ['mT%d%s' % (k, sgn) for k in range(npairs)])
                                xv = xT[:, dc, c0:c0 + n]
                                STT(xv, slot_flat(s)[:, 0:n], cf(par, K_G2P, cond, dc), xv, ALU.mult, ALU.add, slot_res(s) + ['cfG2_%d' % par, 'xT%d%s' % (dc, sgn)], ['xT%d%s' % (dc, sgn)])
                    next_mod_piece()
                else:
                    while (not lastl) and nmod_done[0] < 12:
                        next_mod_piece()
                    if not lastl:
                        coef_next(l)
                    idns = []
                    for dq in range(2):
                        idns.append(ring_fill([(lambda r: r[:, 0:npairs * 512].rearrange("p (a b) -> p a b", a=npairs),
                                                w_down[l, j0 * 128:(j0 + npairs) * 128, dq * 512:(dq + 1) * 512].rearrange("(k p) n -> p k n", p=128))]))
                    wds = [ring[i][:, 0:npairs * 512].rearrange("p (a b) -> p a b", a=npairs) for i in idns]
                    proj_ln(l, 2, lastl, lambda dc: [wds[dc // 4][:, k, (dc % 4) * 128:(dc % 4 + 1) * 128] for k in range(npairs)], npairs,
                            lambda k, a, b: mT[:, k, a:b], lambda sgn: ['mT%d%s' % (k, sgn) for k in range(npairs)], ['ring%d' % i for i in idns], K_G2P, 'cfG2_%d' % par, nfill=0)
                j0 += npairs
            P.transfer(FFN_RES, IO_RES if lastl else MIX_RES)
          except _Stop:
            P.transfer(MIX_RES + FFN_RES, IO_RES)
            break

        for tb in (range(12) if (stop is not None or depth == 0) else []):
            b = tb % 2
            sgn = 'P' if tb < 4 else 'S'
            s = next_slot()
            for c in range(8):
                TR(slot_flat(s)[:, c * 128:(c + 1) * 128], xT[:, c, tb * 128:(tb + 1) * 128], ident32[:], ['xT%d%s' % (c, sgn), 'ident32'], slot_res(s))
            if tb % 2 == 0:
                ACP(xs[b], slot_flat(s), slot_res(s), ['xs%d' % b])
            else:
                VCP(xs[b], slot_flat(s), slot_res(s), ['xs%d' % b])
            DMA('sp', y_out[tb * 128:(tb + 1) * 128, :], xs[b], ['xs%d' % b], ['y_out%d' % b], 'yo%d' % b)
        P.add('sp', None, (), None, ['y_out0', 'y_out1', 'nk_out', 'nv_out'], [])
        P.emit(nc, st)
        nc._prog_stats = (len(P.ops), P.n_waits)
    return nc


def _constants():
    half = 32
    inv = (10000.0 ** (-np.arange(0, half, 2, dtype=np.float32) / half)).astype(np.float32)
    t = np.arange(1024)
    row = (t // 64).astype(np.float32)
    col = (t % 64).astype(np.float32)
    C = np.zeros((128, 1024), np.float32)
    S = np.zeros((128, 1024), np.float32)
    for p in range(128):
        d = p % 64
        pos = row if d < 32 else col
        f = inv[(d % 32) % 16]
        ang = (pos * f).astype(np.float32)
        C[p] = np.cos(ang)
        S[p] = np.sin(ang)
    R = np.zeros((128, 128), np.float32)
    for m in range(128):
        if (m % 32) < 16:
            R[m + 16, m] = -1.0
        else:
            R[m - 16, m] = 1.0
    ident = np.eye(128, dtype=np.float32)
    cc = np.arange(128)[:, None]
    rr = np.arange(128)[None, :]
    mA = np.where(cc >= rr, 0.0, NEG).astype(np.float32)
    mB = np.where(cc <= rr, 0.0, NEG).astype(np.float32)
    masks = np.concatenate([np.tile(mA, (1, 4)), np.tile(mB, (1, 4))], axis=1)
    ecf = np.ones((128, 256), np.float32)
    for g, w in enumerate((2, 4, 8, 16)):
        hw = w // 2
        for si in range(2):
            base = (g * 2 + si) * 32
            for sq in range(2):
                for i in range(hw):
                    ecf[:, base + sq * 8 + i] = w / float(i + hw)
                for i in range(hw - 1):
                    ecf[:, base + 16 + sq * 8 + i] = w / float(2 * hw - 1 - i)
    return C, S, R, ident, masks, ecf


def _run(inputs, depth=DEPTH, trace=False, stop=None):
    f = lambda a: np.ascontiguousarray(np.asarray(a, dtype=np.float32))
    x_prompt, x_sample = f(inputs['x_prompt']), f(inputs['x_sample'])
    cache_k, cache_v = f(inputs['cache_k']), f(inputs['cache_v'])
    c, c_ctx = f(inputs['c']), f(inputs['c_ctx'])
    C, S, R, ident, masks, ecf = _constants()
    common = np.concatenate([
        f(inputs['b_mod']).reshape(-1, 128), f(inputs['pool_scale']).reshape(-1, 128),
        f(inputs['ln1_g']).reshape(-1, 128), f(inputs['ln1_b']).reshape(-1, 128),
        f(inputs['ln2_g']).reshape(-1, 128), f(inputs['ln2_b']).reshape(-1, 128),
        f(inputs['conv_w']).reshape(-1, 128), f(inputs['conv_b']).reshape(-1, 128),
        c_ctx.reshape(8, 128)], axis=0)
    assert common.shape[0] == V_COND + 8
    shared = {
        'ropeC': C, 'ropeS': S, 'rperm': R, 'ident': ident, 'masks': masks, 'ecf': ecf,
        'sink': f(inputs['attn_sink']),
        'w_mod': f(inputs['w_mod']), 'w_in': f(inputs['w_in']), 'w_pool': f(inputs['w_pool']),
        'w_out': f(inputs['w_out']), 'w_up': f(inputs['w_up']), 'w_down': f(inputs['w_down']),
    }
    in_maps = []
    for core in range(8):
        vp = np.zeros((V_ROWS, 128), np.float32)
        vp[:common.shape[0]] = common
        vp[V_COND + 8:V_COND + 16] = c[core].reshape(8, 128)
        m = dict(shared)
        m['x_in'] = np.ascontiguousarray(np.concatenate([x_prompt[2 * core], x_prompt[2 * core + 1], x_sample[core]], axis=0))
        m['ck'] = np.ascontiguousarray(cache_k[core].reshape(DEPTH, 256, 128))
        m['cv'] = np.ascontiguousarray(cache_v[core].reshape(DEPTH, 256, 128))
        m['vecpack'] = vp
        in_maps.append(m)
    nc = build_program(depth, stop)
    res = run_bass_kernel_spmd(nc, in_maps, core_ids=list(range(8)), trace=trace)
    y_prompt = np.zeros((16, 256, D), np.float32)
    y_sample = np.zeros((8, 1024, D), np.float32)
    nk = np.zeros((16, DEPTH, 256, 2, 64), np.float32)
    nv = np.zeros((16, DEPTH, 256, 2, 64), np.float32)
    for core in range(8):
        r = res.results[core]
        y = np.asarray(r['y'])
        y_prompt[2 * core] = y[0:256]
        y_prompt[2 * core + 1] = y[256:512]
        y_sample[core] = y[512:1536]
        nk[2 * core:2 * core + 2] = np.asarray(r['nk']).reshape(2, DEPTH, 256, 2, 64)
        nv[2 * core:2 * core + 2] = np.asarray(r['nv']).reshape(2, DEPTH, 256, 2, 64)
    return (y_prompt, y_sample, nk, nv), res


def kernel(**inputs):
    outs, _ = _run(inputs)
    return outs
```
